# Optimizing a Trainium2 kernel written in Bass

```python
import jax, jax.numpy as jnp
from jax import lax
import numpy as np

D_MODEL = 1024
BATCH = 16
SEQ = 2048
DEPTH = 2

D_FF = 2816
NORM_EPS = 1e-6
FFN_RES_WEIGHT = 0.5

D_CONV = D_MODEL
CONV_WIDTH = 31
CONV_LN_EPS = 1e-5

MLA_HEADS = 8
Q_LORA = 512
KV_LORA = 256
QK_NOPE = 128
QK_ROPE = 64
V_HEAD = 128
ROPE_THETA = 10000.0
Q_BLOCK = 128

IN_COLS = 2 * D_CONV + Q_LORA + KV_LORA + QK_ROPE
OUT_COLS = D_CONV + MLA_HEADS * V_HEAD

RWKV_HEAD = 64
RWKV_HEADS = D_MODEL // RWKV_HEAD
DECAY_LORA = 64
AAA_LORA = 64
GATE_LORA = 128
RWKV_GN_EPS = 64e-5
N_SHIFT_MIX = 6

kernel_name = 'hybrid_conv_mla_rwkv7_macaron'


def rms_norm(x, g, eps=NORM_EPS):
    xf = x.astype(jnp.float32)
    y = xf * lax.rsqrt(jnp.mean(xf * xf, axis=-1, keepdims=True) + eps)
    return (y * g.astype(jnp.float32)).astype(x.dtype)


def swiglu_ffn(h, w_gate, w_up, w_down):
    return (jax.nn.silu(h @ w_gate) * (h @ w_up)) @ w_down


def rope_tables(positions):
    inv_freq = 1.0 / (ROPE_THETA ** (jnp.arange(0, QK_ROPE, 2, dtype=jnp.float32) / QK_ROPE))
    ang = positions.astype(jnp.float32)[..., None] * inv_freq
    return jnp.cos(ang), jnp.sin(ang)


def apply_rope(x, cos, sin):
    xf = x.astype(jnp.float32)
    x1, x2 = jnp.split(xf, 2, axis=-1)
    return jnp.concatenate([x1 * cos - x2 * sin, x1 * sin + x2 * cos], axis=-1).astype(x.dtype)


def conformer_conv(a, gate, conv_w, conv_b, ln_g, ln_b):
    h = a * jax.nn.sigmoid(gate)
    h = lax.conv_general_dilated(
        h, conv_w[:, None, :].astype(h.dtype), window_strides=(1,),
        padding=[(CONV_WIDTH - 1, 0)],
        dimension_numbers=('NWC', 'WIO', 'NWC'),
        feature_group_count=D_CONV) + conv_b
    hf = h.astype(jnp.float32)
    mu = jnp.mean(hf, axis=-1, keepdims=True)
    var = jnp.mean(jnp.square(hf - mu), axis=-1, keepdims=True)
    hn = (hf - mu) * lax.rsqrt(var + CONV_LN_EPS) * ln_g + ln_b
    return jax.nn.silu(hn).astype(a.dtype)


def mla_attention(q_nope, q_pe, k_nope, k_pe, v):
    seq = q_nope.shape[1]
    scale = (QK_NOPE + QK_ROPE) ** -0.5
    outs = []
    for blk in range(seq // Q_BLOCK):
        q0, q1 = blk * Q_BLOCK, (blk + 1) * Q_BLOCK
        s = (jnp.einsum('bqhd,bkhd->bhqk', q_nope[:, q0:q1], k_nope[:, :q1])
             + jnp.einsum('bqhr,bkr->bhqk', q_pe[:, q0:q1], k_pe[:, :q1]))
        s = s.astype(jnp.float32) * scale
        causal = (q0 + jnp.arange(Q_BLOCK))[:, None] >= jnp.arange(q1)[None, :]
        p = jax.nn.softmax(jnp.where(causal, s, -jnp.inf), axis=-1).astype(v.dtype)
        outs.append(jnp.einsum('bhqk,bkhd->bqhd', p, v[:, :q1]))
    return jnp.concatenate(outs, axis=1)


def conv_mla_mixer(h, cos, sin, w_in, conv_w, conv_b, conv_ln_g, conv_ln_b,
                   q_norm, w_uq, kv_norm, w_ukv, w_out):
    b, s, _ = h.shape
    z = h @ w_in
    conv_a, conv_gate, q_lat, kv_lat, k_pe = jnp.split(
        z, [D_CONV, 2 * D_CONV, 2 * D_CONV + Q_LORA, 2 * D_CONV + Q_LORA + KV_LORA], axis=-1)
    conv_out = conformer_conv(conv_a, conv_gate, conv_w, conv_b, conv_ln_g, conv_ln_b)
    q = (rms_norm(q_lat, q_norm) @ w_uq).reshape(b, s, MLA_HEADS, QK_NOPE + QK_ROPE)
    q_nope = q[..., :QK_NOPE]
    q_pe = apply_rope(q[..., QK_NOPE:], cos[:, :, None, :], sin[:, :, None, :])
    kv = (rms_norm(kv_lat, kv_norm) @ w_ukv).reshape(b, s, MLA_HEADS, QK_NOPE + V_HEAD)
    k_nope, v = kv[..., :QK_NOPE], kv[..., QK_NOPE:]
    k_pe = apply_rope(k_pe, cos, sin)
    attn = mla_attention(q_nope, q_pe, k_nope, k_pe, v).reshape(b, s, MLA_HEADS * V_HEAD)
    return jnp.concatenate([conv_out, attn], axis=-1) @ w_out


def rwkv7_time_mix(h, time_mu, w_r, w_k, w_v, w_o, w0, w1, w2, a0, a1, a2,
                   g1, g2, k_k, k_a, r_k, ln_x_g, ln_x_b):
    b, s, d = h.shape
    f32 = jnp.float32
    hh = jnp.pad(h, ((0, 0), (1, 0), (0, 0)))[:, :-1] - h
    xr, xw, xk, xv, xa, xg = [h + hh * time_mu[i] for i in range(N_SHIFT_MIX)]
    r = (xr @ w_r).astype(f32)
    k = (xk @ w_k).astype(f32)
    v = (xv @ w_v).astype(f32)
    w_log = -jax.nn.softplus(-(w0 + jnp.tanh(xw @ w1) @ w2).astype(f32)) - 0.5
    decay = jnp.exp(-jnp.exp(w_log))
    a = jax.nn.sigmoid((a0 + (xa @ a1) @ a2).astype(f32))
    g = (jax.nn.sigmoid(xg @ g1) @ g2).astype(f32)
    kk = (k * k_k).reshape(b, s, RWKV_HEADS, RWKV_HEAD)
    kk = kk / jnp.maximum(jnp.linalg.norm(kk, axis=-1, keepdims=True), 1e-12)
    k = k * (1.0 + (a - 1.0) * k_a)
    heads = lambda t: t.reshape(b, s, RWKV_HEADS, RWKV_HEAD)
    r, k, v, decay, a = heads(r), heads(k), heads(v), heads(decay), heads(a)

    def step(state, inp):
        r_t, w_t, k_t, v_t, kk_t, a_t = inp
        sa = jnp.einsum('bhvk,bhk->bhv', state, -kk_t)
        state = (state * w_t[:, :, None, :]
                 + sa[..., None] * (kk_t * a_t)[:, :, None, :]
                 + v_t[..., None] * k_t[:, :, None, :])
        return state, jnp.einsum('bhvk,bhk->bhv', state, r_t)

    xs = tuple(jnp.moveaxis(t, 1, 0) for t in (r, decay, k, v, kk, a))
    state0 = jnp.zeros((b, RWKV_HEADS, RWKV_HEAD, RWKV_HEAD), f32)
    _, y = lax.scan(step, state0, xs)
    y = jnp.moveaxis(y, 0, 1)
    mu = jnp.mean(y, axis=-1, keepdims=True)
    var = jnp.mean(jnp.square(y - mu), axis=-1, keepdims=True)
    y = ((y - mu) * lax.rsqrt(var + RWKV_GN_EPS)).reshape(b, s, d) * ln_x_g + ln_x_b
    bonus = jnp.sum(r * k * r_k, axis=-1, keepdims=True) * v
    y = y + bonus.reshape(b, s, d)
    return (y * g).astype(h.dtype) @ w_o


def setup_inputs(seed: int = 0) -> dict:
    key = jax.random.key(seed)
    keys = iter(jax.random.split(key, 48))
    f32 = jnp.float32
    nrm = lambda shape, scale: jax.random.normal(next(keys), shape, f32) * scale
    gain = lambda shape: 1.0 + nrm(shape, 0.02)
    ne, no = (DEPTH + 1) // 2, DEPTH // 2
    x = jax.random.normal(next(keys), (BATCH, SEQ, D_MODEL), f32)
    offset = jax.random.randint(next(keys), (BATCH, 1), 0, 1024, dtype=jnp.int32)
    positions = jnp.arange(SEQ, dtype=jnp.int32)[None, :] + offset
    return {
        'x': x,
        'positions': positions,
        'ffn_norm': gain((DEPTH, 2, D_MODEL)),
        'ffn_w_gate': nrm((DEPTH, 2, D_MODEL, D_FF), D_MODEL ** -0.5),
        'ffn_w_up': nrm((DEPTH, 2, D_MODEL, D_FF), D_MODEL ** -0.5),
        'ffn_w_down': nrm((DEPTH, 2, D_FF, D_MODEL), D_FF ** -0.5),
        'mix_norm_even': gain((ne, D_MODEL)),
        'w_in': nrm((ne, D_MODEL, IN_COLS), D_MODEL ** -0.5),
        'conv_w': nrm((ne, CONV_WIDTH, D_CONV), CONV_WIDTH ** -0.5),
        'conv_b': nrm((ne, D_CONV), 0.01),
        'conv_ln_g': gain((ne, D_CONV)),
        'conv_ln_b': nrm((ne, D_CONV), 0.01),
        'q_norm': gain((ne, Q_LORA)),
        'w_uq': nrm((ne, Q_LORA, MLA_HEADS * (QK_NOPE + QK_ROPE)), Q_LORA ** -0.5),
        'kv_norm': gain((ne, KV_LORA)),
        'w_ukv': nrm((ne, KV_LORA, MLA_HEADS * (QK_NOPE + V_HEAD)), KV_LORA ** -0.5),
        'w_out': nrm((ne, OUT_COLS, D_MODEL), OUT_COLS ** -0.5),
        'mix_norm_odd': gain((no, D_MODEL)),
        'time_mu': jax.random.uniform(next(keys), (no, N_SHIFT_MIX, D_MODEL), f32),
        'w_r': nrm((no, D_MODEL, D_MODEL), D_MODEL ** -0.5),
        'w_k': nrm((no, D_MODEL, D_MODEL), D_MODEL ** -0.5),
        'w_v': nrm((no, D_MODEL, D_MODEL), D_MODEL ** -0.5),
        'w_o': nrm((no, D_MODEL, D_MODEL), D_MODEL ** -0.5),
        'w0': jax.random.uniform(next(keys), (no, D_MODEL), f32, -5.0, -0.5),
        'w1': nrm((no, D_MODEL, DECAY_LORA), D_MODEL ** -0.5),
        'w2': nrm((no, DECAY_LORA, D_MODEL), DECAY_LORA ** -0.5),
        'a0': nrm((no, D_MODEL), 0.1),
        'a1': nrm((no, D_MODEL, AAA_LORA), D_MODEL ** -0.5),
        'a2': nrm((no, AAA_LORA, D_MODEL), AAA_LORA ** -0.5),
        'g1': nrm((no, D_MODEL, GATE_LORA), D_MODEL ** -0.5),
        'g2': nrm((no, GATE_LORA, D_MODEL), GATE_LORA ** -0.5),
        'k_k': 0.85 + nrm((no, D_MODEL), 0.02),
        'k_a': gain((no, D_MODEL)),
        'r_k': nrm((no, RWKV_HEADS, RWKV_HEAD), 0.1),
        'ln_x_g': gain((no, D_MODEL)),
        'ln_x_b': nrm((no, D_MODEL), 0.01),
        'final_norm': gain((D_MODEL,)),
    }


def reference(x, positions, ffn_norm, ffn_w_gate, ffn_w_up, ffn_w_down,
              mix_norm_even, w_in, conv_w, conv_b, conv_ln_g, conv_ln_b,
              q_norm, w_uq, kv_norm, w_ukv, w_out,
              mix_norm_odd, time_mu, w_r, w_k, w_v, w_o, w0, w1, w2,
              a0, a1, a2, g1, g2, k_k, k_a, r_k, ln_x_g, ln_x_b, final_norm):
    cos, sin = rope_tables(positions)
    for layer in range(DEPTH):
        x = x + FFN_RES_WEIGHT * swiglu_ffn(rms_norm(x, ffn_norm[layer, 0]),
                                            ffn_w_gate[layer, 0], ffn_w_up[layer, 0],
                                            ffn_w_down[layer, 0])
        if layer % 2 == 0:
            e = layer // 2
            x = x + conv_mla_mixer(rms_norm(x, mix_norm_even[e]), cos, sin, w_in[e],
                                   conv_w[e], conv_b[e], conv_ln_g[e], conv_ln_b[e],
                                   q_norm[e], w_uq[e], kv_norm[e], w_ukv[e], w_out[e])
        else:
            o = layer // 2
            x = x + rwkv7_time_mix(rms_norm(x, mix_norm_odd[o]), time_mu[o], w_r[o], w_k[o],
                                   w_v[o], w_o[o], w0[o], w1[o], w2[o], a0[o], a1[o],
                                   a2[o], g1[o], g2[o], k_k[o], k_a[o], r_k[o],
                                   ln_x_g[o], ln_x_b[o])
        x = x + FFN_RES_WEIGHT * swiglu_ffn(rms_norm(x, ffn_norm[layer, 1]),
                                            ffn_w_gate[layer, 1], ffn_w_up[layer, 1],
                                            ffn_w_down[layer, 1])
    return rms_norm(x, final_norm)
```

```python
import numpy as np
import concourse.bass as bass
import concourse.mybir as mybir
from concourse.bass_utils import run_bass_kernel_spmd

F32 = mybir.dt.float32
F32R = mybir.dt.float32r
BF16 = mybir.dt.bfloat16
I32 = mybir.dt.int32
AF = mybir.ActivationFunctionType
ALU = mybir.AluOpType
AX = mybir.AxisListType

D = 1024
S = 2048
DFF = 2816
NCH = 8
TT = 512
NTT = S // TT
EPS = 1e-6
NCORES = 8
SEQ_PER_CORE = 2

ENGS = ("pe", "act", "dve", "pool", "sp")


class _Op:
    __slots__ = ("eng", "fn", "waits", "key", "idx", "needed", "value", "is_dma", "stage", "stream")

    def __init__(self, eng, fn, key, idx, is_dma):
        self.eng = eng
        self.fn = fn
        self.waits = []
        self.key = key
        self.idx = idx
        self.needed = is_dma
        self.value = None
        self.is_dma = is_dma
        self.stage = None
        self.stream = None


class Prog:
    def __init__(self, nc):
        self.nc = nc
        self.ops = {e: [] for e in ENGS}
        self.bykey = {}
        self.last_w = {}
        self.readers = {}
        self.seen = {e: {} for e in ENGS}
        self.stage = None
        self.stream = None
        self.last_in_stream = {}
        self.prefix = ''
        self.annotate = False

    def _need(self, eng, prod, waits):
        if prod is None:
            return
        if prod.key == eng and eng == "pe" and not prod.is_dma:
            return
        sk = self.seen[eng]
        if sk.get(prod.key, -1) >= prod.idx:
            return
        sk[prod.key] = prod.idx
        prod.needed = True
        waits.append(prod)

    def op(self, eng, meth, kw=None, reads=(), writes=(), dma_key=None):
        fn = None if meth is None else (meth, kw)
        is_dma = dma_key is not None
        key = ("dma", dma_key) if is_dma else eng
        lst = self.bykey.setdefault(key, [])
        o = _Op(eng, fn, key, len(lst), is_dma)
        o.stage = (self.prefix + self.stage) if self.stage else None
        o.stream = self.stream
        if self.stream is not None and fn is not None:
            self.last_in_stream[(self.stream, key)] = o
        lst.append(o)
        waits = []
        for t in reads:
            self._need(eng, self.last_w.get(t), waits)
            if t[0] == "ps":
                for r in self.readers.get(t, ()):
                    if r.eng != eng:
                        self._need(eng, r, waits)
        for t in writes:
            self._need(eng, self.last_w.get(t), waits)
            for r in self.readers.get(t, ()):
                if r.key == eng and not r.is_dma:
                    continue
                self._need(eng, r, waits)
        best = {}
        for w in waits:
            if w.key not in best or best[w.key].idx < w.idx:
                best[w.key] = w
        o.waits = list(best.values())
        for t in reads:
            self.readers.setdefault(t, []).append(o)
        for t in writes:
            self.last_w[t] = o
            self.readers[t] = []
        self.ops[eng].append(o)
        return o

    def barrier(self):
        lasts = [lst[-1] for lst in self.bykey.values() if lst]
        for eng in ENGS:
            o = _Op(eng, None, eng, len(self.bykey.setdefault(eng, [])), False)
            self.bykey[eng].append(o)
            waits = []
            for p in lasts:
                if p.fn is None and not p.is_dma:
                    continue
                if p.key == eng and not p.is_dma and eng == "pe":
                    continue
                self._need(eng, p, waits)
            o.waits = waits
            self.ops[eng].append(o)

    def stream_barrier(self, stream):
        lasts = [o for (st, k), o in self.last_in_stream.items() if st == stream]
        for eng in ENGS:
            o = _Op(eng, None, eng, len(self.bykey.setdefault(eng, [])), False)
            self.bykey[eng].append(o)
            waits = []
            for p in lasts:
                if p.key == eng and not p.is_dma and eng == "pe":
                    continue
                self._need(eng, p, waits)
            o.waits = waits
            self.ops[eng].append(o)

    def finalize_and_emit(self):
        nc = self.nc
        keys = list(self.bykey.keys())
        for k in keys:
            cnt = 0
            for o in self.bykey[k]:
                if o.needed:
                    cnt += 16 if o.is_dma else 1
                    o.value = cnt
            assert cnt < 60000, (k, cnt)
        import contextlib
        with contextlib.ExitStack() as st:
            sems = {}
            for i, k in enumerate(keys):
                sems[k] = st.enter_context(nc.semaphore("s%d" % i))
            block = st.enter_context(nc.Block())

            def run(engname):
                def body(e):
                    for o in self.ops[engname]:
                        for w in o.waits:
                            e.wait_ge(sems[w.key], w.value)
                        if o.fn is None:
                            continue
                        ins = getattr(e, o.fn[0])(**o.fn[1])
                        if self.annotate and o.stage:
                            ins.annotate(o.stage)
                        if o.needed:
                            ins.then_inc(sems[o.key], 16 if o.is_dma else 1)
                return body

            block.sync(run("sp"))
            block.scalar(run("act"))
            block.vector(run("dve"))
            block.gpsimd(run("pool"))
            block.tensor(run("pe"))


class Arena:
    def __init__(self, base_ap_u8, size):
        self.base = base_ap_u8
        self.size = size
        self.off = 0
        self.stack = []

    def push(self):
        self.stack.append(self.off)

    def pop(self):
        self.off = self.stack.pop()

    def alloc(self, shape_free, dtype, parts=128):
        esz = mybir.dt.size(dtype)
        n = 1
        for d in shape_free:
            n *= d
        nbytes = n * esz
        self.off = (self.off + 63) // 64 * 64
        assert self.off + nbytes <= self.size, ("SBUF arena overflow", self.off, nbytes, self.size)
        v = self.base[0:parts, self.off:self.off + nbytes].bitcast(dtype)
        self.off += nbytes
        if len(shape_free) == 2:
            v = v.rearrange("p (a b) -> p a b", a=shape_free[0])
        elif len(shape_free) == 3:
            v = v.rearrange("p (a b c) -> p a b c", a=shape_free[0], b=shape_free[1])
        return v


class Ctx:
    pass


def _tags(name, *idx):
    return (name,) + tuple(idx)


def MM(P, out, lhsT, rhs, start, stop, reads, writes):
    P.op("pe", "matmul", dict(out=out, lhsT=lhsT, rhs=rhs, start=start, stop=stop), reads=reads, writes=writes)


def ACT(P, out, in_, func, reads, writes, bias=None, scale=None):
    kw = dict(out=out, in_=in_, func=func)
    if bias is not None:
        kw["bias"] = bias
    if scale is not None:
        kw["scale"] = scale
    P.op("act", "activation", kw, reads=reads, writes=writes)


def TT_(P, eng, out, in0, in1, op, reads, writes):
    P.op(eng, "tensor_tensor", dict(out=out, in0=in0, in1=in1, op=op), reads=reads, writes=writes)


def TS(P, eng, out, in0, s1, s2, op0, op1, reads, writes):
    kw = dict(out=out, in0=in0, scalar1=s1, scalar2=s2, op0=op0)
    if op1 is not None:
        kw["op1"] = op1
    P.op(eng, "tensor_scalar", kw, reads=reads, writes=writes)


def STT(P, out, in0, scalar, in1, op0, op1, reads, writes):
    P.op("dve", "scalar_tensor_tensor", dict(out=out, in0=in0, scalar=scalar, in1=in1, op0=op0, op1=op1),
         reads=reads, writes=writes)


def DMA(P, q, out, in_, reads, writes, key):
    P.op(q, "dma_start", dict(out=out, in_=in_), reads=reads, writes=writes, dma_key=key)


def load_x(C, xs, s):
    src, stag = C.xsrc[s]
    for tt in range(NTT):
        sl = slice(tt * TT, (tt + 1) * TT)
        DMA(C.P, "sp", xs[:, :, sl], src[:, :, sl], [(stag, s, tt)], [("x", tt)], ("xl", tt))


def store_x(C, xs, s, dst=None, dtag="xres"):
    if dst is None:
        dst = C.xres[s]
        C.xsrc[s] = (C.xres[s], "xres")
    for tt in range(NTT):
        sl = slice(tt * TT, (tt + 1) * TT)
        DMA(C.P, "sp", dst[:, :, sl], xs[:, :, sl], [("x", tt)], [(dtag, s, tt)], ("xs", tt))


def rms_to(C, src_fn, nchunk, gain_ap, out_fn, src_tags, out_tags, dim, eps=EPS, parts=128, ones=None):
    P = C.P
    n = src_fn(0).shape[-1]
    ps = C.psum[7][0:parts, 0:n]
    ones = C.ones_f32 if ones is None else ones
    rstd = C.rstd[0:parts, 0:n]
    for c in range(nchunk):
        nsq = len(C.sq)
        sq = C.sq[c % nsq][0:parts, 0:n]
        ACT(P, sq, src_fn(c), AF.Square, [src_tags(c)], [("sq", c % nsq)])
        MM(P, ps, ones[0:parts, 0:parts], sq, c == 0, c == nchunk - 1, [("sq", c % nsq), ("const",)], [("ps", 7)])
    ACT(P, rstd, ps, AF.Ln, [("ps", 7), ("const",)], [("rstd",)], bias=C.eps_ap(eps)[0:parts], scale=1.0 / dim)
    ACT(P, rstd, rstd, AF.Exp, [("rstd",)], [("rstd",)], scale=-0.5)
    for c in range(nchunk):
        STT(P, out_fn(c), src_fn(c), gain_ap[0:parts, c:c + 1], rstd, ALU.mult, ALU.mult,
            [src_tags(c), ("rstd",), ("gains",)], [out_tags(c)])


def phase_ffn(C, s, fi, load=True, store=True, final=False):
    P = C.P
    A = C.arena
    if load:
        P.barrier()
    A.push()
    xs = A.alloc([NCH, S], F32)
    hT = A.alloc([NCH, S], BF16)
    wg = [A.alloc([NCH, 512], BF16) for _ in range(2)]
    wu = [A.alloc([NCH, 512], BF16) for _ in range(2)]
    wd = [A.alloc([4, D], BF16) for _ in range(2)]
    act = [A.alloc([4, TT], BF16) for _ in range(2)]
    sg = [A.alloc([TT], F32) for _ in range(2)]
    C.sq = [A.alloc([TT], F32) for _ in range(8)]
    C.rstd = A.alloc([TT], F32)

    P.stage = "ffn.norm"
    if load:
        load_x(C, xs, s)
    g = C.gains[:, fi * NCH:(fi + 1) * NCH]

    def norm(tt):
        if tt >= NTT:
            return
        sl = slice(tt * TT, (tt + 1) * TT)
        P.stage = "ffn.norm"
        rms_to(C, lambda c: xs[:, c, sl], NCH, g, lambda c: hT[:, c, sl],
               lambda c: ("x", tt), lambda c: ("hT", tt), D)
        P.stage = "ffn.main"

    groups = [(0, 4), (4, 8), (8, 12), (12, 16), (16, 20), (20, 22)]

    def wload(gi):
        if gi >= len(groups):
            return
        c0, c1 = groups[gi]
        b = gi % 2
        nf = c1 - c0
        f0, f1 = c0 * 128, c1 * 128
        DMA(P, "pool", wg[b][:, :, 0:nf * 128], C.w_gate[fi][:, :, f0:f1], [], [("wg", b)], ("wg", b))
        DMA(P, "pool", wu[b][:, :, 0:nf * 128], C.w_up[fi][:, :, f0:f1], [], [("wu", b)], ("wu", b))
        DMA(P, "pool", wd[b][:, 0:nf, :], C.w_down[fi][:, c0:c1, :], [], [("wd", b)], ("wd", b))

    wload(0)
    norm(0)
    P.stage = "ffn.main"
    it = 0
    for gi, (c0, c1) in enumerate(groups):
        b = gi % 2
        nf = c1 - c0
        wload(gi + 1)
        for tt in range(NTT):
            sl = slice(tt * TT, (tt + 1) * TT)
            ab = it % 2
            it += 1
            if gi == 0:
                norm(tt + 1)
            for fc in range(nf):
                pg = fc % 2
                psG = C.psum[pg * 2]
                psU = C.psum[pg * 2 + 1]
                for k in range(NCH):
                    MM(P, psG, wg[b][:, k, fc * 128:(fc + 1) * 128], hT[:, k, sl], k == 0, k == NCH - 1,
                       [("wg", b), ("hT", tt)], [("ps", pg * 2)])
                for k in range(NCH):
                    MM(P, psU, wu[b][:, k, fc * 128:(fc + 1) * 128], hT[:, k, sl], k == 0, k == NCH - 1,
                       [("wu", b), ("hT", tt)], [("ps", pg * 2 + 1)])
                ACT(P, sg[pg], psG, AF.Silu, [("ps", pg * 2)], [("sg", pg)])
                TT_(P, "dve", act[ab][:, fc, :], psU, sg[pg], ALU.mult, [("ps", pg * 2 + 1), ("sg", pg)], [("act", ab, fc)])
            for dc in range(NCH):
                pb = 4 + dc % 3
                psY = C.psum[pb]
                for fc in range(nf):
                    MM(P, psY, wd[b][:, fc, dc * 128:(dc + 1) * 128], act[ab][:, fc, :], fc == 0, fc == nf - 1,
                       [("wd", b), ("act", ab, fc)], [("ps", pb)])
                STT(P, xs[:, dc, sl], psY, 0.5, xs[:, dc, sl], ALU.mult, ALU.add, [("ps", pb), ("x", tt)], [("x", tt)])
    if final:
        P.stage = "final"
        gf = C.gains[:, C.G_FINAL * NCH:(C.G_FINAL + 1) * NCH]
        for tt in range(NTT):
            sl = slice(tt * TT, (tt + 1) * TT)
            rms_to(C, lambda c: xs[:, c, sl], NCH, gf, lambda c: xs[:, c, sl],
                   lambda c: ("x", tt), lambda c: ("x", tt), D)
        store_x(C, xs, s, dst=C.out[s], dtag="out")
    elif store:
        store_x(C, xs, s)
    A.pop()


def phase_final(C, s):
    P = C.P
    A = C.arena
    P.barrier()
    A.push()
    xs = A.alloc([NCH, S], F32)
    C.sq = [A.alloc([TT], F32) for _ in range(2)]
    C.rstd = A.alloc([TT], F32)
    P.stage = "final"
    load_x(C, xs, s)
    g = C.gains[:, C.G_FINAL * NCH:(C.G_FINAL + 1) * NCH]
    for tt in range(NTT):
        sl = slice(tt * TT, (tt + 1) * TT)
        rms_to(C, lambda c: xs[:, c, sl], NCH, g, lambda c: xs[:, c, sl],
               lambda c: ("x", tt), lambda c: ("x", tt), D)
    store_x(C, xs, s, dst=C.out[s], dtag="out")
    A.pop()


def phase_dump(C, s):
    A = C.arena
    C.P.barrier()
    A.push()
    xs = A.alloc([NCH, S], F32)
    load_x(C, xs, s)
    store_x(C, xs, s, dst=C.out[s], dtag="out")
    A.pop()

CW = 31
HALO = CW - 1
MLA_H = 8
ATT_SCALE = (128 + 64) ** -0.5
PI = float(np.pi)


def rope_tables(C, s, CCt, SSt, tmpA, tmpB, tmpI):
    P = C.P
    p64 = slice(0, 64)
    DMA(P, "sp", tmpI[p64, :], C.pos[s].partition_broadcast(64), [], [("ropeI",)], ("rope",))
    P.op("dve", "tensor_copy", dict(out=tmpA[p64, :], in_=tmpI[p64, :]), reads=[("ropeI",)], writes=[("ropeA",)])
    ang = tmpA
    TS(P, "dve", ang[p64, :], tmpA[p64, :], C.ropec[p64, 0:1], None, ALU.mult, None, [("ropeA",), ("const",)], [("ropeA",)])
    for tbl, col, tag in ((CCt, 1, "CC"), (SSt, 2, "SS")):
        t = tbl
        TS(P, "dve", t[p64, :], ang[p64, :], C.ropec[p64, col:col + 1], None, ALU.add, None, [("ropeA",), ("const",)], [(tag,)])
        TS(P, "dve", tmpB[p64, :], t[p64, :], 1.0 / (2 * PI), None, ALU.mult, None, [(tag,)], [("ropeB",)])
        P.op("dve", "tensor_copy", dict(out=tmpI[p64, :], in_=tmpB[p64, :]), reads=[("ropeB",)], writes=[("ropeI",)])
        P.op("dve", "tensor_copy", dict(out=tmpB[p64, :], in_=tmpI[p64, :]), reads=[("ropeI",)], writes=[("ropeB",)])
        STT(P, t[p64, :], tmpB[p64, :], -6.28125, t[p64, :], ALU.mult, ALU.add, [("ropeB",), (tag,)], [(tag,)])
        STT(P, t[p64, :], tmpB[p64, :], -(2 * PI - 6.28125), t[p64, :], ALU.mult, ALU.add, [("ropeB",), (tag,)], [(tag,)])
        TS(P, "dve", t[p64, :], t[p64, :], -PI, PI, ALU.max, ALU.min, [(tag,)], [(tag,)])
        ACT(P, tbl[p64, :], t[p64, :], AF.Sin, [(tag,)], [(tag,)])


def phase_mix0(C, s):
    P = C.P
    A = C.arena
    P.barrier()
    A.push()
    hT = A.alloc([NCH, S], BF16)
    C.sq = [A.alloc([TT], F32) for _ in range(2)]
    C.rstd = A.alloc([TT], F32)
    p64 = slice(0, 64)
    src, stag = C.xsrc[s]
    g = C.gains[:, 4 * NCH:5 * NCH]

    P.stage = 'm0.A3'
    A.push()
    xt = A.alloc([NCH, TT], F32)
    wa = [A.alloc([NCH, 128], BF16) for _ in range(2)]
    wgt = [A.alloc([NCH, 128], BF16) for _ in range(2)]
    woc = A.alloc([NCH, D], BF16)
    glu = A.alloc([NCH, HALO + TT], BF16)
    dg = [A.alloc([CW, 128], BF16) for _ in range(2)]
    hc = [A.alloc([NCH, TT], F32) for _ in range(2)]
    co = A.alloc([NCH, TT], BF16)
    sig = [A.alloc([TT], F32) for _ in range(2)]
    xt1 = A.alloc([NCH, TT], F32)
    mean = A.alloc([TT], F32)
    nmr = A.alloc([TT], F32)
    var = A.alloc([TT], F32)
    DMA(P, "pool", woc, C.m0_wout[:, 0:NCH, :], [], [("woc",)], ("woc",))
    cv = C.m0_cvec
    NIT = NTT * NCH

    def xload(tt):
        if tt >= NTT:
            return
        sl = slice(tt * TT, (tt + 1) * TT)
        DMA(P, "sp", xt, src[:, :, sl], [(stag, s, tt)], [("xt",)], ("xt",))

    def norm(tt):
        if tt >= NTT:
            return
        sl = slice(tt * TT, (tt + 1) * TT)
        P.stage = 'm0.A1'
        rms_to(C, lambda c: xt[:, c, :], NCH, g, lambda c: hT[:, c, sl],
               lambda c: ("xt",), lambda c: ("hT", tt), D)
        xload(tt + 1)
        P.stage = 'm0.A3'

    def wload(i):
        if i >= NIT:
            return
        c, b = i % NCH, i % 2
        DMA(P, "pool", wa[b], C.m0_win_conv[:, c], [], [("wa", b)], ("wa", b))
        DMA(P, "pool", wgt[b], C.m0_win_conv[:, NCH + c], [], [("wgt", b)], ("wgt", b))

    def ag(i):
        if i >= NIT:
            return
        tt, c, b = i // NCH, i % NCH, i % 2
        sl = slice(tt * TT, (tt + 1) * TT)
        for k in range(NCH):
            MM(P, C.psum[2 * b], wa[b][:, k, :], hT[:, k, sl], k == 0, k == NCH - 1, [("wa", b), ("hT", tt)], [("ps", 2 * b)])
        for k in range(NCH):
            MM(P, C.psum[2 * b + 1], wgt[b][:, k, :], hT[:, k, sl], k == 0, k == NCH - 1, [("wgt", b), ("hT", tt)], [("ps", 2 * b + 1)])

    def diag(i):
        if i >= NIT:
            return
        c, b = i % NCH, i % 2
        for j in range(CW):
            if j % 3 == 2:
                TS(P, "dve", dg[b][:, j, :], C.rw_id, cv[:, c, j:j + 1], None, ALU.mult, None, [("const",)], [("dg", b, j)])
            else:
                ACT(P, dg[b][:, j, :], C.rw_id, AF.Copy, [("const",)], [("dg", b, j)], scale=cv[:, c, j:j + 1])

    def ln_piece(tt, k):
        par = tt % 2
        h_ = hc[par]
        sl = slice(tt * TT, (tt + 1) * TT)
        if k == 0:
            DMA(P, "sp", xt1, src[:, :, sl], [(stag, s, tt)], [("xt1",)], ("xt1",))
            for c2 in range(NCH):
                MM(P, C.psum[7], C.ones_f32, h_[:, c2, :], c2 == 0, c2 == NCH - 1, [("hc", par, c2), ("const",)], [("ps", 7)])
            for c2 in range(NCH):
                sq = C.sq[c2 % 2]
                ACT(P, sq, h_[:, c2, :], AF.Square, [("hc", par, c2)], [("sq", c2 % 2)])
                MM(P, C.psum[6], C.ones_f32, sq, c2 == 0, c2 == NCH - 1, [("sq", c2 % 2), ("const",)], [("ps", 6)])
        elif k == 1:
            TS(P, "dve", mean, C.psum[7], 1.0 / D, None, ALU.mult, None, [("ps", 7)], [("mean",)])
            TT_(P, "dve", var, mean, mean, ALU.mult, [("mean",)], [("var",)])
            STT(P, var, C.psum[6], 1.0 / D, var, ALU.mult, ALU.subtract, [("ps", 6), ("var",)], [("var",)])
            ACT(P, var, var, AF.Ln, [("var",), ("const",)], [("var",)], bias=C.eps_ap(1e-5), scale=1.0)
            ACT(P, var, var, AF.Exp, [("var",)], [("var",)], scale=-0.5)
            STT(P, nmr, mean, -1.0, var, ALU.mult, ALU.mult, [("mean",), ("var",)], [("nmr",)])
        elif k in (2, 3):
            for c2 in range((k - 2) * 4, (k - 1) * 4):
                TT_(P, "dve", h_[:, c2, :], h_[:, c2, :], var, ALU.mult, [("hc", par, c2), ("var",)], [("hc", par, c2)])
                TT_(P, "dve", h_[:, c2, :], h_[:, c2, :], nmr, ALU.add, [("hc", par, c2), ("nmr",)], [("hc", par, c2)])
                ACT(P, co[:, c2, :], h_[:, c2, :], AF.Silu, [("hc", par, c2), ("const",)], [("co", c2)],
                    bias=cv[:, c2, 33:34], scale=cv[:, c2, 32:33])
        else:
            for dc in range((k - 4) * 4, (k - 3) * 4):
                pb = 6 + dc % 2
                for c2 in range(NCH):
                    MM(P, C.psum[pb], woc[:, c2, dc * 128:(dc + 1) * 128], co[:, c2, :], c2 == 0, c2 == NCH - 1,
                       [("woc",), ("co", c2)], [("ps", pb)])
                TT_(P, "dve", xt1[:, dc, :], C.psum[pb], xt1[:, dc, :], ALU.add, [("ps", pb), ("xt1",)], [("xt1",)])
            if k == 5:
                DMA(P, "sp", C.xres[s][:, :, sl], xt1, [("xt1",)], [("xres", s, tt)], ("xt1s",))

    xload(0)
    norm(0)
    wload(0)
    wload(1)
    ag(0)
    diag(0)
    for i in range(NIT):
        tt, c, b = i // NCH, i % NCH, i % 2
        par = tt % 2
        if c == 2:
            norm(tt + 1)
        ag(i + 1)
        wload(i + 2)
        if tt == 0:
            P.op("dve", "memset", dict(ap=glu[:, c, 0:HALO], constant=0.0), writes=[("glu", c)])
        else:
            P.op("dve", "tensor_copy", dict(out=glu[:, c, 0:HALO], in_=glu[:, c, TT:TT + HALO]),
                 reads=[("glu", c)], writes=[("glu", c)])
        ACT(P, sig[b], C.psum[2 * b + 1], AF.Sigmoid, [("ps", 2 * b + 1)], [("sig", b)])
        TT_(P, "dve", glu[:, c, HALO:HALO + TT], C.psum[2 * b], sig[b], ALU.mult, [("ps", 2 * b), ("sig", b)], [("glu", c)])
        diag(i + 1)
        psC = C.psum[4 + b]
        for j in range(CW):
            MM(P, psC, dg[b][:, j, :], glu[:, c, j:j + TT], j == 0, j == CW - 1, [("dg", b, j), ("glu", c)], [("ps", 4 + b)])
        ACT(P, hc[par][:, c, :], psC, AF.Identity, [("ps", 4 + b)], [("hc", par, c)], bias=cv[:, c, 31:32])
        if tt > 0 and c < 6:
            ln_piece(tt - 1, c)
    for k in range(6):
        ln_piece(NTT - 1, k)
    C.xsrc[s] = (C.xres[s], "xres")
    src, stag = C.xsrc[s]
    P.barrier()
    A.pop()

    P.stage = 'm0.A2'
    qn = A.alloc([4, S], BF16)
    kvn = A.alloc([2, S], BF16)
    kpe = A.alloc([S], BF16)
    CCt = A.alloc([S], F32)
    SSt = A.alloc([S], F32)
    A.push()
    tmpA = A.alloc([S], F32)
    tmpB = A.alloc([S], F32)
    tmpI = A.alloc([S], I32)
    wl = A.alloc([NCH, 896], BF16)
    t1 = A.alloc([TT], F32)
    t2 = A.alloc([TT], F32)
    DMA(P, "pool", wl, C.m0_win_lat, [], [("wl",)], ("wl",))
    rope_tables(C, s, CCt, SSt, tmpA, tmpB, tmpI)
    for tt in range(NTT):
        sl = slice(tt * TT, (tt + 1) * TT)
        for m in range(4):
            for k in range(NCH):
                MM(P, C.psum[m], wl[:, k, m * 128:(m + 1) * 128], hT[:, k, sl], k == 0, k == NCH - 1,
                   [("wl",), ("hT", tt)], [("ps", m)])
        rms_to(C, lambda m: C.psum[m], 4, C.m0_vec[:, 0:4], lambda m: qn[:, m, sl],
               lambda m: ("ps", m), lambda m: ("qn", tt), 512)
        for m in range(2):
            for k in range(NCH):
                MM(P, C.psum[4 + m], wl[:, k, 512 + m * 128:512 + (m + 1) * 128], hT[:, k, sl], k == 0, k == NCH - 1,
                   [("wl",), ("hT", tt)], [("ps", 4 + m)])
        rms_to(C, lambda m: C.psum[4 + m], 2, C.m0_vec[:, 4:6], lambda m: kvn[:, m, sl],
               lambda m: ("ps", 4 + m), lambda m: ("kvn", tt), 256)
        for k in range(NCH):
            MM(P, C.psum[6][p64, :], wl[:, k, 768:832], hT[:, k, sl], k == 0, k == NCH - 1, [("wl",), ("hT", tt)], [("ps", 6)])
        for k in range(NCH):
            MM(P, C.psum[0][p64, :], wl[:, k, 832:896], hT[:, k, sl], k == 0, k == NCH - 1, [("wl",), ("hT", tt)], [("ps", 0)])
        TT_(P, "dve", t1[p64, :], C.psum[6][p64, :], CCt[p64, sl], ALU.mult, [("ps", 6), ("CC",)], [("t1",)])
        TT_(P, "dve", t2[p64, :], C.psum[0][p64, :], SSt[p64, sl], ALU.mult, [("ps", 0), ("SS",)], [("t2",)])
        TT_(P, "dve", kpe[p64, sl], t1[p64, :], t2[p64, :], ALU.add, [("t1",), ("t2",)], [("kpe", tt)])
    P.barrier()
    A.pop()

    P.stage = 'm0.C'
    attnT = hT
    A.push()
    wq = [A.alloc([4, 256], BF16) for _ in range(2)]
    wkv = [A.alloc([2, 256], BF16) for _ in range(2)]
    qno = A.alloc([S], BF16)
    qpe = A.alloc([S], BF16)
    kno = A.alloc([S], BF16)
    vh = A.alloc([S // 128, 128], BF16)
    Eb = [A.alloc([TT], BF16) for _ in range(3)]
    rden = A.alloc([TT], F32)
    t1 = A.alloc([TT], F32)
    t2 = A.alloc([TT], F32)
    for h in range(MLA_H):
        b = h % 2
        P.stage = 'm0.Cprep'
        DMA(P, "pool", wq[b], C.m0_wuq[:, h], [], [("wq", b)], ("wq", b))
        DMA(P, "pool", wkv[b], C.m0_wukv[:, h], [], [("wkv", b)], ("wkv", b))
        for tt in range(NTT):
            sl = slice(tt * TT, (tt + 1) * TT)
            for l in range(4):
                MM(P, C.psum[0], wq[b][:, l, 0:128], qn[:, l, sl], l == 0, l == 3, [("wq", b), ("qn", tt)], [("ps", 0)])
            ACT(P, qno[:, sl], C.psum[0], AF.Copy, [("ps", 0)], [("qno", tt)])
            for l in range(4):
                MM(P, C.psum[1][p64, :], wq[b][:, l, 128:192], qn[:, l, sl], l == 0, l == 3, [("wq", b), ("qn", tt)], [("ps", 1)])
            for l in range(4):
                MM(P, C.psum[2][p64, :], wq[b][:, l, 192:256], qn[:, l, sl], l == 0, l == 3, [("wq", b), ("qn", tt)], [("ps", 2)])
            TT_(P, "dve", t1[p64, :], C.psum[1][p64, :], CCt[p64, sl], ALU.mult, [("ps", 1), ("CC",)], [("t1",)])
            TT_(P, "dve", t2[p64, :], C.psum[2][p64, :], SSt[p64, sl], ALU.mult, [("ps", 2), ("SS",)], [("t2",)])
            TT_(P, "dve", qpe[p64, sl], t1[p64, :], t2[p64, :], ALU.add, [("t1",), ("t2",)], [("qpe", tt)])
            for l in range(2):
                MM(P, C.psum[3], wkv[b][:, l, 0:128], kvn[:, l, sl], l == 0, l == 1, [("wkv", b), ("kvn", tt)], [("ps", 3)])
            ACT(P, kno[:, sl], C.psum[3], AF.Copy, [("ps", 3)], [("kno", tt)])
            for i in range(4):
                tsl = slice(tt * TT + i * 128, tt * TT + (i + 1) * 128)
                for l in range(2):
                    MM(P, C.psum[4][:, i * 128:(i + 1) * 128], kvn[:, l, tsl], wkv[b][:, l, 128:256], l == 0, l == 1,
                       [("wkv", b), ("kvn", tt)], [("ps", 4)])
            P.op("dve", "tensor_copy", dict(out=vh[:, tt * 4:(tt + 1) * 4, :], in_=C.psum[4].rearrange("p (a b) -> p a b", a=4)),
                 reads=[("ps", 4)], writes=[("vh", tt)])
        ecount = 0
        P.stage = 'm0.Cattn'
        for qt in range(NTT):
            nk = 4 * (qt + 1)
            psO = C.psum[3 + qt % 2]
            psD = C.psum[5 + qt % 2]
            otag, dtag = ("ps", 3 + qt % 2), ("ps", 5 + qt % 2)

            def s_stage(kt, e):
                j = kt - 4 * qt
                c0 = max(j, 0) * 128
                qsl = slice(qt * TT + c0, (qt + 1) * TT)
                ksl = slice(kt * 128, (kt + 1) * 128)
                psS = C.psum[e]
                MM(P, psS[:, c0:TT], kno[:, ksl], qno[:, qsl], True, False, [("kno", kt // 4), ("qno", qt)], [("ps", e)])
                MM(P, psS[:, c0:TT], kpe[p64, ksl], qpe[p64, qsl], False, True, [("kpe", kt // 4), ("qpe", qt)], [("ps", e)])
                ACT(P, Eb[e][:, c0:TT], psS[:, c0:TT], AF.Exp, [("ps", e)], [("E", e)], scale=ATT_SCALE)
                if j >= 0:
                    TT_(P, "dve", Eb[e][:, c0:c0 + 128], Eb[e][:, c0:c0 + 128], C.tri_bf, ALU.mult,
                        [("E", e), ("const",)], [("E", e)])

            def o_stage(kt, e):
                j = kt - 4 * qt
                c0 = max(j, 0) * 128
                MM(P, psO[:, c0:TT], vh[:, kt, :], Eb[e][:, c0:TT], kt == 0, kt == nk - 1, [("vh", kt // 4), ("E", e)], [otag])
                MM(P, psD[:, c0:TT], C.ones_bf, Eb[e][:, c0:TT], kt == 0, kt == nk - 1, [("const",), ("E", e)], [dtag])

            es = [(ecount + kt) % 3 for kt in range(nk)]
            ecount += nk
            s_stage(0, es[0])
            for kt in range(nk):
                if kt + 1 < nk:
                    s_stage(kt + 1, es[kt + 1])
                o_stage(kt, es[kt])
            sl = slice(qt * TT, (qt + 1) * TT)
            P.op("dve", "reciprocal", dict(out=rden, in_=psD), reads=[dtag], writes=[("rden",)])
            TT_(P, "dve", attnT[:, h, sl], psO, rden, ALU.mult, [otag, ("rden",)], [("attnT", h, qt)])
    P.barrier()
    A.pop()

    P.stage = 'm0.D'
    A.push()
    wo = A.alloc([NCH, D], BF16)
    xt2 = [A.alloc([NCH, TT], F32) for _ in range(2)]
    DMA(P, "pool", wo, C.m0_wout[:, NCH:2 * NCH, :], [], [("wo",)], ("wo",))
    for tt in range(NTT):
        sl = slice(tt * TT, (tt + 1) * TT)
        b = tt % 2
        DMA(P, "sp", xt2[b], src[:, :, sl], [(stag, s, tt)], [("xt2", b)], ("xt2", b))
        for dc in range(NCH):
            pb = dc % 3
            for hh in range(MLA_H):
                MM(P, C.psum[pb], wo[:, hh, dc * 128:(dc + 1) * 128], attnT[:, hh, sl], hh == 0, hh == MLA_H - 1,
                   [("wo",), ("attnT", hh, tt)], [("ps", pb)])
            TT_(P, "dve", xt2[b][:, dc, :], C.psum[pb], xt2[b][:, dc, :], ALU.add, [("ps", pb), ("xt2", b)], [("xt2", b)])
        DMA(P, "sp", C.xres[s][:, :, sl], xt2[b], [("xt2", b)], [("xres", s, tt)], ("xt2s", b))
    A.pop()
    A.pop()

TR = 256
NTR = S // TR
RH = 16
NEG_EXP_HALF = -float(np.exp(-0.5))
HORDER = [0, 2, 4, 6, 8, 10, 12, 14, 1, 3, 5, 7, 9, 11, 13, 15]
HPOS = {h: i for i, h in enumerate(HORDER)}
NVEC = 14


def rwkv_decl(C, dt):
    C.rw_w4 = dt("rw_w4", [4, 128, NCH, D])
    C.rw_w4b = dt("rw_w4b", [4, 128, NCH, D], BF16, kind="Internal")
    C.rw_l1_d = dt("rw_l1", [128, NCH, 256])
    C.rw_l2_d = dt("rw_l2", [128, 3, D])
    C.rw_vec_d = dt("rw_vec", [128, NCH, NVEC])
    C.rw_cm_d = dt("rw_cm", [128, 5, 512])
    C.rw_bo_d = dt("rw_bo", [128, 128])
    C.rw_sm_d = dt("rw_sm", [128, TR])
    C.rw_id_d = dt("rw_id", [128, 128])


def rwkv_consts(C, CA):
    P = C.P
    C.rw_vec = CA.alloc([NCH, NVEC], F32)
    C.rw_cm = CA.alloc([5, 512], BF16)
    C.rw_bo = CA.alloc([128], F32)
    C.rw_sm = CA.alloc([TR], F32)
    C.rw_id = CA.alloc([128], BF16)
    DMA(P, "sp", C.rw_vec, C.rw_vec_d, [], [("c", 20)], "c20")
    DMA(P, "pool", C.rw_cm, C.rw_cm_d, [], [("c", 21)], "c21")
    DMA(P, "sp", C.rw_bo, C.rw_bo_d, [], [("c", 22)], "c22")
    DMA(P, "sp", C.rw_sm, C.rw_sm_d, [], [("c", 23)], "c23")
    DMA(P, "pool", C.rw_id, C.rw_id_d, [], [("c", 24)], "c24")
    A = C.arena
    A.push()
    tmpw = [A.alloc([NCH, D], BF16) for _ in range(2)]
    for i in range(4):
        DMA(P, "pool", tmpw[i % 2], C.rw_w4[i], [], [("tmpw", i % 2)], ("tmpw", i % 2))
        DMA(P, "sp", C.rw_w4b[i], tmpw[i % 2], [("tmpw", i % 2)], [("w4b", i)], ("tmpws", i % 2))
    A.pop()
    TS(P, "dve", C.rw_vec[:, :, 13:14], C.rw_vec[:, :, 7:8], -1.0, 1.0, ALU.mult, ALU.add, [("c", 20)], [("c", 20)])


def rwkv_prep(inp, sh):
    f32 = np.float32
    g = lambda n: np.asarray(inp[n], f32)[0]
    sh["rw_w4"] = np.stack([_fm(g("w_r"), NCH), _fm(g("w_k"), NCH), _fm(g("w_v"), NCH), _fm(g("w_o"), NCH)], 0)
    sh["rw_l1"] = _fm(np.concatenate([g("w1"), g("a1"), g("g1")], axis=1), NCH)
    l2 = np.zeros((128, 3, D), f32)
    l2[0:64, 0] = g("w2")
    l2[0:64, 1] = g("a2")
    l2[:, 2] = g("g2")
    sh["rw_l2"] = l2
    vec = np.zeros((128, NCH, NVEC), f32)
    mu = g("time_mu")
    for i in range(6):
        vec[:, :, i] = _vec(mu[i], NCH)
    for j, n in enumerate(["k_k", "k_a", "w0", "a0"]):
        vec[:, :, 6 + j] = _vec(g(n), NCH)
    vec[:, :, 10] = _vec(g("r_k").reshape(D), NCH)
    vec[:, :, 11] = _vec(g("ln_x_g"), NCH)
    vec[:, :, 12] = _vec(g("ln_x_b"), NCH)
    sh["rw_vec"] = vec
    one = np.ones((128, 128), f32)
    sl = np.tril(one, -1)
    su = np.triu(one, 1)
    ui = np.triu(one, 0)
    idn = np.eye(128, dtype=f32)
    bd = np.zeros((128, 128), f32)
    bd[0:64, 0:64] = 1
    bd[64:128, 64:128] = 1
    sh["rw_cm"] = np.ascontiguousarray(np.stack([np.tile(m, (1, 4)) for m in (sl, su, ui, idn, bd)], axis=1))
    sh["rw_bo"] = bd
    sm = np.ones((128, TR), f32)
    sm[:, 0::128] = 0
    sh["rw_sm"] = sm
    sh["rw_id"] = idn


def phase_rwkv(C, s):
    P = C.P
    A = C.arena
    P.barrier()
    A.push()
    vec = C.rw_vec
    SLm, SUm, UIm, ID4, BD4 = (C.rw_cm[:, i, :] for i in range(5))
    bank_ctr = [0]

    def nb():
        bank_ctr[0] = (bank_ctr[0] + 1) % 7
        return bank_ctr[0]

    Sbd32 = A.alloc([NCH, 128], F32)
    Sbd = A.alloc([NCH, 128], BF16)
    hprev = A.alloc([NCH, 1], F32)
    l1 = A.alloc([NCH, 256], BF16)
    l2 = A.alloc([3, D], BF16)
    wbuf = [A.alloc([NCH, D], BF16) for _ in range(2)]
    C.sq = [A.alloc([TR], F32) for _ in range(8)]
    C.rstd = A.alloc([TR], F32)
    AhT = A.alloc([NCH, TR], BF16)
    RT = A.alloc([NCH, TR], BF16)
    BT = A.alloc([NCH, TR], BF16)
    KT = A.alloc([NCH, TR], BF16)
    BpT = A.alloc([NCH, TR], BF16)
    KpT = A.alloc([NCH, TR], BF16)
    vb = A.alloc([NCH, TR], BF16)
    gT = A.alloc([NCH, TR], BF16)
    bonus = A.alloc([NCH, TR], F32)
    WLt = A.alloc([NCH, 2], F32)

    P.op("dve", "memset", dict(ap=Sbd32, constant=0.0), writes=[("Sbd32",)])
    P.op("dve", "memset", dict(ap=Sbd, constant=0.0), writes=[("Sbd",)])
    P.op("dve", "memset", dict(ap=hprev, constant=0.0), writes=[("hprev",)])
    DMA(P, "pool", l1, C.rw_l1_d, [], [("l1",)], ("l1",))
    DMA(P, "pool", l2, C.rw_l2_d, [], [("l2",)], ("l2",))
    g = C.gains[:, 5 * NCH:6 * NCH]
    wcnt = [0]

    def load_w(i):
        b = wcnt[0] % 2
        wcnt[0] += 1
        DMA(P, "sp", wbuf[b], C.rw_w4b[i], [("w4b", i)], [("wbuf", b)], ("wbuf", b))
        return b

    for ti in range(NTR):
        tsl = slice(ti * TR, (ti + 1) * TR)
        src, stag = C.xsrc[s]
        rtag = (stag, s, ti // 2)
        P.stage = 'rw.P#%d' % ti
        A.push()
        xh = A.alloc([NCH, TR + 1], F32)
        dd = A.alloc([NCH, TR], F32)
        xi2 = A.alloc([2, NCH, TR], BF16)
        xi = [xi2[:, 0], xi2[:, 1]]
        rr = A.alloc([NCH, TR], F32)
        kx = A.alloc([NCH, TR], F32)
        vv = A.alloc([NCH, TR], F32)
        lw = A.alloc([NCH, TR], F32)
        aa = A.alloc([NCH, TR], F32)
        kk = A.alloc([NCH, TR], F32)
        lt = A.alloc([TR], BF16)
        e_x1 = A.alloc([NCH, TR], F32)
        lastC = A.alloc([NCH, 2], F32)

        DMA(P, "sp", xh[:, :, 1:TR + 1], src[:, :, tsl], [rtag], [("xh",)], ("xh",))
        P.op("pool", "tensor_copy", dict(out=xh[:, :, 0:1], in_=hprev), reads=[("hprev",)], writes=[("xh0",)])
        rms_to(C, lambda c: xh[:, c, 1:TR + 1], NCH, g, lambda c: xh[:, c, 1:TR + 1],
               lambda c: ("xh",), lambda c: ("xh",), D)
        P.op("pool", "tensor_copy", dict(out=hprev, in_=xh[:, :, TR:TR + 1]), reads=[("xh",)], writes=[("hprev",)])
        TT_(P, "dve", dd, xh[:, :, 0:TR], xh[:, :, 1:TR + 1], ALU.subtract, [("xh",), ("xh0",)], [("dd",)])
        mixcnt = [0]

        def mix(i):
            b = mixcnt[0] % 2
            mixcnt[0] += 1
            for c in range(NCH):
                STT(P, xi[b][:, c, :], dd[:, c, :], vec[:, c, i:i + 1], xh[:, c, 1:TR + 1], ALU.mult, ALU.add,
                    [("dd",), ("xh",)], [("xi", b)])
            return b

        def big_proj(mi, wi, dst, dtag):
            b = mix(mi)
            wb = load_w(wi)
            for oc in range(NCH):
                pb = nb()
                for k in range(NCH):
                    MM(P, C.psum[pb][:, 0:TR], wbuf[wb][:, k, oc * 128:(oc + 1) * 128], xi[b][:, k, :], k == 0, k == NCH - 1,
                       [("wbuf", wb), ("xi", b)], [("ps", pb)])
                ACT(P, dst[:, oc, :], C.psum[pb][:, 0:TR], AF.Copy, [("ps", pb)], [(dtag,)])

        big_proj(0, 0, rr, "rr")
        big_proj(2, 1, kx, "kx")
        big_proj(3, 2, vv, "vv")

        def lora(mi, c0, c1, func, l2i, dst, dtag, fin, bias_col):
            b = mix(mi)
            nl = c1 - c0
            pb = nb()
            for k in range(NCH):
                MM(P, C.psum[pb][0:nl, 0:TR], l1[:, k, c0:c1], xi[b][:, k, :], k == 0, k == NCH - 1, [("l1",), ("xi", b)], [("ps", pb)])
            ACT(P, lt[0:nl, :], C.psum[pb][0:nl, 0:TR], func, [("ps", pb)], [("lt",)])
            for oc in range(NCH):
                pb = nb()
                MM(P, C.psum[pb][:, 0:TR], l2[0:nl, l2i, oc * 128:(oc + 1) * 128], lt[0:nl, :], True, True, [("l2",), ("lt",)], [("ps", pb)])
                if bias_col is None:
                    ACT(P, dst[:, oc, :], C.psum[pb][:, 0:TR], fin, [("ps", pb)], [(dtag,)])
                else:
                    ACT(P, dst[:, oc, :], C.psum[pb][:, 0:TR], fin, [("ps", pb)], [(dtag,)], bias=vec[:, oc, bias_col:bias_col + 1])

        lora(1, 0, 64, AF.Tanh, 0, lw, "lw", AF.Sigmoid, 8)
        lora(4, 64, 128, AF.Copy, 1, aa, "aa", AF.Sigmoid, 9)
        lora(5, 128, 256, AF.Sigmoid, 2, gT, "gT", AF.Copy, None)

        P.stage = 'rw.E#%d' % ti
        P.barrier()
        e_cw = dd
        e_x0 = xh[:, :, 0:TR]
        e_n = xi2.rearrange("p a c t -> p (a c t)").bitcast(F32).rearrange("p (c t) -> p c t", c=NCH)
        f2 = lambda a: a
        for c in range(NCH):
            ACT(P, kk[:, c, :], kx[:, c, :], AF.Copy, [("kx",)], [("kk",)], scale=vec[:, c, 6:7])
        for c in range(NCH):
            ACT(P, e_x0[:, c, :], kk[:, c, :], AF.Square, [("kk",)], [("e_x0",)])
        pbs = [nb() for _ in range(4)]
        for c in range(NCH):
            MM(P, C.psum[pbs[c // 2]][:, (c % 2) * TR:(c % 2 + 1) * TR], C.rw_bo, e_x0[:, c, :], True, True, [("e_x0",)], [("ps", pbs[c // 2])])
        for i in range(4):
            ACT(P, e_n[:, 2 * i:2 * i + 2, :], C.psum[pbs[i]].rearrange("p (a t) -> p a t", a=2), AF.Ln, [("ps", pbs[i])], [("e_n",)],
                bias=C.eps_ap(1e-24), scale=1.0)
        for c in range(NCH):
            P.op("dve", "tensor_tensor_scan", dict(out=e_cw[:, c, :], data0=C.rw_sm, data1=lw[:, c, :], initial=0.0, op0=ALU.mult, op1=ALU.add),
                 reads=[("lw",)], writes=[("e_cw",)])
        TS(P, "dve", lastC, e_cw[:, :, 127::128], NEG_EXP_HALF, None, ALU.mult, None, [("e_cw",)], [("lastC",)])
        TT_(P, "dve", f2(lw), f2(e_cw), f2(lw), ALU.subtract, [("e_cw",), ("lw",)], [("lw",)])
        for c in range(NCH):
            TS(P, "dve", e_x1[:, c, :], aa[:, c, :], vec[:, c, 7:8], vec[:, c, 13:14], ALU.mult, ALU.add, [("aa",)], [("e_x1",)])
        TT_(P, "dve", f2(kx), f2(kx), f2(e_x1), ALU.mult, [("kx",), ("e_x1",)], [("kx",)])
        ACT(P, f2(e_n), f2(e_n), AF.Exp, [("e_n",)], [("e_n",)], scale=-0.5)
        TT_(P, "dve", f2(kk), f2(kk), f2(e_n), ALU.mult, [("kk",), ("e_n",)], [("kk",)])
        TT_(P, "dve", f2(aa), f2(aa), f2(kk), ALU.mult, [("aa",), ("kk",)], [("aa",)])
        for c in range(NCH):
            STT(P, e_x1[:, c, :], rr[:, c, :], vec[:, c, 10:11], kx[:, c, :], ALU.mult, ALU.mult, [("rr",), ("kx",), ("e_x1",)], [("e_x1",)])
        pbs = [nb() for _ in range(4)]
        for c in range(NCH):
            MM(P, C.psum[pbs[c // 2]][:, (c % 2) * TR:(c % 2 + 1) * TR], C.rw_bo, e_x1[:, c, :], True, True, [("e_x1",)], [("ps", pbs[c // 2])])
        for i in range(4):
            TT_(P, "dve", bonus[:, 2 * i:2 * i + 2, :], C.psum[pbs[i]].rearrange("p (a t) -> p a t", a=2), vv[:, 2 * i:2 * i + 2, :], ALU.mult,
                [("ps", pbs[i]), ("vv",)], [("bonus",)])
        ACT(P, f2(vb), f2(vv), AF.Copy, [("vv",)], [("vb",)])
        ACT(P, f2(e_x0), f2(lw), AF.Exp, [("lw",)], [("e_x0",)], scale=NEG_EXP_HALF)
        STT(P, f2(AhT), f2(kk), -1.0, f2(e_x0), ALU.mult, ALU.mult, [("kk",), ("e_x0",)], [("AhT",)])
        ACT(P, f2(e_x1), f2(e_cw), AF.Exp, [("e_cw",), ("e_x1",)], [("e_x1",)], scale=NEG_EXP_HALF)
        TT_(P, "dve", f2(RT), f2(rr), f2(e_x1), ALU.mult, [("rr",), ("e_x1",)], [("RT",)])
        ACT(P, f2(e_x0), f2(e_cw), AF.Exp, [("e_cw",), ("e_x0",)], [("e_x0",)], scale=-NEG_EXP_HALF)
        TT_(P, "dve", f2(BT), f2(aa), f2(e_x0), ALU.mult, [("aa",), ("e_x0",)], [("BT",)])
        TT_(P, "dve", f2(KT), f2(kx), f2(e_x0), ALU.mult, [("kx",), ("e_x0",)], [("KT",)])
        for c in range(NCH):
            for q in range(2):
                qs = slice(q * 128, (q + 1) * 128)
                ACT(P, e_x1[:, c, qs], e_cw[:, c, qs], AF.Exp, [("e_cw",), ("lastC",), ("e_x1",)], [("e_x1",)],
                    bias=lastC[:, c, q:q + 1], scale=-NEG_EXP_HALF)
        ACT(P, f2(WLt), f2(lastC), AF.Exp, [("lastC",)], [("WLt",)])
        TT_(P, "dve", f2(BpT), f2(aa), f2(e_x1), ALU.mult, [("aa",), ("e_x1",)], [("BpT",)])
        TT_(P, "dve", f2(KpT), f2(kx), f2(e_x1), ALU.mult, [("kx",), ("e_x1",)], [("KpT",)])
        P.barrier()
        A.pop()

        A.push()
        Bptm = A.alloc([D], BF16)
        Kptm = A.alloc([D], BF16)
        Vtm = A.alloc([D], BF16)
        Mm = A.alloc([RH * 128], BF16)
        Mt = A.alloc([RH * 128], BF16)
        NrbT = A.alloc([RH * 128], BF16)
        NrkT = A.alloc([RH * 128], BF16)
        MakT = A.alloc([RH * 128], BF16)
        XT = A.alloc([RH * 128], BF16)
        Tpp = [[A.alloc([512], BF16) for _ in range(2)] for _ in range(4)]
        Ttpp = [[A.alloc([512], BF16) for _ in range(2)] for _ in range(4)]
        Zb = A.alloc([D], BF16)
        Ub = A.alloc([D], BF16)
        tmpm = A.alloc([D], F32)
        Ytm = A.alloc([2, D], F32)
        ynb = A.alloc([2, D], BF16)
        ysq = A.alloc([2, D], F32)
        gst = A.alloc([6, 2 * RH], F32)
        xt = A.alloc([NCH, TR], F32)
        yg = A.alloc([NCH, TR], BF16)
        ytmp = A.alloc([NCH, TR], F32)

        for q in range(2):
            qs = slice(q * 128, (q + 1) * 128)
            P.stage = 'rw.T#%d' % ti
            for arr, dst, rtag_, wtag in ((BpT, Bptm, "BpT", "Bptm"), (KpT, Kptm, "KpT", "Kptm"), (vb, Vtm, "vb", "Vtm")):
                pb = nb()
                psb = C.psum[pb].bitcast(BF16)
                for c in range(NCH):
                    P.op("pe", "transpose", dict(out=psb[:, c * 128:(c + 1) * 128], in_=arr[:, c, qs], identity=C.rw_id),
                         reads=[(rtag_,)], writes=[("ps", pb)])
                ACT(P, dst, psb, AF.Copy, [("ps", pb)], [(wtag,)])
            P.stage = 'rw.N#%d' % ti
            for gi in range(4):
                heads = HORDER[gi * 4:(gi + 1) * 4]
                gsl = slice(gi * 512, (gi + 1) * 512)
                specs = ((Mm, AhT, BT, SLm, "Mm"), (Mt, BT, AhT, SUm, "Mt"), (NrbT, BT, RT, UIm, "NrbT"),
                         (NrkT, KT, RT, UIm, "NrkT"), (MakT, KT, AhT, SUm, "MakT"))
                for dst, la, ra, mask, tg in specs:
                    pb = nb()
                    for i, h in enumerate(heads):
                        p_, rb = h // 2, 64 * (h % 2)
                        MM(P, C.psum[pb][:, i * 128:(i + 1) * 128], la[rb:rb + 64, p_, qs], ra[rb:rb + 64, p_, qs], True, True,
                           [("AhT",), ("BT",), ("RT",), ("KT",)], [("ps", pb)])
                    TT_(P, "dve", dst[:, gsl], C.psum[pb], mask, ALU.mult, [("ps", pb)], [(tg, gi)])
            P.stage = 'rw.Neu#%d' % ti
            cur = [0, 0, 0, 0]
            for gi in range(4):
                gsl = slice(gi * 512, (gi + 1) * 512)
                TT_(P, "dve", XT[:, gsl], Mt[:, gsl], ID4, ALU.add, [("Mt", gi)], [("XT", gi)])
            for step in range(6):
                for gi in range(4):
                    gsl = slice(gi * 512, (gi + 1) * 512)
                    if step == 0:
                        Tc, Ttc, tcr = Mm[:, gsl], Mt[:, gsl], [("Mm", gi), ("Mt", gi)]
                    else:
                        Tc, Ttc = Tpp[gi][cur[gi]], Ttpp[gi][cur[gi]]
                        tcr = [("Tpp", gi, cur[gi]), ("Ttpp", gi, cur[gi])]
                    nxt = 1 - cur[gi] if step > 0 else 0
                    pb = nb()
                    for i in range(4):
                        bs = slice(i * 128, (i + 1) * 128)
                        MM(P, C.psum[pb][:, bs], Ttc[:, bs], Tc[:, bs], True, True, tcr, [("ps", pb)])
                    ACT(P, Tpp[gi][nxt], C.psum[pb], AF.Copy, [("ps", pb)], [("Tpp", gi, nxt)])
                    if step < 5:
                        pb2 = nb()
                        for i in range(4):
                            bs = slice(i * 128, (i + 1) * 128)
                            MM(P, C.psum[pb2][:, bs], Tc[:, bs], Ttc[:, bs], True, True, tcr, [("ps", pb2)])
                        ACT(P, Ttpp[gi][nxt], C.psum[pb2], AF.Copy, [("ps", pb2)], [("Ttpp", gi, nxt)])
                    cur[gi] = nxt
                for gi in range(4):
                    gsl = slice(gi * 512, (gi + 1) * 512)
                    Tn = Tpp[gi][cur[gi]]
                    pb = nb()
                    for i in range(4):
                        bs = slice(i * 128, (i + 1) * 128)
                        MM(P, C.psum[pb][:, bs], Tn[:, bs], XT[:, gi * 512 + i * 128:gi * 512 + (i + 1) * 128], True, True,
                           [("Tpp", gi, cur[gi]), ("XT", gi)], [("ps", pb)])
                    TT_(P, "dve", XT[:, gsl], C.psum[pb], XT[:, gsl], ALU.add, [("ps", pb), ("XT", gi)], [("XT", gi)])

            def hm(arr, h):
                pos = HPOS[h]
                return arr[:, pos * 128:(pos + 1) * 128]

            def htag(name, h):
                return (name, HPOS[h] // 4)

            P.stage = 'rw.S#%d' % ti
            zb = [nb(), nb()]
            for p_ in range(NCH):
                bk = C.psum[zb[p_ // 4]]
                co = (p_ % 4) * 128
                MM(P, bk[:, co:co + 128], AhT[:, p_, qs], Sbd[:, p_, :], True, False, [("AhT",), ("Sbd",)], [("ps", zb[p_ // 4])])
                for hh in range(2):
                    h = 2 * p_ + hh
                    MM(P, bk[:, co + hh * 64:co + (hh + 1) * 64], hm(MakT, h), Vtm[:, h * 64:(h + 1) * 64], False, hh == 1,
                       [htag("MakT", h), ("Vtm",)], [("ps", zb[p_ // 4])])
            ACT(P, Zb[:, 0:512], C.psum[zb[0]], AF.Copy, [("ps", zb[0])], [("Zb", 0)])
            P.op("dve", "tensor_copy", dict(out=Zb[:, 512:1024], in_=C.psum[zb[1]]), reads=[("ps", zb[1])], writes=[("Zb", 1)])
            ub = [nb(), nb()]
            for h in range(RH):
                bk = C.psum[ub[h // 8]]
                co = (h % 8) * 64
                MM(P, bk[:, co:co + 64], hm(XT, h), Zb[:, h * 64:(h + 1) * 64], True, True,
                   [htag("XT", h), ("Zb", h // 8)], [("ps", ub[h // 8])])
            ACT(P, Ub[:, 0:512], C.psum[ub[0]], AF.Copy, [("ps", ub[0])], [("Ub", 0)])
            P.op("dve", "tensor_copy", dict(out=Ub[:, 512:1024], in_=C.psum[ub[1]]), reads=[("ps", ub[1])], writes=[("Ub", 1)])
            yb = [nb(), nb()]
            for p_ in range(NCH):
                bk = C.psum[yb[p_ // 4]]
                co = (p_ % 4) * 128
                MM(P, bk[:, co:co + 128], RT[:, p_, qs], Sbd[:, p_, :], True, False, [("RT",), ("Sbd",)], [("ps", yb[p_ // 4])])
                for hh in range(2):
                    h = 2 * p_ + hh
                    MM(P, bk[:, co + hh * 64:co + (hh + 1) * 64], hm(NrbT, h), Ub[:, h * 64:(h + 1) * 64], False, False,
                       [htag("NrbT", h), ("Ub", h // 8)], [("ps", yb[p_ // 4])])
                    MM(P, bk[:, co + hh * 64:co + (hh + 1) * 64], hm(NrkT, h), Vtm[:, h * 64:(h + 1) * 64], False, hh == 1,
                       [htag("NrkT", h), ("Vtm",)], [("ps", yb[p_ // 4])])
            ACT(P, Ytm[:, q, 0:512], C.psum[yb[0]], AF.Copy, [("ps", yb[0])], [("Ytm", q)])
            P.op("dve", "tensor_copy", dict(out=Ytm[:, q, 512:1024], in_=C.psum[yb[1]]), reads=[("ps", yb[1])], writes=[("Ytm", q)])
            sb_ = [nb(), nb()]
            for p_ in range(NCH):
                bk = C.psum[sb_[p_ // 4]]
                co = (p_ % 4) * 128
                ps_ = slice(p_ * 128, (p_ + 1) * 128)
                MM(P, bk[:, co:co + 128], Bptm[:, ps_], Ub[:, ps_], True, False, [("Bptm",), ("Ub", p_ // 4)], [("ps", sb_[p_ // 4])])
                MM(P, bk[:, co:co + 128], Kptm[:, ps_], Vtm[:, ps_], False, True, [("Kptm",), ("Vtm",)], [("ps", sb_[p_ // 4])])
            for hb in range(2):
                TT_(P, "dve", tmpm[:, hb * 512:(hb + 1) * 512], C.psum[sb_[hb]], BD4, ALU.mult, [("ps", sb_[hb])], [("tmpm",)])
            for p_ in range(NCH):
                STT(P, Sbd32[:, p_, :], Sbd32[:, p_, :], WLt[:, p_, q:q + 1], tmpm[:, p_ * 128:(p_ + 1) * 128], ALU.mult, ALU.add,
                    [("Sbd32",), ("WLt",), ("tmpm",)], [("Sbd32",)])
            P.op("dve", "tensor_copy", dict(out=Sbd, in_=Sbd32), reads=[("Sbd32",)], writes=[("Sbd",)])

        P.stage = 'rw.G#%d' % ti
        trb = [nb(), nb()]
        y4 = Ytm.rearrange("p q (h n) -> p (q h) n", h=RH)
        P.op("dve", "tensor_reduce", dict(out=gst[:, 0, :], in_=y4, axis=AX.X, op=ALU.add), reads=[("Ytm", 0), ("Ytm", 1)], writes=[("gst",)])
        ACT(P, ysq, Ytm, AF.Square, [("Ytm", 0), ("Ytm", 1)], [("ysq",)])
        P.op("dve", "tensor_reduce", dict(out=gst[:, 1, :], in_=ysq.rearrange("p q (h n) -> p (q h) n", h=RH), axis=AX.X, op=ALU.add),
             reads=[("ysq",)], writes=[("gst",)])
        TS(P, "dve", gst[:, 2, :], gst[:, 0, :], 1.0 / 64, None, ALU.mult, None, [("gst",)], [("gst",)])
        TT_(P, "dve", gst[:, 3, :], gst[:, 2, :], gst[:, 2, :], ALU.mult, [("gst",)], [("gst",)])
        STT(P, gst[:, 3, :], gst[:, 1, :], 1.0 / 64, gst[:, 3, :], ALU.mult, ALU.subtract, [("gst",)], [("gst",)])
        ACT(P, gst[:, 3, :], gst[:, 3, :], AF.Sqrt, [("gst",)], [("gst",)], bias=C.eps_ap(64e-5), scale=1.0)
        P.op("dve", "reciprocal", dict(out=gst[:, 3, :], in_=gst[:, 3, :]), reads=[("gst",)], writes=[("gst",)])
        STT(P, gst[:, 4, :], gst[:, 2, :], -1.0, gst[:, 3, :], ALU.mult, ALU.mult, [("gst",)], [("gst",)])
        for q in range(2):
            for h in range(RH):
                j = q * RH + h
                if h % 2 == 0:
                    ACT(P, ynb[:, q, h * 64:(h + 1) * 64], Ytm[:, q, h * 64:(h + 1) * 64], AF.Identity, [("Ytm", q), ("gst",)], [("ynb", q, h)],
                        bias=gst[:, 4, j:j + 1], scale=gst[:, 3, j:j + 1])
                else:
                    TS(P, "dve", ynb[:, q, h * 64:(h + 1) * 64], Ytm[:, q, h * 64:(h + 1) * 64], gst[:, 2, j:j + 1], gst[:, 3, j:j + 1],
                       ALU.subtract, ALU.mult, [("Ytm", q), ("gst",)], [("ynb", q, h)])
            for c in range(NCH):
                psb = C.psum[trb[c // 4]].bitcast(BF16)
                o0 = (c % 4) * TR + q * 128
                P.op("pe", "transpose", dict(out=psb[:, o0:o0 + 128], in_=ynb[:, q, c * 128:(c + 1) * 128], identity=C.rw_id),
                     reads=[("ynb", q, 2 * c), ("ynb", q, 2 * c + 1)], writes=[("ps", trb[c // 4])])
        for c in range(NCH):
            psb = C.psum[trb[c // 4]].bitcast(BF16)
            o0 = (c % 4) * TR
            TS(P, "dve", ytmp[:, c, :], psb[:, o0:o0 + TR], vec[:, c, 11:12], vec[:, c, 12:13], ALU.mult, ALU.add, [("ps", trb[c // 4])], [("ytmp", c // 4)])
        for hf in range(2):
            cs = slice(hf * 4, (hf + 1) * 4)
            TT_(P, "dve", ytmp[:, cs, :], ytmp[:, cs, :], bonus[:, cs, :], ALU.add, [("ytmp", hf), ("bonus",)], [("ytmp", hf)])
            TT_(P, "dve", yg[:, cs, :], ytmp[:, cs, :], gT[:, cs, :], ALU.mult, [("ytmp", hf), ("gT",)], [("yg", hf)])
        P.stage = 'rw.O#%d' % ti
        wb = load_w(3)
        DMA(P, "sp", xt, src[:, :, tsl], [rtag], [("xt",)], ("xt",))
        for dc in range(NCH):
            pb = nb()
            for c in range(NCH):
                MM(P, C.psum[pb][:, 0:TR], wbuf[wb][:, c, dc * 128:(dc + 1) * 128], yg[:, c, :], c == 0, c == NCH - 1,
                   [("wbuf", wb), ("yg", c // 4)], [("ps", pb)])
            TT_(P, "dve", xt[:, dc, :], C.psum[pb][:, 0:TR], xt[:, dc, :], ALU.add, [("ps", pb), ("xt",)], [("xt",)])
        DMA(P, "sp", C.xres[s][:, :, tsl], xt, [("xt",)], [("xres", s, ti // 2)], ("xts",))
        P.barrier()
        A.pop()
    C.xsrc[s] = (C.xres[s], "xres")
    A.pop()

CVALS = [1e-6, 1e-5, 64e-5, 0.0, 1.0, -1.0, 0.5, 1e-24]
C_NG = 7


def build(phases, annotate=False):
    nc = bass.Bass("TRN2", target_bir_lowering=False)
    C = Ctx()
    C.nc = nc
    dt = lambda name, shape, dtype=F32, kind="ExternalInput": nc.dram_tensor(name, list(shape), dtype, kind=kind).ap()
    xin = dt("xin", [SEQ_PER_CORE, 128, NCH, S])
    C.out = dt("out", [SEQ_PER_CORE, 128, NCH, S], kind="ExternalOutput")
    C.xres = dt("xres", [SEQ_PER_CORE, 128, NCH, S], kind="Internal")
    C.xsrc = {s: (xin[s], "xin") for s in range(SEQ_PER_CORE)}
    C.pos = dt("pos", [SEQ_PER_CORE, 1, S], I32)
    C.w_gate = dt("w_gate", [4, 128, NCH, DFF])
    C.w_up = dt("w_up", [4, 128, NCH, DFF])
    C.w_down = dt("w_down", [4, 128, DFF // 128, D])
    gains_d = dt("gains", [128, C_NG * NCH])
    ones_d = dt("ones_f32", [128, 128])
    cvals_d = dt("cvals", [128, len(CVALS)])
    tri_d = dt("tri", [128, 128])
    ropec_d = dt("ropec", [128, 4])
    C.m0_win_conv = dt("m0_win_conv", [128, 2 * NCH, NCH, 128])
    C.m0_win_lat = dt("m0_win_lat", [128, NCH, 896])
    m0_cvec_d = dt("m0_cvec", [128, NCH, 34])
    m0_vec_d = dt("m0_vec", [128, 6])
    C.m0_wuq = dt("m0_wuq", [128, MLA_H, 4, 256])
    C.m0_wukv = dt("m0_wukv", [128, MLA_H, 2, 256])
    C.m0_wout = dt("m0_wout", [128, 2 * NCH, D])
    rwkv_decl(C, dt)

    import contextlib
    with contextlib.ExitStack() as st:
        ARENA = 194 * 1024
        CONST = 12 * 1024
        arena_t = st.enter_context(nc.sbuf_tensor("arena", [128, ARENA], mybir.dt.uint8))
        cst_t = st.enter_context(nc.sbuf_tensor("consts", [128, CONST], mybir.dt.uint8))
        C.arena = Arena(arena_t[:], ARENA)
        CA = Arena(cst_t[:], CONST)
        C.psum = [st.enter_context(nc.psum_tensor("ps%d" % i, [128, 512], F32))[:] for i in range(8)]
        C.P = Prog(nc)
        P = C.P
        P.annotate = annotate
        C.gains = CA.alloc([C_NG * NCH], F32)
        C.ones_f32 = CA.alloc([128], F32)
        C.ones_bf = CA.alloc([128], BF16)
        C.tri_bf = CA.alloc([128], BF16)
        C.cvals = CA.alloc([len(CVALS)], F32)
        C.ropec = CA.alloc([4], F32)
        C.m0_cvec = CA.alloc([NCH, 34], F32)
        C.m0_vec = CA.alloc([6], F32)
        C.eps_ap = lambda v: C.cvals[:, CVALS.index(v):CVALS.index(v) + 1]
        C.G_FINAL = 6
        DMA(P, "sp", C.gains, gains_d, [], [("c", 0)], "c0")
        DMA(P, "sp", C.ones_f32, ones_d, [], [("c", 1)], "c1")
        DMA(P, "pool", C.ones_bf, ones_d, [], [("c", 2)], "c2")
        DMA(P, "pool", C.tri_bf, tri_d, [], [("c", 3)], "c3")
        DMA(P, "sp", C.cvals, cvals_d, [], [("c", 4)], "c4")
        DMA(P, "sp", C.ropec, ropec_d, [], [("c", 5)], "c5")
        DMA(P, "sp", C.m0_cvec, m0_cvec_d, [], [("c", 6)], "c6")
        DMA(P, "sp", C.m0_vec, m0_vec_d, [], [("c", 7)], "c7")
        rwkv_consts(C, CA)
        P.barrier()

        for pi, ph in enumerate(phases):
            kind = ph[0]
            P.prefix = '%02d%s:' % (pi, kind)
            if kind == "ffn":
                phase_ffn(C, ph[1], ph[2], *ph[3:])
            elif kind == "final":
                phase_final(C, ph[1])
            elif kind == "dump":
                phase_dump(C, ph[1])
            elif kind == "mix0":
                phase_mix0(C, ph[1])
            elif kind == "rwkv":
                phase_rwkv(C, ph[1])
            else:
                raise ValueError(kind)
        tags = []
        for s in range(SEQ_PER_CORE):
            for tt in range(NTT):
                tags.append(("out", s, tt))
                tags.append(("xres", s, tt))
        P.op("sp", None, None, reads=tags)
        P.barrier()
        P.finalize_and_emit()
    return nc


def _fm(w, nk):
    w = np.asarray(w, np.float32)
    return np.ascontiguousarray(w.reshape(nk, 128, -1).transpose(1, 0, 2))


def _vec(v, nk):
    return np.ascontiguousarray(np.asarray(v, np.float32).reshape(nk, 128).T)


def prep_shared(inp):
    f32 = np.float32
    sh = {}
    wg = np.asarray(inp["ffn_w_gate"], f32).reshape(4, NCH, 128, DFF).transpose(0, 2, 1, 3)
    wu = np.asarray(inp["ffn_w_up"], f32).reshape(4, NCH, 128, DFF).transpose(0, 2, 1, 3)
    wd = np.asarray(inp["ffn_w_down"], f32).reshape(4, DFF // 128, 128, D).transpose(0, 2, 1, 3)
    sh["w_gate"] = np.ascontiguousarray(wg)
    sh["w_up"] = np.ascontiguousarray(wu)
    sh["w_down"] = np.ascontiguousarray(wd)
    gl = [np.asarray(inp["ffn_norm"], f32).reshape(4, D)[i] for i in range(4)]
    gl.append(np.asarray(inp["mix_norm_even"], f32).reshape(D))
    gl.append(np.asarray(inp["mix_norm_odd"], f32).reshape(D))
    gl.append(np.asarray(inp["final_norm"], f32).reshape(D))
    gains = np.stack([g.reshape(NCH, 128).T for g in gl], axis=1)
    sh["gains"] = np.ascontiguousarray(gains.reshape(128, C_NG * NCH))
    sh["ones_f32"] = np.ones((128, 128), f32)
    sh["cvals"] = np.tile(np.asarray(CVALS, f32)[None, :], (128, 1))
    sh["tri"] = np.triu(np.ones((128, 128), f32))
    invf = (1.0 / (np.float32(10000.0) ** (np.arange(0, 64, 2, dtype=f32) / f32(64)))).astype(f32)
    rc = np.zeros((128, 4), f32)
    rc[0:64, 0] = np.concatenate([invf, invf])
    rc[0:64, 1] = np.pi / 2
    rc[0:32, 2] = np.pi
    sh["ropec"] = rc
    w_in = np.asarray(inp["w_in"], f32)[0]
    sh["m0_win_conv"] = np.ascontiguousarray(w_in[:, 0:2 * D].reshape(NCH, 128, 2 * NCH, 128).transpose(1, 2, 0, 3))
    lat = np.concatenate([w_in[:, 2 * D:2 * D + 832], w_in[:, 2 * D + 800:2 * D + 832], w_in[:, 2 * D + 768:2 * D + 800]], axis=1)
    sh["m0_win_lat"] = _fm(lat, NCH)
    cvec = np.zeros((128, NCH, 34), f32)
    cvec[:, :, 0:31] = np.asarray(inp["conv_w"], f32)[0].reshape(31, NCH, 128).transpose(2, 1, 0)
    cvec[:, :, 31] = _vec(inp["conv_b"][0], NCH)
    cvec[:, :, 32] = _vec(inp["conv_ln_g"][0], NCH)
    cvec[:, :, 33] = _vec(inp["conv_ln_b"][0], NCH)
    sh["m0_cvec"] = cvec
    sh["m0_vec"] = np.concatenate([_vec(inp["q_norm"][0], 4), _vec(inp["kv_norm"][0], 2)], axis=1)
    wuq = np.asarray(inp["w_uq"], f32)[0].reshape(512, MLA_H, 192)
    wuq = np.concatenate([wuq, wuq[:, :, 160:192], wuq[:, :, 128:160]], axis=2)
    sh["m0_wuq"] = np.ascontiguousarray(wuq.reshape(4, 128, MLA_H, 256).transpose(1, 2, 0, 3))
    wukv = np.asarray(inp["w_ukv"], f32)[0].reshape(256, MLA_H, 256)
    sh["m0_wukv"] = np.ascontiguousarray(wukv.reshape(2, 128, MLA_H, 256).transpose(1, 2, 0, 3))
    sh["m0_wout"] = _fm(np.asarray(inp["w_out"], f32)[0], 2 * NCH)
    rwkv_prep(inp, sh)
    return sh


def default_phases():
    ph = []
    for s in range(SEQ_PER_CORE):
        ph += [("ffn", s, 0), ("mix0", s), ("ffn", s, 1, True, False), ("ffn", s, 2, False, True), ("rwkv", s), ("ffn", s, 3, True, False, True)]
    return ph


def core_inputs(inp, sh, c):
    x = np.asarray(inp["x"], np.float32)
    xc = x[c * SEQ_PER_CORE:(c + 1) * SEQ_PER_CORE]
    xT = xc.reshape(SEQ_PER_CORE, S, NCH, 128).transpose(0, 3, 2, 1)
    m = dict(sh)
    m["xin"] = np.ascontiguousarray(xT)
    m["pos"] = np.ascontiguousarray(np.asarray(inp["positions"], np.int32)[c * SEQ_PER_CORE:(c + 1) * SEQ_PER_CORE].reshape(SEQ_PER_CORE, 1, S))
    return m


def kernel(**inp):
    sh = prep_shared(inp)
    nc = build(default_phases())
    in_maps = [core_inputs(inp, sh, c) for c in range(NCORES)]
    res = run_bass_kernel_spmd(nc, in_maps, core_ids=list(range(NCORES)))
    outs = []
    for c in range(NCORES):
        o = np.asarray(res.results[c]["out"], np.float32)
        outs.append(o.transpose(0, 3, 2, 1).reshape(SEQ_PER_CORE, S, D))
    return np.ascontiguousarray(np.concatenate(outs, axis=0))
```

```python
import numpy as np
import concourse.bass as bass
import concourse.mybir as mybir
from concourse.bass_utils import run_bass_kernel_spmd

F32 = mybir.dt.float32
F32R = mybir.dt.float32r
BF16 = mybir.dt.bfloat16
I32 = mybir.dt.int32
AF = mybir.ActivationFunctionType
ALU = mybir.AluOpType
AX = mybir.AxisListType

D = 1024
S = 2048
DFF = 2816
NCH = 8
TT = 512
NTT = S // TT
EPS = 1e-6
NCORES = 8
SEQ_PER_CORE = 2

ENGS = ("pe", "act", "dve", "pool", "sp")


class _Op:
    __slots__ = ("eng", "fn", "waits", "key", "idx", "needed", "value", "is_dma", "stage", "stream")

    def __init__(self, eng, fn, key, idx, is_dma):
        self.eng = eng
        self.fn = fn
        self.waits = []
        self.key = key
        self.idx = idx
        self.needed = is_dma
        self.value = None
        self.is_dma = is_dma
        self.stage = None
        self.stream = None


class Prog:
    def __init__(self, nc):
        self.nc = nc
        self.ops = {e: [] for e in ENGS}
        self.bykey = {}
        self.last_w = {}
        self.readers = {}
        self.seen = {e: {} for e in ENGS}
        self.stage = None
        self.stream = None
        self.last_in_stream = {}
        self.prefix = ''
        self.annotate = False

    def _need(self, eng, prod, waits):
        if prod is None:
            return
        if prod.key == eng and eng == "pe" and not prod.is_dma:
            return
        sk = self.seen[eng]
        if sk.get(prod.key, -1) >= prod.idx:
            return
        sk[prod.key] = prod.idx
        prod.needed = True
        waits.append(prod)

    def op(self, eng, meth, kw=None, reads=(), writes=(), dma_key=None):
        fn = None if meth is None else (meth, kw)
        is_dma = dma_key is not None
        key = ("dma", dma_key) if is_dma else eng
        lst = self.bykey.setdefault(key, [])
        o = _Op(eng, fn, key, len(lst), is_dma)
        o.stage = (self.prefix + self.stage) if self.stage else None
        o.stream = self.stream
        if self.stream is not None and fn is not None:
            self.last_in_stream[(self.stream, key)] = o
        lst.append(o)
        waits = []
        for t in reads:
            self._need(eng, self.last_w.get(t), waits)
            if t[0] == "ps":
                for r in self.readers.get(t, ()):
                    if r.eng != eng:
                        self._need(eng, r, waits)
        for t in writes:
            self._need(eng, self.last_w.get(t), waits)
            for r in self.readers.get(t, ()):
                if r.key == eng and not r.is_dma:
                    continue
                self._need(eng, r, waits)
        best = {}
        for w in waits:
            if w.key not in best or best[w.key].idx < w.idx:
                best[w.key] = w
        o.waits = list(best.values())
        for t in reads:
            self.readers.setdefault(t, []).append(o)
        for t in writes:
            self.last_w[t] = o
            self.readers[t] = []
        self.ops[eng].append(o)
        return o

    def barrier(self):
        lasts = [lst[-1] for lst in self.bykey.values() if lst]
        for eng in ENGS:
            o = _Op(eng, None, eng, len(self.bykey.setdefault(eng, [])), False)
            self.bykey[eng].append(o)
            waits = []
            for p in lasts:
                if p.fn is None and not p.is_dma:
                    continue
                if p.key == eng and not p.is_dma and eng == "pe":
                    continue
                self._need(eng, p, waits)
            o.waits = waits
            self.ops[eng].append(o)

    def stream_barrier(self, stream):
        lasts = [o for (st, k), o in self.last_in_stream.items() if st == stream]
        for eng in ENGS:
            o = _Op(eng, None, eng, len(self.bykey.setdefault(eng, [])), False)
            self.bykey[eng].append(o)
            waits = []
            for p in lasts:
                if p.key == eng and not p.is_dma and eng == "pe":
                    continue
                self._need(eng, p, waits)
            o.waits = waits
            self.ops[eng].append(o)

    def finalize_and_emit(self):
        nc = self.nc
        keys = list(self.bykey.keys())
        for k in keys:
            cnt = 0
            for o in self.bykey[k]:
                if o.needed:
                    cnt += 16 if o.is_dma else 1
                    o.value = cnt
            assert cnt < 60000, (k, cnt)
        import contextlib
        with contextlib.ExitStack() as st:
            sems = {}
            for i, k in enumerate(keys):
                sems[k] = st.enter_context(nc.semaphore("s%d" % i))
            block = st.enter_context(nc.Block())

            def run(engname):
                def body(e):
                    for o in self.ops[engname]:
                        for w in o.waits:
                            e.wait_ge(sems[w.key], w.value)
                        if o.fn is None:
                            continue
                        ins = getattr(e, o.fn[0])(**o.fn[1])
                        if self.annotate and o.stage:
                            ins.annotate(o.stage)
                        if o.needed:
                            ins.then_inc(sems[o.key], 16 if o.is_dma else 1)
                return body

            block.sync(run("sp"))
            block.scalar(run("act"))
            block.vector(run("dve"))
            block.gpsimd(run("pool"))
            block.tensor(run("pe"))


class Arena:
    def __init__(self, base_ap_u8, size):
        self.base = base_ap_u8
        self.size = size
        self.off = 0
        self.stack = []

    def push(self):
        self.stack.append(self.off)

    def pop(self):
        self.off = self.stack.pop()

    def alloc(self, shape_free, dtype, parts=128):
        esz = mybir.dt.size(dtype)
        n = 1
        for d in shape_free:
            n *= d
        nbytes = n * esz
        self.off = (self.off + 63) // 64 * 64
        assert self.off + nbytes <= self.size, ("SBUF arena overflow", self.off, nbytes, self.size)
        v = self.base[0:parts, self.off:self.off + nbytes].bitcast(dtype)
        self.off += nbytes
        if len(shape_free) == 2:
            v = v.rearrange("p (a b) -> p a b", a=shape_free[0])
        elif len(shape_free) == 3:
            v = v.rearrange("p (a b c) -> p a b c", a=shape_free[0], b=shape_free[1])
        return v


class Ctx:
    pass


def _tags(name, *idx):
    return (name,) + tuple(idx)


def MM(P, out, lhsT, rhs, start, stop, reads, writes):
    P.op("pe", "matmul", dict(out=out, lhsT=lhsT, rhs=rhs, start=start, stop=stop), reads=reads, writes=writes)


def ACT(P, out, in_, func, reads, writes, bias=None, scale=None):
    kw = dict(out=out, in_=in_, func=func)
    if bias is not None:
        kw["bias"] = bias
    if scale is not None:
        kw["scale"] = scale
    P.op("act", "activation", kw, reads=reads, writes=writes)


def TT_(P, eng, out, in0, in1, op, reads, writes):
    P.op(eng, "tensor_tensor", dict(out=out, in0=in0, in1=in1, op=op), reads=reads, writes=writes)


def TS(P, eng, out, in0, s1, s2, op0, op1, reads, writes):
    kw = dict(out=out, in0=in0, scalar1=s1, scalar2=s2, op0=op0)
    if op1 is not None:
        kw["op1"] = op1
    P.op(eng, "tensor_scalar", kw, reads=reads, writes=writes)


def STT(P, out, in0, scalar, in1, op0, op1, reads, writes):
    P.op("dve", "scalar_tensor_tensor", dict(out=out, in0=in0, scalar=scalar, in1=in1, op0=op0, op1=op1),
         reads=reads, writes=writes)


def DMA(P, q, out, in_, reads, writes, key):
    P.op(q, "dma_start", dict(out=out, in_=in_), reads=reads, writes=writes, dma_key=key)


def load_x(C, xs, s):
    src, stag = C.xsrc[s]
    for tt in range(NTT):
        sl = slice(tt * TT, (tt + 1) * TT)
        DMA(C.P, "sp", xs[:, :, sl], src[:, :, sl], [(stag, s, tt)], [("x", tt)], ("xl", tt))


def store_x(C, xs, s, dst=None, dtag="xres"):
    if dst is None:
        dst = C.xres[s]
        C.xsrc[s] = (C.xres[s], "xres")
    for tt in range(NTT):
        sl = slice(tt * TT, (tt + 1) * TT)
        DMA(C.P, "sp", dst[:, :, sl], xs[:, :, sl], [("x", tt)], [(dtag, s, tt)], ("xs", tt))


def rms_to(C, src_fn, nchunk, gain_ap, out_fn, src_tags, out_tags, dim, eps=EPS, parts=128, ones=None):
    P = C.P
    n = src_fn(0).shape[-1]
    ps = C.psum[7][0:parts, 0:n]
    ones = C.ones_f32 if ones is None else ones
    rstd = C.rstd[0:parts, 0:n]
    for c in range(nchunk):
        nsq = len(C.sq)
        sq = C.sq[c % nsq][0:parts, 0:n]
        ACT(P, sq, src_fn(c), AF.Square, [src_tags(c)], [("sq", c % nsq)])
        MM(P, ps, ones[0:parts, 0:parts], sq, c == 0, c == nchunk - 1, [("sq", c % nsq), ("const",)], [("ps", 7)])
    ACT(P, rstd, ps, AF.Ln, [("ps", 7), ("const",)], [("rstd",)], bias=C.eps_ap(eps)[0:parts], scale=1.0 / dim)
    ACT(P, rstd, rstd, AF.Exp, [("rstd",)], [("rstd",)], scale=-0.5)
    for c in range(nchunk):
        STT(P, out_fn(c), src_fn(c), gain_ap[0:parts, c:c + 1], rstd, ALU.mult, ALU.mult,
            [src_tags(c), ("rstd",), ("gains",)], [out_tags(c)])


def phase_ffn(C, s, fi, load=True, store=True, final=False):
    P = C.P
    A = C.arena
    if load:
        P.barrier()
    A.push()
    xs = A.alloc([NCH, S], F32)
    hT = A.alloc([NCH, S], BF16)
    wg = [A.alloc([NCH, 512], BF16) for _ in range(2)]
    wu = [A.alloc([NCH, 512], BF16) for _ in range(2)]
    wd = [A.alloc([4, D], BF16) for _ in range(2)]
    act = [A.alloc([4, TT], BF16) for _ in range(2)]
    sg = [A.alloc([TT], F32) for _ in range(2)]
    C.sq = [A.alloc([TT], F32) for _ in range(8)]
    C.rstd = A.alloc([TT], F32)

    P.stage = "ffn.norm"
    if load:
        load_x(C, xs, s)
    g = C.gains[:, fi * NCH:(fi + 1) * NCH]

    def norm(tt):
        if tt >= NTT:
            return
        sl = slice(tt * TT, (tt + 1) * TT)
        P.stage = "ffn.norm"
        rms_to(C, lambda c: xs[:, c, sl], NCH, g, lambda c: hT[:, c, sl],
               lambda c: ("x", tt), lambda c: ("hT", tt), D)
        P.stage = "ffn.main"

    groups = [(0, 4), (4, 8), (8, 12), (12, 16), (16, 20), (20, 22)]

    def wload(gi):
        if gi >= len(groups):
            return
        c0, c1 = groups[gi]
        b = gi % 2
        nf = c1 - c0
        f0, f1 = c0 * 128, c1 * 128
        DMA(P, "pool", wg[b][:, :, 0:nf * 128], C.w_gate[fi][:, :, f0:f1], [], [("wg", b)], ("wg", b))
        DMA(P, "pool", wu[b][:, :, 0:nf * 128], C.w_up[fi][:, :, f0:f1], [], [("wu", b)], ("wu", b))
        DMA(P, "pool", wd[b][:, 0:nf, :], C.w_down[fi][:, c0:c1, :], [], [("wd", b)], ("wd", b))

    wload(0)
    norm(0)
    P.stage = "ffn.main"
    it = 0
    for gi, (c0, c1) in enumerate(groups):
        b = gi % 2
        nf = c1 - c0
        wload(gi + 1)
        for tt in range(NTT):
            sl = slice(tt * TT, (tt + 1) * TT)
            ab = it % 2
            it += 1
            if gi == 0:
                norm(tt + 1)
            for fc in range(nf):
                pg = fc % 2
                psG = C.psum[pg * 2]
                psU = C.psum[pg * 2 + 1]
                for k in range(NCH):
                    MM(P, psG, wg[b][:, k, fc * 128:(fc + 1) * 128], hT[:, k, sl], k == 0, k == NCH - 1,
                       [("wg", b), ("hT", tt)], [("ps", pg * 2)])
                for k in range(NCH):
                    MM(P, psU, wu[b][:, k, fc * 128:(fc + 1) * 128], hT[:, k, sl], k == 0, k == NCH - 1,
                       [("wu", b), ("hT", tt)], [("ps", pg * 2 + 1)])
                ACT(P, sg[pg], psG, AF.Silu, [("ps", pg * 2)], [("sg", pg)])
                TT_(P, "dve", act[ab][:, fc, :], psU, sg[pg], ALU.mult, [("ps", pg * 2 + 1), ("sg", pg)], [("act", ab, fc)])
            for dc in range(NCH):
                pb = 4 + dc % 3
                psY = C.psum[pb]
                for fc in range(nf):
                    MM(P, psY, wd[b][:, fc, dc * 128:(dc + 1) * 128], act[ab][:, fc, :], fc == 0, fc == nf - 1,
                       [("wd", b), ("act", ab, fc)], [("ps", pb)])
                STT(P, xs[:, dc, sl], psY, 0.5, xs[:, dc, sl], ALU.mult, ALU.add, [("ps", pb), ("x", tt)], [("x", tt)])
    if final:
        P.stage = "final"
        gf = C.gains[:, C.G_FINAL * NCH:(C.G_FINAL + 1) * NCH]
        for tt in range(NTT):
            sl = slice(tt * TT, (tt + 1) * TT)
            rms_to(C, lambda c: xs[:, c, sl], NCH, gf, lambda c: xs[:, c, sl],
                   lambda c: ("x", tt), lambda c: ("x", tt), D)
        store_x(C, xs, s, dst=C.out[s], dtag="out")
    elif store:
        store_x(C, xs, s)
    A.pop()


def phase_final(C, s):
    P = C.P
    A = C.arena
    P.barrier()
    A.push()
    xs = A.alloc([NCH, S], F32)
    C.sq = [A.alloc([TT], F32) for _ in range(2)]
    C.rstd = A.alloc([TT], F32)
    P.stage = "final"
    load_x(C, xs, s)
    g = C.gains[:, C.G_FINAL * NCH:(C.G_FINAL + 1) * NCH]
    for tt in range(NTT):
        sl = slice(tt * TT, (tt + 1) * TT)
        rms_to(C, lambda c: xs[:, c, sl], NCH, g, lambda c: xs[:, c, sl],
               lambda c: ("x", tt), lambda c: ("x", tt), D)
    store_x(C, xs, s, dst=C.out[s], dtag="out")
    A.pop()


def phase_dump(C, s):
    A = C.arena
    C.P.barrier()
    A.push()
    xs = A.alloc([NCH, S], F32)
    load_x(C, xs, s)
    store_x(C, xs, s, dst=C.out[s], dtag="out")
    A.pop()

CW = 31
HALO = CW - 1
MLA_H = 8
ATT_SCALE = (128 + 64) ** -0.5
PI = float(np.pi)


def rope_tables(C, s, CCt, SSt, tmpA, tmpB, tmpI):
    P = C.P
    p64 = slice(0, 64)
    DMA(P, "sp", tmpI[p64, :], C.pos[s].partition_broadcast(64), [], [("ropeI",)], ("rope",))
    P.op("dve", "tensor_copy", dict(out=tmpA[p64, :], in_=tmpI[p64, :]), reads=[("ropeI",)], writes=[("ropeA",)])
    ang = tmpA
    TS(P, "dve", ang[p64, :], tmpA[p64, :], C.ropec[p64, 0:1], None, ALU.mult, None, [("ropeA",), ("const",)], [("ropeA",)])
    for tbl, col, tag in ((CCt, 1, "CC"), (SSt, 2, "SS")):
        t = tbl
        TS(P, "dve", t[p64, :], ang[p64, :], C.ropec[p64, col:col + 1], None, ALU.add, None, [("ropeA",), ("const",)], [(tag,)])
        TS(P, "dve", tmpB[p64, :], t[p64, :], 1.0 / (2 * PI), None, ALU.mult, None, [(tag,)], [("ropeB",)])
        P.op("dve", "tensor_copy", dict(out=tmpI[p64, :], in_=tmpB[p64, :]), reads=[("ropeB",)], writes=[("ropeI",)])
        P.op("dve", "tensor_copy", dict(out=tmpB[p64, :], in_=tmpI[p64, :]), reads=[("ropeI",)], writes=[("ropeB",)])
        STT(P, t[p64, :], tmpB[p64, :], -6.28125, t[p64, :], ALU.mult, ALU.add, [("ropeB",), (tag,)], [(tag,)])
        STT(P, t[p64, :], tmpB[p64, :], -(2 * PI - 6.28125), t[p64, :], ALU.mult, ALU.add, [("ropeB",), (tag,)], [(tag,)])
        TS(P, "dve", t[p64, :], t[p64, :], -PI, PI, ALU.max, ALU.min, [(tag,)], [(tag,)])
        ACT(P, tbl[p64, :], t[p64, :], AF.Sin, [(tag,)], [(tag,)])


def phase_mix0(C, s):
    P = C.P
    A = C.arena
    P.barrier()
    A.push()
    hT = A.alloc([NCH, S], BF16)
    C.sq = [A.alloc([TT], F32) for _ in range(2)]
    C.rstd = A.alloc([TT], F32)
    p64 = slice(0, 64)
    src, stag = C.xsrc[s]
    g = C.gains[:, 4 * NCH:5 * NCH]

    P.stage = 'm0.A3'
    A.push()
    xt = A.alloc([NCH, TT], F32)
    wa = [A.alloc([NCH, 128], BF16) for _ in range(2)]
    wgt = [A.alloc([NCH, 128], BF16) for _ in range(2)]
    woc = A.alloc([NCH, D], BF16)
    glu = A.alloc([NCH, HALO + TT], BF16)
    dg = [A.alloc([CW, 128], BF16) for _ in range(2)]
    hc = [A.alloc([NCH, TT], F32) for _ in range(2)]
    co = A.alloc([NCH, TT], BF16)
    sig = [A.alloc([TT], F32) for _ in range(2)]
    xt1 = A.alloc([NCH, TT], F32)
    mean = A.alloc([TT], F32)
    nmr = A.alloc([TT], F32)
    var = A.alloc([TT], F32)
    DMA(P, "pool", woc, C.m0_wout[:, 0:NCH, :], [], [("woc",)], ("woc",))
    cv = C.m0_cvec
    NIT = NTT * NCH

    def xload(tt):
        if tt >= NTT:
            return
        sl = slice(tt * TT, (tt + 1) * TT)
        DMA(P, "sp", xt, src[:, :, sl], [(stag, s, tt)], [("xt",)], ("xt",))

    def norm(tt):
        if tt >= NTT:
            return
        sl = slice(tt * TT, (tt + 1) * TT)
        P.stage = 'm0.A1'
        rms_to(C, lambda c: xt[:, c, :], NCH, g, lambda c: hT[:, c, sl],
               lambda c: ("xt",), lambda c: ("hT", tt), D)
        xload(tt + 1)
        P.stage = 'm0.A3'

    def wload(i):
        if i >= NIT:
            return
        c, b = i % NCH, i % 2
        DMA(P, "pool", wa[b], C.m0_win_conv[:, c], [], [("wa", b)], ("wa", b))
        DMA(P, "pool", wgt[b], C.m0_win_conv[:, NCH + c], [], [("wgt", b)], ("wgt", b))

    def ag(i):
        if i >= NIT:
            return
        tt, c, b = i // NCH, i % NCH, i % 2
        sl = slice(tt * TT, (tt + 1) * TT)
        for k in range(NCH):
            MM(P, C.psum[2 * b], wa[b][:, k, :], hT[:, k, sl], k == 0, k == NCH - 1, [("wa", b), ("hT", tt)], [("ps", 2 * b)])
        for k in range(NCH):
            MM(P, C.psum[2 * b + 1], wgt[b][:, k, :], hT[:, k, sl], k == 0, k == NCH - 1, [("wgt", b), ("hT", tt)], [("ps", 2 * b + 1)])

    def diag(i):
        if i >= NIT:
            return
        c, b = i % NCH, i % 2
        for j in range(CW):
            if j % 3 == 2:
                TS(P, "dve", dg[b][:, j, :], C.rw_id, cv[:, c, j:j + 1], None, ALU.mult, None, [("const",)], [("dg", b, j)])
            else:
                ACT(P, dg[b][:, j, :], C.rw_id, AF.Copy, [("const",)], [("dg", b, j)], scale=cv[:, c, j:j + 1])

    def ln_piece(tt, k):
        par = tt % 2
        h_ = hc[par]
        sl = slice(tt * TT, (tt + 1) * TT)
        if k == 0:
            DMA(P, "sp", xt1, src[:, :, sl], [(stag, s, tt)], [("xt1",)], ("xt1",))
            for c2 in range(NCH):
                MM(P, C.psum[7], C.ones_f32, h_[:, c2, :], c2 == 0, c2 == NCH - 1, [("hc", par, c2), ("const",)], [("ps", 7)])
            for c2 in range(NCH):
                sq = C.sq[c2 % 2]
                ACT(P, sq, h_[:, c2, :], AF.Square, [("hc", par, c2)], [("sq", c2 % 2)])
                MM(P, C.psum[6], C.ones_f32, sq, c2 == 0, c2 == NCH - 1, [("sq", c2 % 2), ("const",)], [("ps", 6)])
        elif k == 1:
            TS(P, "dve", mean, C.psum[7], 1.0 / D, None, ALU.mult, None, [("ps", 7)], [("mean",)])
            TT_(P, "dve", var, mean, mean, ALU.mult, [("mean",)], [("var",)])
            STT(P, var, C.psum[6], 1.0 / D, var, ALU.mult, ALU.subtract, [("ps", 6), ("var",)], [("var",)])
            ACT(P, var, var, AF.Ln, [("var",), ("const",)], [("var",)], bias=C.eps_ap(1e-5), scale=1.0)
            ACT(P, var, var, AF.Exp, [("var",)], [("var",)], scale=-0.5)
            STT(P, nmr, mean, -1.0, var, ALU.mult, ALU.mult, [("mean",), ("var",)], [("nmr",)])
        elif k in (2, 3):
            for c2 in range((k - 2) * 4, (k - 1) * 4):
                TT_(P, "dve", h_[:, c2, :], h_[:, c2, :], var, ALU.mult, [("hc", par, c2), ("var",)], [("hc", par, c2)])
                TT_(P, "dve", h_[:, c2, :], h_[:, c2, :], nmr, ALU.add, [("hc", par, c2), ("nmr",)], [("hc", par, c2)])
                ACT(P, co[:, c2, :], h_[:, c2, :], AF.Silu, [("hc", par, c2), ("const",)], [("co", c2)],
                    bias=cv[:, c2, 33:34], scale=cv[:, c2, 32:33])
        else:
            for dc in range((k - 4) * 4, (k - 3) * 4):
                pb = 6 + dc % 2
                for c2 in range(NCH):
                    MM(P, C.psum[pb], woc[:, c2, dc * 128:(dc + 1) * 128], co[:, c2, :], c2 == 0, c2 == NCH - 1,
                       [("woc",), ("co", c2)], [("ps", pb)])
                TT_(P, "dve", xt1[:, dc, :], C.psum[pb], xt1[:, dc, :], ALU.add, [("ps", pb), ("xt1",)], [("xt1",)])
            if k == 5:
                DMA(P, "sp", C.xres[s][:, :, sl], xt1, [("xt1",)], [("xres", s, tt)], ("xt1s",))

    xload(0)
    norm(0)
    wload(0)
    wload(1)
    ag(0)
    diag(0)
    for i in range(NIT):
        tt, c, b = i // NCH, i % NCH, i % 2
        par = tt % 2
        if c == 2:
            norm(tt + 1)
        ag(i + 1)
        wload(i + 2)
        if tt == 0:
            P.op("dve", "memset", dict(ap=glu[:, c, 0:HALO], constant=0.0), writes=[("glu", c)])
        else:
            P.op("dve", "tensor_copy", dict(out=glu[:, c, 0:HALO], in_=glu[:, c, TT:TT + HALO]),
                 reads=[("glu", c)], writes=[("glu", c)])
        ACT(P, sig[b], C.psum[2 * b + 1], AF.Sigmoid, [("ps", 2 * b + 1)], [("sig", b)])
        TT_(P, "dve", glu[:, c, HALO:HALO + TT], C.psum[2 * b], sig[b], ALU.mult, [("ps", 2 * b), ("sig", b)], [("glu", c)])
        diag(i + 1)
        psC = C.psum[4 + b]
        for j in range(CW):
            MM(P, psC, dg[b][:, j, :], glu[:, c, j:j + TT], j == 0, j == CW - 1, [("dg", b, j), ("glu", c)], [("ps", 4 + b)])
        ACT(P, hc[par][:, c, :], psC, AF.Identity, [("ps", 4 + b)], [("hc", par, c)], bias=cv[:, c, 31:32])
        if tt > 0 and c < 6:
            ln_piece(tt - 1, c)
    for k in range(6):
        ln_piece(NTT - 1, k)
    C.xsrc[s] = (C.xres[s], "xres")
    src, stag = C.xsrc[s]
    P.barrier()
    A.pop()

    P.stage = 'm0.A2'
    qn = A.alloc([4, S], BF16)
    kvn = A.alloc([2, S], BF16)
    kpe = A.alloc([S], BF16)
    CCt = A.alloc([S], F32)
    SSt = A.alloc([S], F32)
    A.push()
    tmpA = A.alloc([S], F32)
    tmpB = A.alloc([S], F32)
    tmpI = A.alloc([S], I32)
    wl = A.alloc([NCH, 896], BF16)
    t1 = A.alloc([TT], F32)
    t2 = A.alloc([TT], F32)
    DMA(P, "pool", wl, C.m0_win_lat, [], [("wl",)], ("wl",))
    rope_tables(C, s, CCt, SSt, tmpA, tmpB, tmpI)
    for tt in range(NTT):
        sl = slice(tt * TT, (tt + 1) * TT)
        for m in range(4):
            for k in range(NCH):
                MM(P, C.psum[m], wl[:, k, m * 128:(m + 1) * 128], hT[:, k, sl], k == 0, k == NCH - 1,
                   [("wl",), ("hT", tt)], [("ps", m)])
        rms_to(C, lambda m: C.psum[m], 4, C.m0_vec[:, 0:4], lambda m: qn[:, m, sl],
               lambda m: ("ps", m), lambda m: ("qn", tt), 512)
        for m in range(2):
            for k in range(NCH):
                MM(P, C.psum[4 + m], wl[:, k, 512 + m * 128:512 + (m + 1) * 128], hT[:, k, sl], k == 0, k == NCH - 1,
                   [("wl",), ("hT", tt)], [("ps", 4 + m)])
        rms_to(C, lambda m: C.psum[4 + m], 2, C.m0_vec[:, 4:6], lambda m: kvn[:, m, sl],
               lambda m: ("ps", 4 + m), lambda m: ("kvn", tt), 256)
        for k in range(NCH):
            MM(P, C.psum[6][p64, :], wl[:, k, 768:832], hT[:, k, sl], k == 0, k == NCH - 1, [("wl",), ("hT", tt)], [("ps", 6)])
        for k in range(NCH):
            MM(P, C.psum[0][p64, :], wl[:, k, 832:896], hT[:, k, sl], k == 0, k == NCH - 1, [("wl",), ("hT", tt)], [("ps", 0)])
        TT_(P, "dve", t1[p64, :], C.psum[6][p64, :], CCt[p64, sl], ALU.mult, [("ps", 6), ("CC",)], [("t1",)])
        TT_(P, "dve", t2[p64, :], C.psum[0][p64, :], SSt[p64, sl], ALU.mult, [("ps", 0), ("SS",)], [("t2",)])
        TT_(P, "dve", kpe[p64, sl], t1[p64, :], t2[p64, :], ALU.add, [("t1",), ("t2",)], [("kpe", tt)])
    P.barrier()
    A.pop()

    P.stage = 'm0.C'
    attnT = hT
    A.push()
    wq = [A.alloc([4, 256], BF16) for _ in range(2)]
    wkv = [A.alloc([2, 256], BF16) for _ in range(2)]
    qno = A.alloc([S], BF16)
    qpe = A.alloc([S], BF16)
    kno = A.alloc([S], BF16)
    vh = A.alloc([S // 128, 128], BF16)
    Eb = [A.alloc([TT], BF16) for _ in range(3)]
    rden = A.alloc([TT], F32)
    t1 = A.alloc([TT], F32)
    t2 = A.alloc([TT], F32)
    for h in range(MLA_H):
        b = h % 2
        P.stage = 'm0.Cprep'
        DMA(P, "pool", wq[b], C.m0_wuq[:, h], [], [("wq", b)], ("wq", b))
        DMA(P, "pool", wkv[b], C.m0_wukv[:, h], [], [("wkv", b)], ("wkv", b))
        for tt in range(NTT):
            sl = slice(tt * TT, (tt + 1) * TT)
            for l in range(4):
                MM(P, C.psum[0], wq[b][:, l, 0:128], qn[:, l, sl], l == 0, l == 3, [("wq", b), ("qn", tt)], [("ps", 0)])
            ACT(P, qno[:, sl], C.psum[0], AF.Copy, [("ps", 0)], [("qno", tt)])
            for l in range(4):
                MM(P, C.psum[1][p64, :], wq[b][:, l, 128:192], qn[:, l, sl], l == 0, l == 3, [("wq", b), ("qn", tt)], [("ps", 1)])
            for l in range(4):
                MM(P, C.psum[2][p64, :], wq[b][:, l, 192:256], qn[:, l, sl], l == 0, l == 3, [("wq", b), ("qn", tt)], [("ps", 2)])
            TT_(P, "dve", t1[p64, :], C.psum[1][p64, :], CCt[p64, sl], ALU.mult, [("ps", 1), ("CC",)], [("t1",)])
            TT_(P, "dve", t2[p64, :], C.psum[2][p64, :], SSt[p64, sl], ALU.mult, [("ps", 2), ("SS",)], [("t2",)])
            TT_(P, "dve", qpe[p64, sl], t1[p64, :], t2[p64, :], ALU.add, [("t1",), ("t2",)], [("qpe", tt)])
            for l in range(2):
                MM(P, C.psum[3], wkv[b][:, l, 0:128], kvn[:, l, sl], l == 0, l == 1, [("wkv", b), ("kvn", tt)], [("ps", 3)])
            ACT(P, kno[:, sl], C.psum[3], AF.Copy, [("ps", 3)], [("kno", tt)])
            for i in range(4):
                tsl = slice(tt * TT + i * 128, tt * TT + (i + 1) * 128)
                for l in range(2):
                    MM(P, C.psum[4][:, i * 128:(i + 1) * 128], kvn[:, l, tsl], wkv[b][:, l, 128:256], l == 0, l == 1,
                       [("wkv", b), ("kvn", tt)], [("ps", 4)])
            P.op("dve", "tensor_copy", dict(out=vh[:, tt * 4:(tt + 1) * 4, :], in_=C.psum[4].rearrange("p (a b) -> p a b", a=4)),
                 reads=[("ps", 4)], writes=[("vh", tt)])
        ecount = 0
        P.stage = 'm0.Cattn'
        for qt in range(NTT):
            nk = 4 * (qt + 1)
            psO = C.psum[3 + qt % 2]
            psD = C.psum[5 + qt % 2]
            otag, dtag = ("ps", 3 + qt % 2), ("ps", 5 + qt % 2)

            def s_stage(kt, e):
                j = kt - 4 * qt
                c0 = max(j, 0) * 128
                qsl = slice(qt * TT + c0, (qt + 1) * TT)
                ksl = slice(kt * 128, (kt + 1) * 128)
                psS = C.psum[e]
                MM(P, psS[:, c0:TT], kno[:, ksl], qno[:, qsl], True, False, [("kno", kt // 4), ("qno", qt)], [("ps", e)])
                MM(P, psS[:, c0:TT], kpe[p64, ksl], qpe[p64, qsl], False, True, [("kpe", kt // 4), ("qpe", qt)], [("ps", e)])
                ACT(P, Eb[e][:, c0:TT], psS[:, c0:TT], AF.Exp, [("ps", e)], [("E", e)], scale=ATT_SCALE)
                if j >= 0:
                    TT_(P, "dve", Eb[e][:, c0:c0 + 128], Eb[e][:, c0:c0 + 128], C.tri_bf, ALU.mult,
                        [("E", e), ("const",)], [("E", e)])

            def o_stage(kt, e):
                j = kt - 4 * qt
                c0 = max(j, 0) * 128
                MM(P, psO[:, c0:TT], vh[:, kt, :], Eb[e][:, c0:TT], kt == 0, kt == nk - 1, [("vh", kt // 4), ("E", e)], [otag])
                MM(P, psD[:, c0:TT], C.ones_bf, Eb[e][:, c0:TT], kt == 0, kt == nk - 1, [("const",), ("E", e)], [dtag])

            es = [(ecount + kt) % 3 for kt in range(nk)]
            ecount += nk
            s_stage(0, es[0])
            for kt in range(nk):
                if kt + 1 < nk:
                    s_stage(kt + 1, es[kt + 1])
                o_stage(kt, es[kt])
            sl = slice(qt * TT, (qt + 1) * TT)
            P.op("dve", "reciprocal", dict(out=rden, in_=psD), reads=[dtag], writes=[("rden",)])
            TT_(P, "dve", attnT[:, h, sl], psO, rden, ALU.mult, [otag, ("rden",)], [("attnT", h, qt)])
    P.barrier()
    A.pop()

    P.stage = 'm0.D'
    A.push()
    wo = A.alloc([NCH, D], BF16)
    xt2 = [A.alloc([NCH, TT], F32) for _ in range(2)]
    DMA(P, "pool", wo, C.m0_wout[:, NCH:2 * NCH, :], [], [("wo",)], ("wo",))
    def dload(tt):
        if tt >= NTT:
            return
        DMA(P, "sp", xt2[tt % 2], src[:, :, tt * TT:(tt + 1) * TT], [(stag, s, tt)], [("xt2", tt % 2)], ("xt2", tt % 2))

    dload(0)
    for tt in range(NTT):
        sl = slice(tt * TT, (tt + 1) * TT)
        b = tt % 2
        dload(tt + 1)
        for dc in range(NCH):
            pb = dc % 3
            for hh in range(MLA_H):
                MM(P, C.psum[pb], wo[:, hh, dc * 128:(dc + 1) * 128], attnT[:, hh, sl], hh == 0, hh == MLA_H - 1,
                   [("wo",), ("attnT", hh, tt)], [("ps", pb)])
            TT_(P, "dve", xt2[b][:, dc, :], C.psum[pb], xt2[b][:, dc, :], ALU.add, [("ps", pb), ("xt2", b)], [("xt2", b)])
        DMA(P, "sp", C.xres[s][:, :, sl], xt2[b], [("xt2", b)], [("xres", s, tt)], ("xt2s", b))
    A.pop()
    A.pop()

TR = 256
NTR = S // TR
RH = 16
NEG_EXP_HALF = -float(np.exp(-0.5))
HORDER = [0, 2, 4, 6, 8, 10, 12, 14, 1, 3, 5, 7, 9, 11, 13, 15]
HPOS = {h: i for i, h in enumerate(HORDER)}
NVEC = 14


def rwkv_decl(C, dt):
    C.rw_w4 = dt("rw_w4", [4, 128, NCH, D])
    C.rw_w4b = dt("rw_w4b", [4, 128, NCH, D], BF16, kind="Internal")
    C.rw_l1_d = dt("rw_l1", [128, NCH, 256])
    C.rw_l2_d = dt("rw_l2", [128, 3, D])
    C.rw_vec_d = dt("rw_vec", [128, NCH, NVEC])
    C.rw_cm_d = dt("rw_cm", [128, 5, 512])
    C.rw_bo_d = dt("rw_bo", [128, 128])
    C.rw_sm_d = dt("rw_sm", [128, TR])
    C.rw_id_d = dt("rw_id", [128, 128])


def rwkv_consts(C, CA):
    P = C.P
    C.rw_vec = CA.alloc([NCH, NVEC], F32)
    C.rw_cm = CA.alloc([5, 512], BF16)
    C.rw_bo = CA.alloc([128], F32)
    C.rw_sm = CA.alloc([TR], F32)
    C.rw_id = CA.alloc([128], BF16)
    DMA(P, "sp", C.rw_vec, C.rw_vec_d, [], [("c", 20)], "c20")
    DMA(P, "pool", C.rw_cm, C.rw_cm_d, [], [("c", 21)], "c21")
    DMA(P, "sp", C.rw_bo, C.rw_bo_d, [], [("c", 22)], "c22")
    DMA(P, "sp", C.rw_sm, C.rw_sm_d, [], [("c", 23)], "c23")
    DMA(P, "pool", C.rw_id, C.rw_id_d, [], [("c", 24)], "c24")
    A = C.arena
    A.push()
    tmpw = [A.alloc([NCH, D], BF16) for _ in range(2)]
    for i in range(4):
        DMA(P, "pool", tmpw[i % 2], C.rw_w4[i], [], [("tmpw", i % 2)], ("tmpw", i % 2))
        DMA(P, "sp", C.rw_w4b[i], tmpw[i % 2], [("tmpw", i % 2)], [("w4b", i)], ("tmpws", i % 2))
    A.pop()
    TS(P, "dve", C.rw_vec[:, :, 13:14], C.rw_vec[:, :, 7:8], -1.0, 1.0, ALU.mult, ALU.add, [("c", 20)], [("c", 20)])


def rwkv_prep(inp, sh):
    f32 = np.float32
    g = lambda n: np.asarray(inp[n], f32)[0]
    sh["rw_w4"] = np.stack([_fm(g("w_r"), NCH), _fm(g("w_k"), NCH), _fm(g("w_v"), NCH), _fm(g("w_o"), NCH)], 0)
    sh["rw_l1"] = _fm(np.concatenate([g("w1"), g("a1"), g("g1")], axis=1), NCH)
    l2 = np.zeros((128, 3, D), f32)
    l2[0:64, 0] = g("w2")
    l2[0:64, 1] = g("a2")
    l2[:, 2] = g("g2")
    sh["rw_l2"] = l2
    vec = np.zeros((128, NCH, NVEC), f32)
    mu = g("time_mu")
    for i in range(6):
        vec[:, :, i] = _vec(mu[i], NCH)
    for j, n in enumerate(["k_k", "k_a", "w0", "a0"]):
        vec[:, :, 6 + j] = _vec(g(n), NCH)
    vec[:, :, 10] = _vec(g("r_k").reshape(D), NCH)
    vec[:, :, 11] = _vec(g("ln_x_g"), NCH)
    vec[:, :, 12] = _vec(g("ln_x_b"), NCH)
    sh["rw_vec"] = vec
    one = np.ones((128, 128), f32)
    sl = np.tril(one, -1)
    su = np.triu(one, 1)
    ui = np.triu(one, 0)
    idn = np.eye(128, dtype=f32)
    bd = np.zeros((128, 128), f32)
    bd[0:64, 0:64] = 1
    bd[64:128, 64:128] = 1
    sh["rw_cm"] = np.ascontiguousarray(np.stack([np.tile(m, (1, 4)) for m in (sl, su, ui, idn, bd)], axis=1))
    sh["rw_bo"] = bd
    sm = np.ones((128, TR), f32)
    sm[:, 0::128] = 0
    sh["rw_sm"] = sm
    sh["rw_id"] = idn


def phase_rwkv(C, s):
    P = C.P
    A = C.arena
    P.barrier()
    A.push()
    vec = C.rw_vec
    SLm, SUm, UIm, ID4, BD4 = (C.rw_cm[:, i, :] for i in range(5))
    bank_ctr = [0]

    def nb():
        bank_ctr[0] = (bank_ctr[0] + 1) % 7
        return bank_ctr[0]

    Sbd32 = A.alloc([NCH, 128], F32)
    Sbd = A.alloc([NCH, 128], BF16)
    hprev = A.alloc([NCH, 1], F32)
    l1 = A.alloc([NCH, 256], BF16)
    l2 = A.alloc([3, D], BF16)
    wbuf = [A.alloc([NCH, D], BF16) for _ in range(2)]
    C.sq = [A.alloc([TR], F32) for _ in range(8)]
    C.rstd = A.alloc([TR], F32)
    AhT = A.alloc([NCH, TR], BF16)
    RT = A.alloc([NCH, TR], BF16)
    BT = A.alloc([NCH, TR], BF16)
    KT = A.alloc([NCH, TR], BF16)
    BpT = A.alloc([NCH, TR], BF16)
    KpT = A.alloc([NCH, TR], BF16)
    vb = A.alloc([NCH, TR], BF16)
    gT = A.alloc([NCH, TR], BF16)
    bonus = A.alloc([NCH, TR], F32)
    WLt = A.alloc([NCH, 2], F32)

    P.op("dve", "memset", dict(ap=Sbd32, constant=0.0), writes=[("Sbd32",)])
    P.op("dve", "memset", dict(ap=Sbd, constant=0.0), writes=[("Sbd",)])
    P.op("dve", "memset", dict(ap=hprev, constant=0.0), writes=[("hprev",)])
    DMA(P, "pool", l1, C.rw_l1_d, [], [("l1",)], ("l1",))
    DMA(P, "pool", l2, C.rw_l2_d, [], [("l2",)], ("l2",))
    g = C.gains[:, 5 * NCH:6 * NCH]
    wcnt = [0]

    def load_w(i):
        b = wcnt[0] % 2
        wcnt[0] += 1
        DMA(P, "sp", wbuf[b], C.rw_w4b[i], [("w4b", i)], [("wbuf", b)], ("wbuf", b))
        return b

    for ti in range(NTR):
        tsl = slice(ti * TR, (ti + 1) * TR)
        src, stag = C.xsrc[s]
        rtag = (stag, s, ti // 2)
        P.stage = 'rw.P#%d' % ti
        A.push()
        xh = A.alloc([NCH, TR + 1], F32)
        dd = A.alloc([NCH, TR], F32)
        xi2 = A.alloc([2, NCH, TR], BF16)
        xi = [xi2[:, 0], xi2[:, 1]]
        rr = A.alloc([NCH, TR], F32)
        kx = A.alloc([NCH, TR], F32)
        vv = A.alloc([NCH, TR], F32)
        lw = A.alloc([NCH, TR], F32)
        aa = A.alloc([NCH, TR], F32)
        kk = A.alloc([NCH, TR], F32)
        lt = A.alloc([TR], BF16)
        e_x1 = A.alloc([NCH, TR], F32)
        lastC = A.alloc([NCH, 2], F32)

        DMA(P, "sp", xh[:, :, 1:TR + 1], src[:, :, tsl], [rtag], [("xh",)], ("xh",))
        P.op("pool", "tensor_copy", dict(out=xh[:, :, 0:1], in_=hprev), reads=[("hprev",)], writes=[("xh0",)])
        rms_to(C, lambda c: xh[:, c, 1:TR + 1], NCH, g, lambda c: xh[:, c, 1:TR + 1],
               lambda c: ("xh",), lambda c: ("xh",), D)
        P.op("pool", "tensor_copy", dict(out=hprev, in_=xh[:, :, TR:TR + 1]), reads=[("xh",)], writes=[("hprev",)])
        TT_(P, "dve", dd, xh[:, :, 0:TR], xh[:, :, 1:TR + 1], ALU.subtract, [("xh",), ("xh0",)], [("dd",)])
        mixcnt = [0]

        def mix(i):
            b = mixcnt[0] % 2
            mixcnt[0] += 1
            for c in range(NCH):
                STT(P, xi[b][:, c, :], dd[:, c, :], vec[:, c, i:i + 1], xh[:, c, 1:TR + 1], ALU.mult, ALU.add,
                    [("dd",), ("xh",)], [("xi", b)])
            return b

        def big_proj(mi, wi, dst, dtag):
            b = mix(mi)
            wb = load_w(wi)
            for oc in range(NCH):
                pb = nb()
                for k in range(NCH):
                    MM(P, C.psum[pb][:, 0:TR], wbuf[wb][:, k, oc * 128:(oc + 1) * 128], xi[b][:, k, :], k == 0, k == NCH - 1,
                       [("wbuf", wb), ("xi", b)], [("ps", pb)])
                ACT(P, dst[:, oc, :], C.psum[pb][:, 0:TR], AF.Copy, [("ps", pb)], [(dtag,)])

        big_proj(0, 0, rr, "rr")
        big_proj(2, 1, kx, "kx")
        big_proj(3, 2, vv, "vv")

        def lora(mi, c0, c1, func, l2i, dst, dtag, fin, bias_col):
            b = mix(mi)
            nl = c1 - c0
            pb = nb()
            for k in range(NCH):
                MM(P, C.psum[pb][0:nl, 0:TR], l1[:, k, c0:c1], xi[b][:, k, :], k == 0, k == NCH - 1, [("l1",), ("xi", b)], [("ps", pb)])
            ACT(P, lt[0:nl, :], C.psum[pb][0:nl, 0:TR], func, [("ps", pb)], [("lt",)])
            for oc in range(NCH):
                pb = nb()
                MM(P, C.psum[pb][:, 0:TR], l2[0:nl, l2i, oc * 128:(oc + 1) * 128], lt[0:nl, :], True, True, [("l2",), ("lt",)], [("ps", pb)])
                if bias_col is None:
                    ACT(P, dst[:, oc, :], C.psum[pb][:, 0:TR], fin, [("ps", pb)], [(dtag,)])
                else:
                    ACT(P, dst[:, oc, :], C.psum[pb][:, 0:TR], fin, [("ps", pb)], [(dtag,)], bias=vec[:, oc, bias_col:bias_col + 1])

        lora(1, 0, 64, AF.Tanh, 0, lw, "lw", AF.Sigmoid, 8)
        lora(4, 64, 128, AF.Copy, 1, aa, "aa", AF.Sigmoid, 9)
        lora(5, 128, 256, AF.Sigmoid, 2, gT, "gT", AF.Copy, None)

        P.stage = 'rw.E#%d' % ti
        P.barrier()
        e_cw = dd
        e_x0 = xh[:, :, 0:TR]
        e_n = xi2.rearrange("p a c t -> p (a c t)").bitcast(F32).rearrange("p (c t) -> p c t", c=NCH)
        f2 = lambda a: a
        for c in range(NCH):
            ACT(P, kk[:, c, :], kx[:, c, :], AF.Copy, [("kx",)], [("kk",)], scale=vec[:, c, 6:7])
        for c in range(NCH):
            ACT(P, e_x0[:, c, :], kk[:, c, :], AF.Square, [("kk",)], [("e_x0",)])
        pbs = [nb() for _ in range(4)]
        for c in range(NCH):
            MM(P, C.psum[pbs[c // 2]][:, (c % 2) * TR:(c % 2 + 1) * TR], C.rw_bo, e_x0[:, c, :], True, True, [("e_x0",)], [("ps", pbs[c // 2])])
        for i in range(4):
            ACT(P, e_n[:, 2 * i:2 * i + 2, :], C.psum[pbs[i]].rearrange("p (a t) -> p a t", a=2), AF.Ln, [("ps", pbs[i])], [("e_n",)],
                bias=C.eps_ap(1e-24), scale=1.0)
        for c in range(NCH):
            P.op("dve", "tensor_tensor_scan", dict(out=e_cw[:, c, :], data0=C.rw_sm, data1=lw[:, c, :], initial=0.0, op0=ALU.mult, op1=ALU.add),
                 reads=[("lw",)], writes=[("e_cw",)])
        TS(P, "dve", lastC, e_cw[:, :, 127::128], NEG_EXP_HALF, None, ALU.mult, None, [("e_cw",)], [("lastC",)])
        TT_(P, "dve", f2(lw), f2(e_cw), f2(lw), ALU.subtract, [("e_cw",), ("lw",)], [("lw",)])
        for c in range(NCH):
            TS(P, "dve", e_x1[:, c, :], aa[:, c, :], vec[:, c, 7:8], vec[:, c, 13:14], ALU.mult, ALU.add, [("aa",)], [("e_x1",)])
        TT_(P, "dve", f2(kx), f2(kx), f2(e_x1), ALU.mult, [("kx",), ("e_x1",)], [("kx",)])
        ACT(P, f2(e_n), f2(e_n), AF.Exp, [("e_n",)], [("e_n",)], scale=-0.5)
        TT_(P, "dve", f2(kk), f2(kk), f2(e_n), ALU.mult, [("kk",), ("e_n",)], [("kk",)])
        TT_(P, "dve", f2(aa), f2(aa), f2(kk), ALU.mult, [("aa",), ("kk",)], [("aa",)])
        for c in range(NCH):
            STT(P, e_x1[:, c, :], rr[:, c, :], vec[:, c, 10:11], kx[:, c, :], ALU.mult, ALU.mult, [("rr",), ("kx",), ("e_x1",)], [("e_x1",)])
        pbs = [nb() for _ in range(4)]
        for c in range(NCH):
            MM(P, C.psum[pbs[c // 2]][:, (c % 2) * TR:(c % 2 + 1) * TR], C.rw_bo, e_x1[:, c, :], True, True, [("e_x1",)], [("ps", pbs[c // 2])])
        for i in range(4):
            TT_(P, "dve", bonus[:, 2 * i:2 * i + 2, :], C.psum[pbs[i]].rearrange("p (a t) -> p a t", a=2), vv[:, 2 * i:2 * i + 2, :], ALU.mult,
                [("ps", pbs[i]), ("vv",)], [("bonus",)])
        ACT(P, f2(vb), f2(vv), AF.Copy, [("vv",)], [("vb",)])
        ACT(P, f2(e_x0), f2(lw), AF.Exp, [("lw",)], [("e_x0",)], scale=NEG_EXP_HALF)
        STT(P, f2(AhT), f2(kk), -1.0, f2(e_x0), ALU.mult, ALU.mult, [("kk",), ("e_x0",)], [("AhT",)])
        ACT(P, f2(e_x1), f2(e_cw), AF.Exp, [("e_cw",), ("e_x1",)], [("e_x1",)], scale=NEG_EXP_HALF)
        TT_(P, "dve", f2(RT), f2(rr), f2(e_x1), ALU.mult, [("rr",), ("e_x1",)], [("RT",)])
        ACT(P, f2(e_x0), f2(e_cw), AF.Exp, [("e_cw",), ("e_x0",)], [("e_x0",)], scale=-NEG_EXP_HALF)
        TT_(P, "dve", f2(BT), f2(aa), f2(e_x0), ALU.mult, [("aa",), ("e_x0",)], [("BT",)])
        TT_(P, "dve", f2(KT), f2(kx), f2(e_x0), ALU.mult, [("kx",), ("e_x0",)], [("KT",)])
        for c in range(NCH):
            for q in range(2):
                qs = slice(q * 128, (q + 1) * 128)
                ACT(P, e_x1[:, c, qs], e_cw[:, c, qs], AF.Exp, [("e_cw",), ("lastC",), ("e_x1",)], [("e_x1",)],
                    bias=lastC[:, c, q:q + 1], scale=-NEG_EXP_HALF)
        ACT(P, f2(WLt), f2(lastC), AF.Exp, [("lastC",)], [("WLt",)])
        TT_(P, "dve", f2(BpT), f2(aa), f2(e_x1), ALU.mult, [("aa",), ("e_x1",)], [("BpT",)])
        TT_(P, "dve", f2(KpT), f2(kx), f2(e_x1), ALU.mult, [("kx",), ("e_x1",)], [("KpT",)])
        P.barrier()
        A.pop()

        A.push()
        Bptm = A.alloc([D], BF16)
        Kptm = A.alloc([D], BF16)
        Vtm = A.alloc([D], BF16)
        Mm = A.alloc([RH * 128], BF16)
        Mt = A.alloc([RH * 128], BF16)
        NrbT = A.alloc([RH * 128], BF16)
        NrkT = A.alloc([RH * 128], BF16)
        MakT = A.alloc([RH * 128], BF16)
        XT = A.alloc([RH * 128], BF16)
        Tpp = [[A.alloc([512], BF16) for _ in range(2)] for _ in range(4)]
        Ttpp = [[A.alloc([512], BF16) for _ in range(2)] for _ in range(4)]
        Zb = A.alloc([D], BF16)
        Ub = A.alloc([D], BF16)
        tmpm = A.alloc([D], F32)
        Ytm = A.alloc([2, D], F32)
        ynb = A.alloc([2, D], BF16)
        ysq = A.alloc([2, D], F32)
        gst = A.alloc([6, 2 * RH], F32)
        xt = A.alloc([NCH, TR], F32)
        yg = A.alloc([NCH, TR], BF16)
        ytmp = A.alloc([NCH, TR], F32)
        wb = load_w(3)
        DMA(P, "sp", xt, src[:, :, tsl], [rtag], [("xt",)], ("xt",))

        for q in range(2):
            qs = slice(q * 128, (q + 1) * 128)
            P.stage = 'rw.T#%d' % ti
            for arr, dst, rtag_, wtag in ((BpT, Bptm, "BpT", "Bptm"), (KpT, Kptm, "KpT", "Kptm"), (vb, Vtm, "vb", "Vtm")):
                pb = nb()
                psb = C.psum[pb].bitcast(BF16)
                for c in range(NCH):
                    P.op("pe", "transpose", dict(out=psb[:, c * 128:(c + 1) * 128], in_=arr[:, c, qs], identity=C.rw_id),
                         reads=[(rtag_,)], writes=[("ps", pb)])
                ACT(P, dst, psb, AF.Copy, [("ps", pb)], [(wtag,)])
            P.stage = 'rw.N#%d' % ti
            for gi in range(4):
                heads = HORDER[gi * 4:(gi + 1) * 4]
                gsl = slice(gi * 512, (gi + 1) * 512)
                specs = ((Mm, AhT, BT, SLm, "Mm"), (Mt, BT, AhT, SUm, "Mt"), (NrbT, BT, RT, UIm, "NrbT"),
                         (NrkT, KT, RT, UIm, "NrkT"), (MakT, KT, AhT, SUm, "MakT"))
                for dst, la, ra, mask, tg in specs:
                    pb = nb()
                    for i, h in enumerate(heads):
                        p_, rb = h // 2, 64 * (h % 2)
                        MM(P, C.psum[pb][:, i * 128:(i + 1) * 128], la[rb:rb + 64, p_, qs], ra[rb:rb + 64, p_, qs], True, True,
                           [("AhT",), ("BT",), ("RT",), ("KT",)], [("ps", pb)])
                    TT_(P, "dve", dst[:, gsl], C.psum[pb], mask, ALU.mult, [("ps", pb)], [(tg, gi)])
            P.stage = 'rw.Neu#%d' % ti
            cur = [0, 0, 0, 0]
            for gi in range(4):
                gsl = slice(gi * 512, (gi + 1) * 512)
                TT_(P, "dve", XT[:, gsl], Mt[:, gsl], ID4, ALU.add, [("Mt", gi)], [("XT", gi)])
            for step in range(6):
                for gi in range(4):
                    gsl = slice(gi * 512, (gi + 1) * 512)
                    if step == 0:
                        Tc, Ttc, tcr = Mm[:, gsl], Mt[:, gsl], [("Mm", gi), ("Mt", gi)]
                    else:
                        Tc, Ttc = Tpp[gi][cur[gi]], Ttpp[gi][cur[gi]]
                        tcr = [("Tpp", gi, cur[gi]), ("Ttpp", gi, cur[gi])]
                    nxt = 1 - cur[gi] if step > 0 else 0
                    pb = nb()
                    for i in range(4):
                        bs = slice(i * 128, (i + 1) * 128)
                        MM(P, C.psum[pb][:, bs], Ttc[:, bs], Tc[:, bs], True, True, tcr, [("ps", pb)])
                    ACT(P, Tpp[gi][nxt], C.psum[pb], AF.Copy, [("ps", pb)], [("Tpp", gi, nxt)])
                    if step < 5:
                        pb2 = nb()
                        for i in range(4):
                            bs = slice(i * 128, (i + 1) * 128)
                            MM(P, C.psum[pb2][:, bs], Tc[:, bs], Ttc[:, bs], True, True, tcr, [("ps", pb2)])
                        ACT(P, Ttpp[gi][nxt], C.psum[pb2], AF.Copy, [("ps", pb2)], [("Ttpp", gi, nxt)])
                    cur[gi] = nxt
                for gi in range(4):
                    gsl = slice(gi * 512, (gi + 1) * 512)
                    Tn = Tpp[gi][cur[gi]]
                    pb = nb()
                    for i in range(4):
                        bs = slice(i * 128, (i + 1) * 128)
                        MM(P, C.psum[pb][:, bs], Tn[:, bs], XT[:, gi * 512 + i * 128:gi * 512 + (i + 1) * 128], True, True,
                           [("Tpp", gi, cur[gi]), ("XT", gi)], [("ps", pb)])
                    TT_(P, "dve", XT[:, gsl], C.psum[pb], XT[:, gsl], ALU.add, [("ps", pb), ("XT", gi)], [("XT", gi)])

            def hm(arr, h):
                pos = HPOS[h]
                return arr[:, pos * 128:(pos + 1) * 128]

            def htag(name, h):
                return (name, HPOS[h] // 4)

            P.stage = 'rw.S#%d' % ti
            zb = [nb(), nb()]
            for p_ in range(NCH):
                bk = C.psum[zb[p_ // 4]]
                co = (p_ % 4) * 128
                MM(P, bk[:, co:co + 128], AhT[:, p_, qs], Sbd[:, p_, :], True, False, [("AhT",), ("Sbd",)], [("ps", zb[p_ // 4])])
                for hh in range(2):
                    h = 2 * p_ + hh
                    MM(P, bk[:, co + hh * 64:co + (hh + 1) * 64], hm(MakT, h), Vtm[:, h * 64:(h + 1) * 64], False, hh == 1,
                       [htag("MakT", h), ("Vtm",)], [("ps", zb[p_ // 4])])
            ACT(P, Zb[:, 0:512], C.psum[zb[0]], AF.Copy, [("ps", zb[0])], [("Zb", 0)])
            P.op("dve", "tensor_copy", dict(out=Zb[:, 512:1024], in_=C.psum[zb[1]]), reads=[("ps", zb[1])], writes=[("Zb", 1)])
            ub = [nb(), nb()]
            for h in range(RH):
                bk = C.psum[ub[h // 8]]
                co = (h % 8) * 64
                MM(P, bk[:, co:co + 64], hm(XT, h), Zb[:, h * 64:(h + 1) * 64], True, True,
                   [htag("XT", h), ("Zb", h // 8)], [("ps", ub[h // 8])])
            ACT(P, Ub[:, 0:512], C.psum[ub[0]], AF.Copy, [("ps", ub[0])], [("Ub", 0)])
            P.op("dve", "tensor_copy", dict(out=Ub[:, 512:1024], in_=C.psum[ub[1]]), reads=[("ps", ub[1])], writes=[("Ub", 1)])
            yb = [nb(), nb()]
            for p_ in range(NCH):
                bk = C.psum[yb[p_ // 4]]
                co = (p_ % 4) * 128
                MM(P, bk[:, co:co + 128], RT[:, p_, qs], Sbd[:, p_, :], True, False, [("RT",), ("Sbd",)], [("ps", yb[p_ // 4])])
                for hh in range(2):
                    h = 2 * p_ + hh
                    MM(P, bk[:, co + hh * 64:co + (hh + 1) * 64], hm(NrbT, h), Ub[:, h * 64:(h + 1) * 64], False, False,
                       [htag("NrbT", h), ("Ub", h // 8)], [("ps", yb[p_ // 4])])
                    MM(P, bk[:, co + hh * 64:co + (hh + 1) * 64], hm(NrkT, h), Vtm[:, h * 64:(h + 1) * 64], False, hh == 1,
                       [htag("NrkT", h), ("Vtm",)], [("ps", yb[p_ // 4])])
            ACT(P, Ytm[:, q, 0:512], C.psum[yb[0]], AF.Copy, [("ps", yb[0])], [("Ytm", q)])
            P.op("dve", "tensor_copy", dict(out=Ytm[:, q, 512:1024], in_=C.psum[yb[1]]), reads=[("ps", yb[1])], writes=[("Ytm", q)])
            sb_ = [nb(), nb()]
            for p_ in range(NCH):
                bk = C.psum[sb_[p_ // 4]]
                co = (p_ % 4) * 128
                ps_ = slice(p_ * 128, (p_ + 1) * 128)
                MM(P, bk[:, co:co + 128], Bptm[:, ps_], Ub[:, ps_], True, False, [("Bptm",), ("Ub", p_ // 4)], [("ps", sb_[p_ // 4])])
                MM(P, bk[:, co:co + 128], Kptm[:, ps_], Vtm[:, ps_], False, True, [("Kptm",), ("Vtm",)], [("ps", sb_[p_ // 4])])
            for hb in range(2):
                TT_(P, "dve", tmpm[:, hb * 512:(hb + 1) * 512], C.psum[sb_[hb]], BD4, ALU.mult, [("ps", sb_[hb])], [("tmpm",)])
            for p_ in range(NCH):
                STT(P, Sbd32[:, p_, :], Sbd32[:, p_, :], WLt[:, p_, q:q + 1], tmpm[:, p_ * 128:(p_ + 1) * 128], ALU.mult, ALU.add,
                    [("Sbd32",), ("WLt",), ("tmpm",)], [("Sbd32",)])
            P.op("dve", "tensor_copy", dict(out=Sbd, in_=Sbd32), reads=[("Sbd32",)], writes=[("Sbd",)])

        P.stage = 'rw.G#%d' % ti
        trb = [nb(), nb()]
        y4 = Ytm.rearrange("p q (h n) -> p (q h) n", h=RH)
        P.op("dve", "tensor_reduce", dict(out=gst[:, 0, :], in_=y4, axis=AX.X, op=ALU.add), reads=[("Ytm", 0), ("Ytm", 1)], writes=[("gst",)])
        ACT(P, ysq, Ytm, AF.Square, [("Ytm", 0), ("Ytm", 1)], [("ysq",)])
        P.op("dve", "tensor_reduce", dict(out=gst[:, 1, :], in_=ysq.rearrange("p q (h n) -> p (q h) n", h=RH), axis=AX.X, op=ALU.add),
             reads=[("ysq",)], writes=[("gst",)])
        TS(P, "dve", gst[:, 2, :], gst[:, 0, :], 1.0 / 64, None, ALU.mult, None, [("gst",)], [("gst",)])
        TT_(P, "dve", gst[:, 3, :], gst[:, 2, :], gst[:, 2, :], ALU.mult, [("gst",)], [("gst",)])
        STT(P, gst[:, 3, :], gst[:, 1, :], 1.0 / 64, gst[:, 3, :], ALU.mult, ALU.subtract, [("gst",)], [("gst",)])
        ACT(P, gst[:, 3, :], gst[:, 3, :], AF.Sqrt, [("gst",)], [("gst",)], bias=C.eps_ap(64e-5), scale=1.0)
        P.op("dve", "reciprocal", dict(out=gst[:, 3, :], in_=gst[:, 3, :]), reads=[("gst",)], writes=[("gst",)])
        STT(P, gst[:, 4, :], gst[:, 2, :], -1.0, gst[:, 3, :], ALU.mult, ALU.mult, [("gst",)], [("gst",)])
        for q in range(2):
            for h in range(RH):
                j = q * RH + h
                if h % 2 == 0:
                    ACT(P, ynb[:, q, h * 64:(h + 1) * 64], Ytm[:, q, h * 64:(h + 1) * 64], AF.Identity, [("Ytm", q), ("gst",)], [("ynb", q, h)],
                        bias=gst[:, 4, j:j + 1], scale=gst[:, 3, j:j + 1])
                else:
                    TS(P, "dve", ynb[:, q, h * 64:(h + 1) * 64], Ytm[:, q, h * 64:(h + 1) * 64], gst[:, 2, j:j + 1], gst[:, 3, j:j + 1],
                       ALU.subtract, ALU.mult, [("Ytm", q), ("gst",)], [("ynb", q, h)])
            for c in range(NCH):
                psb = C.psum[trb[c // 4]].bitcast(BF16)
                o0 = (c % 4) * TR + q * 128
                P.op("pe", "transpose", dict(out=psb[:, o0:o0 + 128], in_=ynb[:, q, c * 128:(c + 1) * 128], identity=C.rw_id),
                     reads=[("ynb", q, 2 * c), ("ynb", q, 2 * c + 1)], writes=[("ps", trb[c // 4])])
        for c in range(NCH):
            psb = C.psum[trb[c // 4]].bitcast(BF16)
            o0 = (c % 4) * TR
            TS(P, "dve", ytmp[:, c, :], psb[:, o0:o0 + TR], vec[:, c, 11:12], vec[:, c, 12:13], ALU.mult, ALU.add, [("ps", trb[c // 4])], [("ytmp", c // 4)])
        for hf in range(2):
            cs = slice(hf * 4, (hf + 1) * 4)
            TT_(P, "dve", ytmp[:, cs, :], ytmp[:, cs, :], bonus[:, cs, :], ALU.add, [("ytmp", hf), ("bonus",)], [("ytmp", hf)])
            TT_(P, "dve", yg[:, cs, :], ytmp[:, cs, :], gT[:, cs, :], ALU.mult, [("ytmp", hf), ("gT",)], [("yg", hf)])
        P.stage = 'rw.O#%d' % ti
        for dc in range(NCH):
            pb = nb()
            for c in range(NCH):
                MM(P, C.psum[pb][:, 0:TR], wbuf[wb][:, c, dc * 128:(dc + 1) * 128], yg[:, c, :], c == 0, c == NCH - 1,
                   [("wbuf", wb), ("yg", c // 4)], [("ps", pb)])
            TT_(P, "dve", xt[:, dc, :], C.psum[pb][:, 0:TR], xt[:, dc, :], ALU.add, [("ps", pb), ("xt",)], [("xt",)])
        DMA(P, "sp", C.xres[s][:, :, tsl], xt, [("xt",)], [("xres", s, ti // 2)], ("xts",))
        P.barrier()
        A.pop()
    C.xsrc[s] = (C.xres[s], "xres")
    A.pop()

CVALS = [1e-6, 1e-5, 64e-5, 0.0, 1.0, -1.0, 0.5, 1e-24]
C_NG = 7


def build(phases, annotate=False):
    nc = bass.Bass("TRN2", target_bir_lowering=False)
    C = Ctx()
    C.nc = nc
    dt = lambda name, shape, dtype=F32, kind="ExternalInput": nc.dram_tensor(name, list(shape), dtype, kind=kind).ap()
    xin = dt("xin", [SEQ_PER_CORE, 128, NCH, S])
    C.out = dt("out", [SEQ_PER_CORE, 128, NCH, S], kind="ExternalOutput")
    C.xres = dt("xres", [SEQ_PER_CORE, 128, NCH, S], kind="Internal")
    C.xsrc = {s: (xin[s], "xin") for s in range(SEQ_PER_CORE)}
    C.pos = dt("pos", [SEQ_PER_CORE, 1, S], I32)
    C.w_gate = dt("w_gate", [4, 128, NCH, DFF])
    C.w_up = dt("w_up", [4, 128, NCH, DFF])
    C.w_down = dt("w_down", [4, 128, DFF // 128, D])
    gains_d = dt("gains", [128, C_NG * NCH])
    ones_d = dt("ones_f32", [128, 128])
    cvals_d = dt("cvals", [128, len(CVALS)])
    tri_d = dt("tri", [128, 128])
    ropec_d = dt("ropec", [128, 4])
    C.m0_win_conv = dt("m0_win_conv", [128, 2 * NCH, NCH, 128])
    C.m0_win_lat = dt("m0_win_lat", [128, NCH, 896])
    m0_cvec_d = dt("m0_cvec", [128, NCH, 34])
    m0_vec_d = dt("m0_vec", [128, 6])
    C.m0_wuq = dt("m0_wuq", [128, MLA_H, 4, 256])
    C.m0_wukv = dt("m0_wukv", [128, MLA_H, 2, 256])
    C.m0_wout = dt("m0_wout", [128, 2 * NCH, D])
    rwkv_decl(C, dt)

    import contextlib
    with contextlib.ExitStack() as st:
        ARENA = 194 * 1024
        CONST = 12 * 1024
        arena_t = st.enter_context(nc.sbuf_tensor("arena", [128, ARENA], mybir.dt.uint8))
        cst_t = st.enter_context(nc.sbuf_tensor("consts", [128, CONST], mybir.dt.uint8))
        C.arena = Arena(arena_t[:], ARENA)
        CA = Arena(cst_t[:], CONST)
        C.psum = [st.enter_context(nc.psum_tensor("ps%d" % i, [128, 512], F32))[:] for i in range(8)]
        C.P = Prog(nc)
        P = C.P
        P.annotate = annotate
        C.gains = CA.alloc([C_NG * NCH], F32)
        C.ones_f32 = CA.alloc([128], F32)
        C.ones_bf = CA.alloc([128], BF16)
        C.tri_bf = CA.alloc([128], BF16)
        C.cvals = CA.alloc([len(CVALS)], F32)
        C.ropec = CA.alloc([4], F32)
        C.m0_cvec = CA.alloc([NCH, 34], F32)
        C.m0_vec = CA.alloc([6], F32)
        C.eps_ap = lambda v: C.cvals[:, CVALS.index(v):CVALS.index(v) + 1]
        C.G_FINAL = 6
        DMA(P, "sp", C.gains, gains_d, [], [("c", 0)], "c0")
        DMA(P, "sp", C.ones_f32, ones_d, [], [("c", 1)], "c1")
        DMA(P, "pool", C.ones_bf, ones_d, [], [("c", 2)], "c2")
        DMA(P, "pool", C.tri_bf, tri_d, [], [("c", 3)], "c3")
        DMA(P, "sp", C.cvals, cvals_d, [], [("c", 4)], "c4")
        DMA(P, "sp", C.ropec, ropec_d, [], [("c", 5)], "c5")
        DMA(P, "sp", C.m0_cvec, m0_cvec_d, [], [("c", 6)], "c6")
        DMA(P, "sp", C.m0_vec, m0_vec_d, [], [("c", 7)], "c7")
        rwkv_consts(C, CA)
        P.barrier()

        for pi, ph in enumerate(phases):
            kind = ph[0]
            P.prefix = '%02d%s:' % (pi, kind)
            if kind == "ffn":
                phase_ffn(C, ph[1], ph[2], *ph[3:])
            elif kind == "final":
                phase_final(C, ph[1])
            elif kind == "dump":
                phase_dump(C, ph[1])
            elif kind == "mix0":
                phase_mix0(C, ph[1])
            elif kind == "rwkv":
                phase_rwkv(C, ph[1])
            else:
                raise ValueError(kind)
        tags = []
        for s in range(SEQ_PER_CORE):
            for tt in range(NTT):
                tags.append(("out", s, tt))
                tags.append(("xres", s, tt))
        P.op("sp", None, None, reads=tags)
        P.barrier()
        P.finalize_and_emit()
    return nc


def _fm(w, nk):
    w = np.asarray(w, np.float32)
    return np.ascontiguousarray(w.reshape(nk, 128, -1).transpose(1, 0, 2))


def _vec(v, nk):
    return np.ascontiguousarray(np.asarray(v, np.float32).reshape(nk, 128).T)


def prep_shared(inp):
    f32 = np.float32
    sh = {}
    wg = np.asarray(inp["ffn_w_gate"], f32).reshape(4, NCH, 128, DFF).transpose(0, 2, 1, 3)
    wu = np.asarray(inp["ffn_w_up"], f32).reshape(4, NCH, 128, DFF).transpose(0, 2, 1, 3)
    wd = np.asarray(inp["ffn_w_down"], f32).reshape(4, DFF // 128, 128, D).transpose(0, 2, 1, 3)
    sh["w_gate"] = np.ascontiguousarray(wg)
    sh["w_up"] = np.ascontiguousarray(wu)
    sh["w_down"] = np.ascontiguousarray(wd)
    gl = [np.asarray(inp["ffn_norm"], f32).reshape(4, D)[i] for i in range(4)]
    gl.append(np.asarray(inp["mix_norm_even"], f32).reshape(D))
    gl.append(np.asarray(inp["mix_norm_odd"], f32).reshape(D))
    gl.append(np.asarray(inp["final_norm"], f32).reshape(D))
    gains = np.stack([g.reshape(NCH, 128).T for g in gl], axis=1)
    sh["gains"] = np.ascontiguousarray(gains.reshape(128, C_NG * NCH))
    sh["ones_f32"] = np.ones((128, 128), f32)
    sh["cvals"] = np.tile(np.asarray(CVALS, f32)[None, :], (128, 1))
    sh["tri"] = np.triu(np.ones((128, 128), f32))
    invf = (1.0 / (np.float32(10000.0) ** (np.arange(0, 64, 2, dtype=f32) / f32(64)))).astype(f32)
    rc = np.zeros((128, 4), f32)
    rc[0:64, 0] = np.concatenate([invf, invf])
    rc[0:64, 1] = np.pi / 2
    rc[0:32, 2] = np.pi
    sh["ropec"] = rc
    w_in = np.asarray(inp["w_in"], f32)[0]
    sh["m0_win_conv"] = np.ascontiguousarray(w_in[:, 0:2 * D].reshape(NCH, 128, 2 * NCH, 128).transpose(1, 2, 0, 3))
    lat = np.concatenate([w_in[:, 2 * D:2 * D + 832], w_in[:, 2 * D + 800:2 * D + 832], w_in[:, 2 * D + 768:2 * D + 800]], axis=1)
    sh["m0_win_lat"] = _fm(lat, NCH)
    cvec = np.zeros((128, NCH, 34), f32)
    cvec[:, :, 0:31] = np.asarray(inp["conv_w"], f32)[0].reshape(31, NCH, 128).transpose(2, 1, 0)
    cvec[:, :, 31] = _vec(inp["conv_b"][0], NCH)
    cvec[:, :, 32] = _vec(inp["conv_ln_g"][0], NCH)
    cvec[:, :, 33] = _vec(inp["conv_ln_b"][0], NCH)
    sh["m0_cvec"] = cvec
    sh["m0_vec"] = np.concatenate([_vec(inp["q_norm"][0], 4), _vec(inp["kv_norm"][0], 2)], axis=1)
    wuq = np.asarray(inp["w_uq"], f32)[0].reshape(512, MLA_H, 192)
    wuq = np.concatenate([wuq, wuq[:, :, 160:192], wuq[:, :, 128:160]], axis=2)
    sh["m0_wuq"] = np.ascontiguousarray(wuq.reshape(4, 128, MLA_H, 256).transpose(1, 2, 0, 3))
    wukv = np.asarray(inp["w_ukv"], f32)[0].reshape(256, MLA_H, 256)
    sh["m0_wukv"] = np.ascontiguousarray(wukv.reshape(2, 128, MLA_H, 256).transpose(1, 2, 0, 3))
    sh["m0_wout"] = _fm(np.asarray(inp["w_out"], f32)[0], 2 * NCH)
    rwkv_prep(inp, sh)
    return sh


def default_phases():
    ph = []
    for s in range(SEQ_PER_CORE):
        ph += [("ffn", s, 0), ("mix0", s), ("ffn", s, 1, True, False), ("ffn", s, 2, False, True), ("rwkv", s), ("ffn", s, 3, True, False, True)]
    return ph


def core_inputs(inp, sh, c):
    x = np.asarray(inp["x"], np.float32)
    xc = x[c * SEQ_PER_CORE:(c + 1) * SEQ_PER_CORE]
    xT = xc.reshape(SEQ_PER_CORE, S, NCH, 128).transpose(0, 3, 2, 1)
    m = dict(sh)
    m["xin"] = np.ascontiguousarray(xT)
    m["pos"] = np.ascontiguousarray(np.asarray(inp["positions"], np.int32)[c * SEQ_PER_CORE:(c + 1) * SEQ_PER_CORE].reshape(SEQ_PER_CORE, 1, S))
    return m


def kernel(**inp):
    sh = prep_shared(inp)
    nc = build(default_phases())
    in_maps = [core_inputs(inp, sh, c) for c in range(NCORES)]
    res = run_bass_kernel_spmd(nc, in_maps, core_ids=list(range(NCORES)))
    outs = []
    for c in range(NCORES):
        o = np.asarray(res.results[c]["out"], np.float32)
        outs.append(o.transpose(0, 3, 2, 1).reshape(SEQ_PER_CORE, S, D))
    return np.ascontiguousarray(np.concatenate(outs, axis=0))
```

```python
import numpy as np
import concourse.bass as bass
import concourse.mybir as mybir
from concourse.bass_utils import run_bass_kernel_spmd

F32 = mybir.dt.float32
F32R = mybir.dt.float32r
BF16 = mybir.dt.bfloat16
I32 = mybir.dt.int32
AF = mybir.ActivationFunctionType
ALU = mybir.AluOpType
AX = mybir.AxisListType

D = 1024
S = 2048
DFF = 2816
NCH = 8
TT = 512
NTT = S // TT
EPS = 1e-6
NCORES = 8
SEQ_PER_CORE = 2

ENGS = ("pe", "act", "dve", "pool", "sp")


class _Op:
    __slots__ = ("eng", "fn", "waits", "key", "idx", "needed", "value", "is_dma", "stage", "stream")

    def __init__(self, eng, fn, key, idx, is_dma):
        self.eng = eng
        self.fn = fn
        self.waits = []
        self.key = key
        self.idx = idx
        self.needed = is_dma
        self.value = None
        self.is_dma = is_dma
        self.stage = None
        self.stream = None


class Prog:
    def __init__(self, nc):
        self.nc = nc
        self.ops = {e: [] for e in ENGS}
        self.bykey = {}
        self.last_w = {}
        self.readers = {}
        self.seen = {e: {} for e in ENGS}
        self.stage = None
        self.stream = None
        self.last_in_stream = {}
        self.prefix = ''
        self.annotate = False

    def _need(self, eng, prod, waits):
        if prod is None:
            return
        if prod.key == eng and eng == "pe" and not prod.is_dma:
            return
        sk = self.seen[eng]
        if sk.get(prod.key, -1) >= prod.idx:
            return
        sk[prod.key] = prod.idx
        prod.needed = True
        waits.append(prod)

    def op(self, eng, meth, kw=None, reads=(), writes=(), dma_key=None):
        fn = None if meth is None else (meth, kw)
        is_dma = dma_key is not None
        key = ("dma", dma_key) if is_dma else eng
        lst = self.bykey.setdefault(key, [])
        o = _Op(eng, fn, key, len(lst), is_dma)
        o.stage = (self.prefix + self.stage) if self.stage else None
        o.stream = self.stream
        if self.stream is not None and fn is not None:
            self.last_in_stream[(self.stream, key)] = o
        lst.append(o)
        waits = []
        for t in reads:
            self._need(eng, self.last_w.get(t), waits)
            if t[0] == "ps":
                for r in self.readers.get(t, ()):
                    if r.eng != eng:
                        self._need(eng, r, waits)
        for t in writes:
            self._need(eng, self.last_w.get(t), waits)
            for r in self.readers.get(t, ()):
                if r.key == eng and not r.is_dma:
                    continue
                self._need(eng, r, waits)
        best = {}
        for w in waits:
            if w.key not in best or best[w.key].idx < w.idx:
                best[w.key] = w
        o.waits = list(best.values())
        for t in reads:
            self.readers.setdefault(t, []).append(o)
        for t in writes:
            self.last_w[t] = o
            self.readers[t] = []
        self.ops[eng].append(o)
        return o

    def barrier(self):
        lasts = [lst[-1] for lst in self.bykey.values() if lst]
        for eng in ENGS:
            o = _Op(eng, None, eng, len(self.bykey.setdefault(eng, [])), False)
            self.bykey[eng].append(o)
            waits = []
            for p in lasts:
                if p.fn is None and not p.is_dma:
                    continue
                if p.key == eng and not p.is_dma and eng == "pe":
                    continue
                self._need(eng, p, waits)
            o.waits = waits
            self.ops[eng].append(o)

    def stream_barrier(self, stream):
        lasts = [o for (st, k), o in self.last_in_stream.items() if st == stream]
        for eng in ENGS:
            o = _Op(eng, None, eng, len(self.bykey.setdefault(eng, [])), False)
            self.bykey[eng].append(o)
            waits = []
            for p in lasts:
                if p.key == eng and not p.is_dma and eng == "pe":
                    continue
                self._need(eng, p, waits)
            o.waits = waits
            self.ops[eng].append(o)

    def finalize_and_emit(self):
        nc = self.nc
        keys = list(self.bykey.keys())
        for k in keys:
            cnt = 0
            for o in self.bykey[k]:
                if o.needed:
                    cnt += 16 if o.is_dma else 1
                    o.value = cnt
            assert cnt < 60000, (k, cnt)
        import contextlib
        with contextlib.ExitStack() as st:
            sems = {}
            for i, k in enumerate(keys):
                sems[k] = st.enter_context(nc.semaphore("s%d" % i))
            block = st.enter_context(nc.Block())

            def run(engname):
                def body(e):
                    for o in self.ops[engname]:
                        for w in o.waits:
                            e.wait_ge(sems[w.key], w.value)
                        if o.fn is None:
                            continue
                        ins = getattr(e, o.fn[0])(**o.fn[1])
                        if self.annotate and o.stage:
                            ins.annotate(o.stage)
                        if o.needed:
                            ins.then_inc(sems[o.key], 16 if o.is_dma else 1)
                return body

            block.sync(run("sp"))
            block.scalar(run("act"))
            block.vector(run("dve"))
            block.gpsimd(run("pool"))
            block.tensor(run("pe"))


class Arena:
    def __init__(self, base_ap_u8, size):
        self.base = base_ap_u8
        self.size = size
        self.off = 0
        self.stack = []

    def push(self):
        self.stack.append(self.off)

    def pop(self):
        self.off = self.stack.pop()

    def alloc(self, shape_free, dtype, parts=128):
        esz = mybir.dt.size(dtype)
        n = 1
        for d in shape_free:
            n *= d
        nbytes = n * esz
        self.off = (self.off + 63) // 64 * 64
        assert self.off + nbytes <= self.size, ("SBUF arena overflow", self.off, nbytes, self.size)
        v = self.base[0:parts, self.off:self.off + nbytes].bitcast(dtype)
        self.off += nbytes
        if len(shape_free) == 2:
            v = v.rearrange("p (a b) -> p a b", a=shape_free[0])
        elif len(shape_free) == 3:
            v = v.rearrange("p (a b c) -> p a b c", a=shape_free[0], b=shape_free[1])
        return v


class Ctx:
    pass


def _tags(name, *idx):
    return (name,) + tuple(idx)


def MM(P, out, lhsT, rhs, start, stop, reads, writes):
    P.op("pe", "matmul", dict(out=out, lhsT=lhsT, rhs=rhs, start=start, stop=stop), reads=reads, writes=writes)


def ACT(P, out, in_, func, reads, writes, bias=None, scale=None):
    kw = dict(out=out, in_=in_, func=func)
    if bias is not None:
        kw["bias"] = bias
    if scale is not None:
        kw["scale"] = scale
    P.op("act", "activation", kw, reads=reads, writes=writes)


def TT_(P, eng, out, in0, in1, op, reads, writes):
    P.op(eng, "tensor_tensor", dict(out=out, in0=in0, in1=in1, op=op), reads=reads, writes=writes)


def TS(P, eng, out, in0, s1, s2, op0, op1, reads, writes):
    kw = dict(out=out, in0=in0, scalar1=s1, scalar2=s2, op0=op0)
    if op1 is not None:
        kw["op1"] = op1
    P.op(eng, "tensor_scalar", kw, reads=reads, writes=writes)


def STT(P, out, in0, scalar, in1, op0, op1, reads, writes):
    P.op("dve", "scalar_tensor_tensor", dict(out=out, in0=in0, scalar=scalar, in1=in1, op0=op0, op1=op1),
         reads=reads, writes=writes)


def DMA(P, q, out, in_, reads, writes, key):
    P.op(q, "dma_start", dict(out=out, in_=in_), reads=reads, writes=writes, dma_key=key)


def load_x(C, xs, s):
    src, stag = C.xsrc[s]
    for tt in range(NTT):
        sl = slice(tt * TT, (tt + 1) * TT)
        DMA(C.P, "sp", xs[:, :, sl], src[:, :, sl], [(stag, s, tt)], [("x", tt)], ("xl", tt))


def store_x(C, xs, s, dst=None, dtag="xres"):
    if dst is None:
        dst = C.xres[s]
        C.xsrc[s] = (C.xres[s], "xres")
    for tt in range(NTT):
        sl = slice(tt * TT, (tt + 1) * TT)
        DMA(C.P, "sp", dst[:, :, sl], xs[:, :, sl], [("x", tt)], [(dtag, s, tt)], ("xs", tt))


def store_x_tile(C, xs, s, tt, dst=None, dtag="xres"):
    if dst is None:
        dst = C.xres[s]
        C.xsrc[s] = (C.xres[s], "xres")
    sl = slice(tt * TT, (tt + 1) * TT)
    DMA(C.P, "sp", dst[:, :, sl], xs[:, :, sl], [("x", tt)], [(dtag, s, tt)], ("xs", tt))


def rms_to(C, src_fn, nchunk, gain_ap, out_fn, src_tags, out_tags, dim, eps=EPS, parts=128, ones=None):
    P = C.P
    n = src_fn(0).shape[-1]
    ps = C.psum[7][0:parts, 0:n]
    ones = C.ones_f32 if ones is None else ones
    rstd = C.rstd[0:parts, 0:n]
    for c in range(nchunk):
        nsq = len(C.sq)
        sq = C.sq[c % nsq][0:parts, 0:n]
        ACT(P, sq, src_fn(c), AF.Square, [src_tags(c)], [("sq", c % nsq)])
        MM(P, ps, ones[0:parts, 0:parts], sq, c == 0, c == nchunk - 1, [("sq", c % nsq), ("const",)], [("ps", 7)])
    ACT(P, rstd, ps, AF.Ln, [("ps", 7), ("const",)], [("rstd",)], bias=C.eps_ap(eps)[0:parts], scale=1.0 / dim)
    ACT(P, rstd, rstd, AF.Exp, [("rstd",)], [("rstd",)], scale=-0.5)
    for c in range(nchunk):
        STT(P, out_fn(c), src_fn(c), gain_ap[0:parts, c:c + 1], rstd, ALU.mult, ALU.mult,
            [src_tags(c), ("rstd",), ("gains",)], [out_tags(c)])


def phase_ffn(C, s, fi, load=True, store=True, final=False):
    P = C.P
    A = C.arena
    if load:
        P.barrier()
    A.push()
    xs = A.alloc([NCH, S], F32)
    hT = A.alloc([NCH, S], BF16)
    wg = [A.alloc([NCH, 512], BF16) for _ in range(2)]
    wu = [A.alloc([NCH, 512], BF16) for _ in range(2)]
    wd = [A.alloc([4, D], BF16) for _ in range(2)]
    act = [A.alloc([4, TT], BF16) for _ in range(2)]
    sg = [A.alloc([TT], F32) for _ in range(2)]
    C.sq = [A.alloc([TT], F32) for _ in range(8)]
    C.rstd = A.alloc([TT], F32)

    P.stage = "ffn.norm"
    if load:
        load_x(C, xs, s)
    g = C.gains[:, fi * NCH:(fi + 1) * NCH]

    def norm(tt):
        if tt >= NTT:
            return
        sl = slice(tt * TT, (tt + 1) * TT)
        P.stage = "ffn.norm"
        rms_to(C, lambda c: xs[:, c, sl], NCH, g, lambda c: hT[:, c, sl],
               lambda c: ("x", tt), lambda c: ("hT", tt), D)
        P.stage = "ffn.main"

    groups = [(0, 4), (4, 8), (8, 12), (12, 16), (16, 20), (20, 22)]

    def wload(gi):
        if gi >= len(groups):
            return
        c0, c1 = groups[gi]
        b = gi % 2
        nf = c1 - c0
        f0, f1 = c0 * 128, c1 * 128
        DMA(P, "pool", wg[b][:, :, 0:nf * 128], C.w_gate[fi][:, :, f0:f1], [], [("wg", b)], ("wg", b))
        DMA(P, "pool", wu[b][:, :, 0:nf * 128], C.w_up[fi][:, :, f0:f1], [], [("wu", b)], ("wu", b))
        DMA(P, "pool", wd[b][:, 0:nf, :], C.w_down[fi][:, c0:c1, :], [], [("wd", b)], ("wd", b))

    wload(0)
    norm(0)
    P.stage = "ffn.main"
    it = 0
    for gi, (c0, c1) in enumerate(groups):
        b = gi % 2
        nf = c1 - c0
        wload(gi + 1)
        for tt in range(NTT):
            sl = slice(tt * TT, (tt + 1) * TT)
            ab = it % 2
            it += 1
            if gi == 0:
                norm(tt + 1)
            for fc in range(nf):
                pg = fc % 2
                psG = C.psum[pg * 2]
                psU = C.psum[pg * 2 + 1]
                for k in range(NCH):
                    MM(P, psG, wg[b][:, k, fc * 128:(fc + 1) * 128], hT[:, k, sl], k == 0, k == NCH - 1,
                       [("wg", b), ("hT", tt)], [("ps", pg * 2)])
                for k in range(NCH):
                    MM(P, psU, wu[b][:, k, fc * 128:(fc + 1) * 128], hT[:, k, sl], k == 0, k == NCH - 1,
                       [("wu", b), ("hT", tt)], [("ps", pg * 2 + 1)])
                ACT(P, sg[pg], psG, AF.Silu, [("ps", pg * 2)], [("sg", pg)])
                TT_(P, "dve", act[ab][:, fc, :], psU, sg[pg], ALU.mult, [("ps", pg * 2 + 1), ("sg", pg)], [("act", ab, fc)])
            for dc in range(NCH):
                pb = 4 + dc % 3
                psY = C.psum[pb]
                for fc in range(nf):
                    MM(P, psY, wd[b][:, fc, dc * 128:(dc + 1) * 128], act[ab][:, fc, :], fc == 0, fc == nf - 1,
                       [("wd", b), ("act", ab, fc)], [("ps", pb)])
                STT(P, xs[:, dc, sl], psY, 0.5, xs[:, dc, sl], ALU.mult, ALU.add, [("ps", pb), ("x", tt)], [("x", tt)])
            if gi == len(groups) - 1 and store and not final:
                store_x_tile(C, xs, s, tt)
    if final:
        P.stage = "final"
        gf = C.gains[:, C.G_FINAL * NCH:(C.G_FINAL + 1) * NCH]
        for tt in range(NTT):
            sl = slice(tt * TT, (tt + 1) * TT)
            rms_to(C, lambda c: xs[:, c, sl], NCH, gf, lambda c: xs[:, c, sl],
                   lambda c: ("x", tt), lambda c: ("x", tt), D)
            store_x_tile(C, xs, s, tt, dst=C.out[s], dtag="out")
    A.pop()


def phase_final(C, s):
    P = C.P
    A = C.arena
    P.barrier()
    A.push()
    xs = A.alloc([NCH, S], F32)
    C.sq = [A.alloc([TT], F32) for _ in range(2)]
    C.rstd = A.alloc([TT], F32)
    P.stage = "final"
    load_x(C, xs, s)
    g = C.gains[:, C.G_FINAL * NCH:(C.G_FINAL + 1) * NCH]
    for tt in range(NTT):
        sl = slice(tt * TT, (tt + 1) * TT)
        rms_to(C, lambda c: xs[:, c, sl], NCH, g, lambda c: xs[:, c, sl],
               lambda c: ("x", tt), lambda c: ("x", tt), D)
    store_x(C, xs, s, dst=C.out[s], dtag="out")
    A.pop()


def phase_dump(C, s):
    A = C.arena
    C.P.barrier()
    A.push()
    xs = A.alloc([NCH, S], F32)
    load_x(C, xs, s)
    store_x(C, xs, s, dst=C.out[s], dtag="out")
    A.pop()

CW = 31
HALO = CW - 1
MLA_H = 8
ATT_SCALE = (128 + 64) ** -0.5
PI = float(np.pi)


def rope_tables(C, s, CCt, SSt, tmpA, tmpB, tmpI):
    P = C.P
    p64 = slice(0, 64)
    DMA(P, "sp", tmpI[p64, :], C.pos[s].partition_broadcast(64), [], [("ropeI",)], ("rope",))
    P.op("dve", "tensor_copy", dict(out=tmpA[p64, :], in_=tmpI[p64, :]), reads=[("ropeI",)], writes=[("ropeA",)])
    ang = tmpA
    TS(P, "dve", ang[p64, :], tmpA[p64, :], C.ropec[p64, 0:1], None, ALU.mult, None, [("ropeA",), ("const",)], [("ropeA",)])
    for tbl, col, tag in ((CCt, 1, "CC"), (SSt, 2, "SS")):
        t = tbl
        TS(P, "dve", t[p64, :], ang[p64, :], C.ropec[p64, col:col + 1], None, ALU.add, None, [("ropeA",), ("const",)], [(tag,)])
        TS(P, "dve", tmpB[p64, :], t[p64, :], 1.0 / (2 * PI), None, ALU.mult, None, [(tag,)], [("ropeB",)])
        P.op("dve", "tensor_copy", dict(out=tmpI[p64, :], in_=tmpB[p64, :]), reads=[("ropeB",)], writes=[("ropeI",)])
        P.op("dve", "tensor_copy", dict(out=tmpB[p64, :], in_=tmpI[p64, :]), reads=[("ropeI",)], writes=[("ropeB",)])
        STT(P, t[p64, :], tmpB[p64, :], -6.28125, t[p64, :], ALU.mult, ALU.add, [("ropeB",), (tag,)], [(tag,)])
        STT(P, t[p64, :], tmpB[p64, :], -(2 * PI - 6.28125), t[p64, :], ALU.mult, ALU.add, [("ropeB",), (tag,)], [(tag,)])
        TS(P, "dve", t[p64, :], t[p64, :], -PI, PI, ALU.max, ALU.min, [(tag,)], [(tag,)])
        ACT(P, tbl[p64, :], t[p64, :], AF.Sin, [(tag,)], [(tag,)])


def phase_mix0(C, s):
    P = C.P
    A = C.arena
    P.barrier()
    A.push()
    hT = A.alloc([NCH, S], BF16)
    C.sq = [A.alloc([TT], F32) for _ in range(2)]
    C.rstd = A.alloc([TT], F32)
    p64 = slice(0, 64)
    src, stag = C.xsrc[s]
    g = C.gains[:, 4 * NCH:5 * NCH]

    P.stage = 'm0.A3'
    A.push()
    xt = A.alloc([NCH, TT], F32)
    wa = [A.alloc([NCH, 128], BF16) for _ in range(2)]
    wgt = [A.alloc([NCH, 128], BF16) for _ in range(2)]
    woc = A.alloc([NCH, D], BF16)
    glu = A.alloc([NCH, HALO + TT], BF16)
    dg = [A.alloc([CW, 128], BF16) for _ in range(2)]
    hc = [A.alloc([NCH, TT], F32) for _ in range(2)]
    co = A.alloc([NCH, TT], BF16)
    sig = [A.alloc([TT], F32) for _ in range(2)]
    xt1 = A.alloc([NCH, TT], F32)
    mean = A.alloc([TT], F32)
    nmr = A.alloc([TT], F32)
    var = A.alloc([TT], F32)
    DMA(P, "pool", woc, C.m0_wout[:, 0:NCH, :], [], [("woc",)], ("woc",))
    cv = C.m0_cvec
    NIT = NTT * NCH

    def xload(tt):
        if tt >= NTT:
            return
        sl = slice(tt * TT, (tt + 1) * TT)
        DMA(P, "sp", xt, src[:, :, sl], [(stag, s, tt)], [("xt",)], ("xt",))

    def norm(tt):
        if tt >= NTT:
            return
        sl = slice(tt * TT, (tt + 1) * TT)
        P.stage = 'm0.A1'
        rms_to(C, lambda c: xt[:, c, :], NCH, g, lambda c: hT[:, c, sl],
               lambda c: ("xt",), lambda c: ("hT", tt), D)
        xload(tt + 1)
        P.stage = 'm0.A3'

    def wload(i):
        if i >= NIT:
            return
        c, b = i % NCH, i % 2
        DMA(P, "pool", wa[b], C.m0_win_conv[:, c], [], [("wa", b)], ("wa", b))
        DMA(P, "pool", wgt[b], C.m0_win_conv[:, NCH + c], [], [("wgt", b)], ("wgt", b))

    def ag(i):
        if i >= NIT:
            return
        tt, c, b = i // NCH, i % NCH, i % 2
        sl = slice(tt * TT, (tt + 1) * TT)
        for k in range(NCH):
            MM(P, C.psum[2 * b], wa[b][:, k, :], hT[:, k, sl], k == 0, k == NCH - 1, [("wa", b), ("hT", tt)], [("ps", 2 * b)])
        for k in range(NCH):
            MM(P, C.psum[2 * b + 1], wgt[b][:, k, :], hT[:, k, sl], k == 0, k == NCH - 1, [("wgt", b), ("hT", tt)], [("ps", 2 * b + 1)])

    def diag(i):
        if i >= NIT:
            return
        c, b = i % NCH, i % 2
        for j in range(CW):
            if j % 3 == 2:
                TS(P, "dve", dg[b][:, j, :], C.rw_id, cv[:, c, j:j + 1], None, ALU.mult, None, [("const",)], [("dg", b, j)])
            else:
                ACT(P, dg[b][:, j, :], C.rw_id, AF.Copy, [("const",)], [("dg", b, j)], scale=cv[:, c, j:j + 1])

    def ln_piece(tt, k):
        par = tt % 2
        h_ = hc[par]
        sl = slice(tt * TT, (tt + 1) * TT)
        if k == 0:
            DMA(P, "sp", xt1, src[:, :, sl], [(stag, s, tt)], [("xt1",)], ("xt1",))
            for c2 in range(NCH):
                MM(P, C.psum[7], C.ones_f32, h_[:, c2, :], c2 == 0, c2 == NCH - 1, [("hc", par, c2), ("const",)], [("ps", 7)])
            for c2 in range(NCH):
                sq = C.sq[c2 % 2]
                ACT(P, sq, h_[:, c2, :], AF.Square, [("hc", par, c2)], [("sq", c2 % 2)])
                MM(P, C.psum[6], C.ones_f32, sq, c2 == 0, c2 == NCH - 1, [("sq", c2 % 2), ("const",)], [("ps", 6)])
        elif k == 1:
            TS(P, "dve", mean, C.psum[7], 1.0 / D, None, ALU.mult, None, [("ps", 7)], [("mean",)])
            TT_(P, "dve", var, mean, mean, ALU.mult, [("mean",)], [("var",)])
            STT(P, var, C.psum[6], 1.0 / D, var, ALU.mult, ALU.subtract, [("ps", 6), ("var",)], [("var",)])
            ACT(P, var, var, AF.Ln, [("var",), ("const",)], [("var",)], bias=C.eps_ap(1e-5), scale=1.0)
            ACT(P, var, var, AF.Exp, [("var",)], [("var",)], scale=-0.5)
            STT(P, nmr, mean, -1.0, var, ALU.mult, ALU.mult, [("mean",), ("var",)], [("nmr",)])
        elif k in (2, 3):
            for c2 in range((k - 2) * 4, (k - 1) * 4):
                TT_(P, "dve", h_[:, c2, :], h_[:, c2, :], var, ALU.mult, [("hc", par, c2), ("var",)], [("hc", par, c2)])
                TT_(P, "dve", h_[:, c2, :], h_[:, c2, :], nmr, ALU.add, [("hc", par, c2), ("nmr",)], [("hc", par, c2)])
                ACT(P, co[:, c2, :], h_[:, c2, :], AF.Silu, [("hc", par, c2), ("const",)], [("co", c2)],
                    bias=cv[:, c2, 33:34], scale=cv[:, c2, 32:33])
        else:
            for dc in range((k - 4) * 4, (k - 3) * 4):
                pb = 6 + dc % 2
                for c2 in range(NCH):
                    MM(P, C.psum[pb], woc[:, c2, dc * 128:(dc + 1) * 128], co[:, c2, :], c2 == 0, c2 == NCH - 1,
                       [("woc",), ("co", c2)], [("ps", pb)])
                TT_(P, "dve", xt1[:, dc, :], C.psum[pb], xt1[:, dc, :], ALU.add, [("ps", pb), ("xt1",)], [("xt1",)])
            if k == 5:
                DMA(P, "sp", C.xres[s][:, :, sl], xt1, [("xt1",)], [("xres", s, tt)], ("xt1s",))

    xload(0)
    norm(0)
    wload(0)
    wload(1)
    ag(0)
    diag(0)
    for i in range(NIT):
        tt, c, b = i // NCH, i % NCH, i % 2
        par = tt % 2
        if c == 2:
            norm(tt + 1)
        ag(i + 1)
        wload(i + 2)
        if tt == 0:
            P.op("dve", "memset", dict(ap=glu[:, c, 0:HALO], constant=0.0), writes=[("glu", c)])
        else:
            P.op("dve", "tensor_copy", dict(out=glu[:, c, 0:HALO], in_=glu[:, c, TT:TT + HALO]),
                 reads=[("glu", c)], writes=[("glu", c)])
        ACT(P, sig[b], C.psum[2 * b + 1], AF.Sigmoid, [("ps", 2 * b + 1)], [("sig", b)])
        TT_(P, "dve", glu[:, c, HALO:HALO + TT], C.psum[2 * b], sig[b], ALU.mult, [("ps", 2 * b), ("sig", b)], [("glu", c)])
        diag(i + 1)
        psC = C.psum[4 + b]
        for j in range(CW):
            MM(P, psC, dg[b][:, j, :], glu[:, c, j:j + TT], j == 0, j == CW - 1, [("dg", b, j), ("glu", c)], [("ps", 4 + b)])
        ACT(P, hc[par][:, c, :], psC, AF.Identity, [("ps", 4 + b)], [("hc", par, c)], bias=cv[:, c, 31:32])
        if tt > 0 and c < 6:
            ln_piece(tt - 1, c)
    for k in range(6):
        ln_piece(NTT - 1, k)
    C.xsrc[s] = (C.xres[s], "xres")
    src, stag = C.xsrc[s]
    P.barrier()
    A.pop()

    P.stage = 'm0.A2'
    qn = A.alloc([4, S], BF16)
    kvn = A.alloc([2, S], BF16)
    kpe = A.alloc([S], BF16)
    CCt = A.alloc([S], F32)
    SSt = A.alloc([S], F32)
    A.push()
    tmpA = A.alloc([S], F32)
    tmpB = A.alloc([S], F32)
    tmpI = A.alloc([S], I32)
    wl = A.alloc([NCH, 896], BF16)
    t1 = A.alloc([TT], F32)
    t2 = A.alloc([TT], F32)
    DMA(P, "pool", wl, C.m0_win_lat, [], [("wl",)], ("wl",))
    rope_tables(C, s, CCt, SSt, tmpA, tmpB, tmpI)
    for tt in range(NTT):
        sl = slice(tt * TT, (tt + 1) * TT)
        for m in range(4):
            for k in range(NCH):
                MM(P, C.psum[m], wl[:, k, m * 128:(m + 1) * 128], hT[:, k, sl], k == 0, k == NCH - 1,
                   [("wl",), ("hT", tt)], [("ps", m)])
        rms_to(C, lambda m: C.psum[m], 4, C.m0_vec[:, 0:4], lambda m: qn[:, m, sl],
               lambda m: ("ps", m), lambda m: ("qn", tt), 512)
        for m in range(2):
            for k in range(NCH):
                MM(P, C.psum[4 + m], wl[:, k, 512 + m * 128:512 + (m + 1) * 128], hT[:, k, sl], k == 0, k == NCH - 1,
                   [("wl",), ("hT", tt)], [("ps", 4 + m)])
        rms_to(C, lambda m: C.psum[4 + m], 2, C.m0_vec[:, 4:6], lambda m: kvn[:, m, sl],
               lambda m: ("ps", 4 + m), lambda m: ("kvn", tt), 256)
        for k in range(NCH):
            MM(P, C.psum[6][p64, :], wl[:, k, 768:832], hT[:, k, sl], k == 0, k == NCH - 1, [("wl",), ("hT", tt)], [("ps", 6)])
        for k in range(NCH):
            MM(P, C.psum[0][p64, :], wl[:, k, 832:896], hT[:, k, sl], k == 0, k == NCH - 1, [("wl",), ("hT", tt)], [("ps", 0)])
        TT_(P, "dve", t1[p64, :], C.psum[6][p64, :], CCt[p64, sl], ALU.mult, [("ps", 6), ("CC",)], [("t1",)])
        TT_(P, "dve", t2[p64, :], C.psum[0][p64, :], SSt[p64, sl], ALU.mult, [("ps", 0), ("SS",)], [("t2",)])
        TT_(P, "dve", kpe[p64, sl], t1[p64, :], t2[p64, :], ALU.add, [("t1",), ("t2",)], [("kpe", tt)])
    P.barrier()
    A.pop()

    P.stage = 'm0.C'
    attnT = hT
    wo = A.alloc([NCH, D], BF16)
    DMA(P, "pool", wo, C.m0_wout[:, NCH:2 * NCH, :], [], [("wo",)], ("wo",))
    A.push()
    wq = [A.alloc([4, 256], BF16) for _ in range(2)]
    wkv = [A.alloc([2, 256], BF16) for _ in range(2)]
    qno = A.alloc([S], BF16)
    qpe = A.alloc([S], BF16)
    kno = A.alloc([S], BF16)
    vh = A.alloc([S // 128, 128], BF16)
    Eb = [A.alloc([TT], BF16) for _ in range(3)]
    rden = A.alloc([TT], F32)
    t1 = A.alloc([TT], F32)
    t2 = A.alloc([TT], F32)
    def hload(h):
        if h >= MLA_H:
            return
        DMA(P, "pool", wq[h % 2], C.m0_wuq[:, h], [], [("wq", h % 2)], ("wq", h % 2))
        DMA(P, "pool", wkv[h % 2], C.m0_wukv[:, h], [], [("wkv", h % 2)], ("wkv", h % 2))

    hload(0)
    for h in range(MLA_H):
        b = h % 2
        P.stage = 'm0.Cprep'
        hload(h + 1)
        for tt in range(NTT):
            sl = slice(tt * TT, (tt + 1) * TT)
            for l in range(4):
                MM(P, C.psum[0], wq[b][:, l, 0:128], qn[:, l, sl], l == 0, l == 3, [("wq", b), ("qn", tt)], [("ps", 0)])
            ACT(P, qno[:, sl], C.psum[0], AF.Copy, [("ps", 0)], [("qno", tt)])
            for l in range(4):
                MM(P, C.psum[1][p64, :], wq[b][:, l, 128:192], qn[:, l, sl], l == 0, l == 3, [("wq", b), ("qn", tt)], [("ps", 1)])
            for l in range(4):
                MM(P, C.psum[2][p64, :], wq[b][:, l, 192:256], qn[:, l, sl], l == 0, l == 3, [("wq", b), ("qn", tt)], [("ps", 2)])
            TT_(P, "dve", t1[p64, :], C.psum[1][p64, :], CCt[p64, sl], ALU.mult, [("ps", 1), ("CC",)], [("t1",)])
            TT_(P, "dve", t2[p64, :], C.psum[2][p64, :], SSt[p64, sl], ALU.mult, [("ps", 2), ("SS",)], [("t2",)])
            TT_(P, "dve", qpe[p64, sl], t1[p64, :], t2[p64, :], ALU.add, [("t1",), ("t2",)], [("qpe", tt)])
            for l in range(2):
                MM(P, C.psum[3], wkv[b][:, l, 0:128], kvn[:, l, sl], l == 0, l == 1, [("wkv", b), ("kvn", tt)], [("ps", 3)])
            ACT(P, kno[:, sl], C.psum[3], AF.Copy, [("ps", 3)], [("kno", tt)])
            for i in range(4):
                tsl = slice(tt * TT + i * 128, tt * TT + (i + 1) * 128)
                for l in range(2):
                    MM(P, C.psum[4][:, i * 128:(i + 1) * 128], kvn[:, l, tsl], wkv[b][:, l, 128:256], l == 0, l == 1,
                       [("wkv", b), ("kvn", tt)], [("ps", 4)])
            P.op("dve", "tensor_copy", dict(out=vh[:, tt * 4:(tt + 1) * 4, :], in_=C.psum[4].rearrange("p (a b) -> p a b", a=4)),
                 reads=[("ps", 4)], writes=[("vh", tt)])
        ecount = 0
        P.stage = 'm0.Cattn'
        for qt in range(NTT):
            nk = 4 * (qt + 1)
            psO = C.psum[3 + qt % 2]
            psD = C.psum[5 + qt % 2]
            otag, dtag = ("ps", 3 + qt % 2), ("ps", 5 + qt % 2)

            def s_stage(kt, e):
                j = kt - 4 * qt
                c0 = max(j, 0) * 128
                qsl = slice(qt * TT + c0, (qt + 1) * TT)
                ksl = slice(kt * 128, (kt + 1) * 128)
                psS = C.psum[e]
                MM(P, psS[:, c0:TT], kno[:, ksl], qno[:, qsl], True, False, [("kno", kt // 4), ("qno", qt)], [("ps", e)])
                MM(P, psS[:, c0:TT], kpe[p64, ksl], qpe[p64, qsl], False, True, [("kpe", kt // 4), ("qpe", qt)], [("ps", e)])
                ACT(P, Eb[e][:, c0:TT], psS[:, c0:TT], AF.Exp, [("ps", e)], [("E", e)], scale=ATT_SCALE)
                if j >= 0:
                    TT_(P, "dve", Eb[e][:, c0:c0 + 128], Eb[e][:, c0:c0 + 128], C.tri_bf, ALU.mult,
                        [("E", e), ("const",)], [("E", e)])

            def o_stage(kt, e):
                j = kt - 4 * qt
                c0 = max(j, 0) * 128
                MM(P, psO[:, c0:TT], vh[:, kt, :], Eb[e][:, c0:TT], kt == 0, kt == nk - 1, [("vh", kt // 4), ("E", e)], [otag])
                MM(P, psD[:, c0:TT], C.ones_bf, Eb[e][:, c0:TT], kt == 0, kt == nk - 1, [("const",), ("E", e)], [dtag])

            es = [(ecount + kt) % 3 for kt in range(nk)]
            ecount += nk
            s_stage(0, es[0])
            for kt in range(nk):
                if kt + 1 < nk:
                    s_stage(kt + 1, es[kt + 1])
                o_stage(kt, es[kt])
            sl = slice(qt * TT, (qt + 1) * TT)
            P.op("dve", "reciprocal", dict(out=rden, in_=psD), reads=[dtag], writes=[("rden",)])
            TT_(P, "dve", attnT[:, h, sl], psO, rden, ALU.mult, [otag, ("rden",)], [("attnT", h, qt)])
    P.barrier()
    A.pop()

    P.stage = 'm0.D'
    A.push()
    xt2 = [A.alloc([NCH, TT], F32) for _ in range(2)]
    def dload(tt):
        if tt >= NTT:
            return
        DMA(P, "sp", xt2[tt % 2], src[:, :, tt * TT:(tt + 1) * TT], [(stag, s, tt)], [("xt2", tt % 2)], ("xt2", tt % 2))

    dload(0)
    for tt in range(NTT):
        sl = slice(tt * TT, (tt + 1) * TT)
        b = tt % 2
        dload(tt + 1)
        for dc in range(NCH):
            pb = dc % 3
            for hh in range(MLA_H):
                MM(P, C.psum[pb], wo[:, hh, dc * 128:(dc + 1) * 128], attnT[:, hh, sl], hh == 0, hh == MLA_H - 1,
                   [("wo",), ("attnT", hh, tt)], [("ps", pb)])
            TT_(P, "dve", xt2[b][:, dc, :], C.psum[pb], xt2[b][:, dc, :], ALU.add, [("ps", pb), ("xt2", b)], [("xt2", b)])
        DMA(P, "sp", C.xres[s][:, :, sl], xt2[b], [("xt2", b)], [("xres", s, tt)], ("xt2s", b))
    A.pop()
    A.pop()

TR = 256
NTR = S // TR
RH = 16
NEG_EXP_HALF = -float(np.exp(-0.5))
HORDER = [0, 2, 4, 6, 8, 10, 12, 14, 1, 3, 5, 7, 9, 11, 13, 15]
HPOS = {h: i for i, h in enumerate(HORDER)}
NVEC = 14


def rwkv_decl(C, dt):
    C.rw_w4 = dt("rw_w4", [4, 128, NCH, D])
    C.rw_w4b = dt("rw_w4b", [4, 128, NCH, D], BF16, kind="Internal")
    C.rw_l1_d = dt("rw_l1", [128, NCH, 256])
    C.rw_l2_d = dt("rw_l2", [128, 3, D])
    C.rw_vec_d = dt("rw_vec", [128, NCH, NVEC])
    C.rw_cm_d = dt("rw_cm", [128, 5, 512])
    C.rw_bo_d = dt("rw_bo", [128, 128])
    C.rw_sm_d = dt("rw_sm", [128, TR])
    C.rw_id_d = dt("rw_id", [128, 128])


def rwkv_consts(C, CA):
    P = C.P
    C.rw_vec = CA.alloc([NCH, NVEC], F32)
    C.rw_cm = CA.alloc([5, 512], BF16)
    C.rw_bo = CA.alloc([128], F32)
    C.rw_sm = CA.alloc([TR], F32)
    C.rw_id = CA.alloc([128], BF16)
    DMA(P, "sp", C.rw_vec, C.rw_vec_d, [], [("c", 20)], "c20")
    DMA(P, "pool", C.rw_cm, C.rw_cm_d, [], [("c", 21)], "c21")
    DMA(P, "sp", C.rw_bo, C.rw_bo_d, [], [("c", 22)], "c22")
    DMA(P, "sp", C.rw_sm, C.rw_sm_d, [], [("c", 23)], "c23")
    DMA(P, "pool", C.rw_id, C.rw_id_d, [], [("c", 24)], "c24")
    A = C.arena
    A.push()
    tmpw = [A.alloc([NCH, D], BF16) for _ in range(2)]
    for i in range(4):
        DMA(P, "pool", tmpw[i % 2], C.rw_w4[i], [], [("tmpw", i % 2)], ("tmpw", i % 2))
        DMA(P, "sp", C.rw_w4b[i], tmpw[i % 2], [("tmpw", i % 2)], [("w4b", i)], ("tmpws", i % 2))
    A.pop()
    TS(P, "dve", C.rw_vec[:, :, 13:14], C.rw_vec[:, :, 7:8], -1.0, 1.0, ALU.mult, ALU.add, [("c", 20)], [("c", 20)])


def rwkv_prep(inp, sh):
    f32 = np.float32
    g = lambda n: np.asarray(inp[n], f32)[0]
    sh["rw_w4"] = np.stack([_fm(g("w_r"), NCH), _fm(g("w_k"), NCH), _fm(g("w_v"), NCH), _fm(g("w_o"), NCH)], 0)
    sh["rw_l1"] = _fm(np.concatenate([g("w1"), g("a1"), g("g1")], axis=1), NCH)
    l2 = np.zeros((128, 3, D), f32)
    l2[0:64, 0] = g("w2")
    l2[0:64, 1] = g("a2")
    l2[:, 2] = g("g2")
    sh["rw_l2"] = l2
    vec = np.zeros((128, NCH, NVEC), f32)
    mu = g("time_mu")
    for i in range(6):
        vec[:, :, i] = _vec(mu[i], NCH)
    for j, n in enumerate(["k_k", "k_a", "w0", "a0"]):
        vec[:, :, 6 + j] = _vec(g(n), NCH)
    vec[:, :, 10] = _vec(g("r_k").reshape(D), NCH)
    vec[:, :, 11] = _vec(g("ln_x_g"), NCH)
    vec[:, :, 12] = _vec(g("ln_x_b"), NCH)
    sh["rw_vec"] = vec
    one = np.ones((128, 128), f32)
    sl = np.tril(one, -1)
    su = np.triu(one, 1)
    ui = np.triu(one, 0)
    idn = np.eye(128, dtype=f32)
    bd = np.zeros((128, 128), f32)
    bd[0:64, 0:64] = 1
    bd[64:128, 64:128] = 1
    sh["rw_cm"] = np.ascontiguousarray(np.stack([np.tile(m, (1, 4)) for m in (sl, su, ui, idn, bd)], axis=1))
    sh["rw_bo"] = bd
    sm = np.ones((128, TR), f32)
    sm[:, 0::128] = 0
    sh["rw_sm"] = sm
    sh["rw_id"] = idn


def phase_rwkv(C, s):
    P = C.P
    A = C.arena
    P.barrier()
    A.push()
    vec = C.rw_vec
    SLm, SUm, UIm, ID4, BD4 = (C.rw_cm[:, i, :] for i in range(5))
    bank_ctr = [0]

    def nb():
        bank_ctr[0] = (bank_ctr[0] + 1) % 7
        return bank_ctr[0]

    Sbd32 = A.alloc([NCH, 128], F32)
    Sbd = A.alloc([NCH, 128], BF16)
    hprev = A.alloc([NCH, 1], F32)
    l1 = A.alloc([NCH, 256], BF16)
    l2 = A.alloc([3, D], BF16)
    wbuf = [A.alloc([NCH, D], BF16) for _ in range(2)]
    C.sq = [A.alloc([TR], F32) for _ in range(8)]
    C.rstd = A.alloc([TR], F32)
    AhT = A.alloc([NCH, TR], BF16)
    RT = A.alloc([NCH, TR], BF16)
    BT = A.alloc([NCH, TR], BF16)
    KT = A.alloc([NCH, TR], BF16)
    BpT = A.alloc([NCH, TR], BF16)
    KpT = A.alloc([NCH, TR], BF16)
    vb = A.alloc([NCH, TR], BF16)
    gT = A.alloc([NCH, TR], BF16)
    bonus = A.alloc([NCH, TR], F32)
    WLt = A.alloc([NCH, 2], F32)

    P.op("dve", "memset", dict(ap=Sbd32, constant=0.0), writes=[("Sbd32",)])
    P.op("dve", "memset", dict(ap=Sbd, constant=0.0), writes=[("Sbd",)])
    P.op("dve", "memset", dict(ap=hprev, constant=0.0), writes=[("hprev",)])
    DMA(P, "pool", l1, C.rw_l1_d, [], [("l1",)], ("l1",))
    DMA(P, "pool", l2, C.rw_l2_d, [], [("l2",)], ("l2",))
    g = C.gains[:, 5 * NCH:6 * NCH]
    wcnt = [0]

    def load_w(i):
        b = wcnt[0] % 2
        wcnt[0] += 1
        DMA(P, "sp", wbuf[b], C.rw_w4b[i], [("w4b", i)], [("wbuf", b)], ("wbuf", b))
        return b

    for ti in range(NTR):
        tsl = slice(ti * TR, (ti + 1) * TR)
        src, stag = C.xsrc[s]
        rtag = (stag, s, ti // 2)
        P.stage = 'rw.P#%d' % ti
        A.push()
        xh = A.alloc([NCH, TR + 1], F32)
        dd = A.alloc([NCH, TR], F32)
        xi2 = A.alloc([2, NCH, TR], BF16)
        xi = [xi2[:, 0], xi2[:, 1]]
        rr = A.alloc([NCH, TR], F32)
        kx = A.alloc([NCH, TR], F32)
        vv = A.alloc([NCH, TR], F32)
        lw = A.alloc([NCH, TR], F32)
        aa = A.alloc([NCH, TR], F32)
        kk = A.alloc([NCH, TR], F32)
        lt = A.alloc([TR], BF16)
        e_x1 = A.alloc([NCH, TR], F32)
        lastC = A.alloc([NCH, 2], F32)

        DMA(P, "sp", xh[:, :, 1:TR + 1], src[:, :, tsl], [rtag], [("xh",)], ("xh",))
        P.op("pool", "tensor_copy", dict(out=xh[:, :, 0:1], in_=hprev), reads=[("hprev",)], writes=[("xh0",)])
        rms_to(C, lambda c: xh[:, c, 1:TR + 1], NCH, g, lambda c: xh[:, c, 1:TR + 1],
               lambda c: ("xh",), lambda c: ("xh",), D)
        P.op("pool", "tensor_copy", dict(out=hprev, in_=xh[:, :, TR:TR + 1]), reads=[("xh",)], writes=[("hprev",)])
        TT_(P, "dve", dd, xh[:, :, 0:TR], xh[:, :, 1:TR + 1], ALU.subtract, [("xh",), ("xh0",)], [("dd",)])
        mixcnt = [0]

        def mix(i):
            b = mixcnt[0] % 2
            mixcnt[0] += 1
            for c in range(NCH):
                STT(P, xi[b][:, c, :], dd[:, c, :], vec[:, c, i:i + 1], xh[:, c, 1:TR + 1], ALU.mult, ALU.add,
                    [("dd",), ("xh",)], [("xi", b)])
            return b

        def big_proj(mi, wi, dst, dtag):
            b = mix(mi)
            wb = load_w(wi)
            for oc in range(NCH):
                pb = nb()
                for k in range(NCH):
                    MM(P, C.psum[pb][:, 0:TR], wbuf[wb][:, k, oc * 128:(oc + 1) * 128], xi[b][:, k, :], k == 0, k == NCH - 1,
                       [("wbuf", wb), ("xi", b)], [("ps", pb)])
                ACT(P, dst[:, oc, :], C.psum[pb][:, 0:TR], AF.Copy, [("ps", pb)], [(dtag,)])

        big_proj(0, 0, rr, "rr")
        big_proj(2, 1, kx, "kx")
        big_proj(3, 2, vv, "vv")

        def lora(mi, c0, c1, func, l2i, dst, dtag, fin, bias_col):
            b = mix(mi)
            nl = c1 - c0
            pb = nb()
            for k in range(NCH):
                MM(P, C.psum[pb][0:nl, 0:TR], l1[:, k, c0:c1], xi[b][:, k, :], k == 0, k == NCH - 1, [("l1",), ("xi", b)], [("ps", pb)])
            ACT(P, lt[0:nl, :], C.psum[pb][0:nl, 0:TR], func, [("ps", pb)], [("lt",)])
            for oc in range(NCH):
                pb = nb()
                MM(P, C.psum[pb][:, 0:TR], l2[0:nl, l2i, oc * 128:(oc + 1) * 128], lt[0:nl, :], True, True, [("l2",), ("lt",)], [("ps", pb)])
                if bias_col is None:
                    ACT(P, dst[:, oc, :], C.psum[pb][:, 0:TR], fin, [("ps", pb)], [(dtag,)])
                else:
                    ACT(P, dst[:, oc, :], C.psum[pb][:, 0:TR], fin, [("ps", pb)], [(dtag,)], bias=vec[:, oc, bias_col:bias_col + 1])

        lora(1, 0, 64, AF.Tanh, 0, lw, "lw", AF.Sigmoid, 8)
        lora(4, 64, 128, AF.Copy, 1, aa, "aa", AF.Sigmoid, 9)
        lora(5, 128, 256, AF.Sigmoid, 2, gT, "gT", AF.Copy, None)

        P.stage = 'rw.E#%d' % ti
        P.barrier()
        e_cw = dd
        e_x0 = xh[:, :, 0:TR]
        e_n = xi2.rearrange("p a c t -> p (a c t)").bitcast(F32).rearrange("p (c t) -> p c t", c=NCH)
        f2 = lambda a: a
        for c in range(NCH):
            ACT(P, kk[:, c, :], kx[:, c, :], AF.Copy, [("kx",)], [("kk",)], scale=vec[:, c, 6:7])
        for c in range(NCH):
            ACT(P, e_x0[:, c, :], kk[:, c, :], AF.Square, [("kk",)], [("e_x0",)])
        pbs = [nb() for _ in range(4)]
        for c in range(NCH):
            MM(P, C.psum[pbs[c // 2]][:, (c % 2) * TR:(c % 2 + 1) * TR], C.rw_bo, e_x0[:, c, :], True, True, [("e_x0",)], [("ps", pbs[c // 2])])
        for i in range(4):
            ACT(P, e_n[:, 2 * i:2 * i + 2, :], C.psum[pbs[i]].rearrange("p (a t) -> p a t", a=2), AF.Ln, [("ps", pbs[i])], [("e_n",)],
                bias=C.eps_ap(1e-24), scale=1.0)
        for c in range(NCH):
            P.op("dve", "tensor_tensor_scan", dict(out=e_cw[:, c, :], data0=C.rw_sm, data1=lw[:, c, :], initial=0.0, op0=ALU.mult, op1=ALU.add),
                 reads=[("lw",)], writes=[("e_cw",)])
        TS(P, "dve", lastC, e_cw[:, :, 127::128], NEG_EXP_HALF, None, ALU.mult, None, [("e_cw",)], [("lastC",)])
        TT_(P, "dve", f2(lw), f2(e_cw), f2(lw), ALU.subtract, [("e_cw",), ("lw",)], [("lw",)])
        for c in range(NCH):
            TS(P, "dve", e_x1[:, c, :], aa[:, c, :], vec[:, c, 7:8], vec[:, c, 13:14], ALU.mult, ALU.add, [("aa",)], [("e_x1",)])
        TT_(P, "dve", f2(kx), f2(kx), f2(e_x1), ALU.mult, [("kx",), ("e_x1",)], [("kx",)])
        ACT(P, f2(e_n), f2(e_n), AF.Exp, [("e_n",)], [("e_n",)], scale=-0.5)
        TT_(P, "dve", f2(kk), f2(kk), f2(e_n), ALU.mult, [("kk",), ("e_n",)], [("kk",)])
        TT_(P, "dve", f2(aa), f2(aa), f2(kk), ALU.mult, [("aa",), ("kk",)], [("aa",)])
        for c in range(NCH):
            STT(P, e_x1[:, c, :], rr[:, c, :], vec[:, c, 10:11], kx[:, c, :], ALU.mult, ALU.mult, [("rr",), ("kx",), ("e_x1",)], [("e_x1",)])
        pbs = [nb() for _ in range(4)]
        for c in range(NCH):
            MM(P, C.psum[pbs[c // 2]][:, (c % 2) * TR:(c % 2 + 1) * TR], C.rw_bo, e_x1[:, c, :], True, True, [("e_x1",)], [("ps", pbs[c // 2])])
        for i in range(4):
            TT_(P, "dve", bonus[:, 2 * i:2 * i + 2, :], C.psum[pbs[i]].rearrange("p (a t) -> p a t", a=2), vv[:, 2 * i:2 * i + 2, :], ALU.mult,
                [("ps", pbs[i]), ("vv",)], [("bonus",)])
        ACT(P, f2(vb), f2(vv), AF.Copy, [("vv",)], [("vb",)])
        ACT(P, f2(e_x0), f2(lw), AF.Exp, [("lw",)], [("e_x0",)], scale=NEG_EXP_HALF)
        STT(P, f2(AhT), f2(kk), -1.0, f2(e_x0), ALU.mult, ALU.mult, [("kk",), ("e_x0",)], [("AhT",)])
        ACT(P, f2(e_x1), f2(e_cw), AF.Exp, [("e_cw",), ("e_x1",)], [("e_x1",)], scale=NEG_EXP_HALF)
        TT_(P, "dve", f2(RT), f2(rr), f2(e_x1), ALU.mult, [("rr",), ("e_x1",)], [("RT",)])
        ACT(P, f2(e_x0), f2(e_cw), AF.Exp, [("e_cw",), ("e_x0",)], [("e_x0",)], scale=-NEG_EXP_HALF)
        TT_(P, "dve", f2(BT), f2(aa), f2(e_x0), ALU.mult, [("aa",), ("e_x0",)], [("BT",)])
        TT_(P, "dve", f2(KT), f2(kx), f2(e_x0), ALU.mult, [("kx",), ("e_x0",)], [("KT",)])
        for c in range(NCH):
            for q in range(2):
                qs = slice(q * 128, (q + 1) * 128)
                ACT(P, e_x1[:, c, qs], e_cw[:, c, qs], AF.Exp, [("e_cw",), ("lastC",), ("e_x1",)], [("e_x1",)],
                    bias=lastC[:, c, q:q + 1], scale=-NEG_EXP_HALF)
        ACT(P, f2(WLt), f2(lastC), AF.Exp, [("lastC",)], [("WLt",)])
        TT_(P, "dve", f2(BpT), f2(aa), f2(e_x1), ALU.mult, [("aa",), ("e_x1",)], [("BpT",)])
        TT_(P, "dve", f2(KpT), f2(kx), f2(e_x1), ALU.mult, [("kx",), ("e_x1",)], [("KpT",)])
        P.barrier()
        A.pop()

        A.push()
        Bptm = A.alloc([D], BF16)
        Kptm = A.alloc([D], BF16)
        Vtm = A.alloc([D], BF16)
        Mm = A.alloc([RH * 128], BF16)
        Mt = A.alloc([RH * 128], BF16)
        NrbT = A.alloc([RH * 128], BF16)
        NrkT = A.alloc([RH * 128], BF16)
        MakT = A.alloc([RH * 128], BF16)
        XT = A.alloc([RH * 128], BF16)
        Tpp = [[A.alloc([512], BF16) for _ in range(2)] for _ in range(4)]
        Ttpp = [[A.alloc([512], BF16) for _ in range(2)] for _ in range(4)]
        Zb = A.alloc([D], BF16)
        Ub = A.alloc([D], BF16)
        tmpm = A.alloc([D], F32)
        Ytm = A.alloc([2, D], F32)
        ynb = A.alloc([2, D], BF16)
        ysq = A.alloc([2, D], F32)
        gst = A.alloc([6, 2 * RH], F32)
        xt = A.alloc([NCH, TR], F32)
        yg = A.alloc([NCH, TR], BF16)
        ytmp = A.alloc([NCH, TR], F32)
        wb = load_w(3)
        DMA(P, "sp", xt, src[:, :, tsl], [rtag], [("xt",)], ("xt",))

        for q in range(2):
            qs = slice(q * 128, (q + 1) * 128)
            P.stage = 'rw.T#%d' % ti
            for arr, dst, rtag_, wtag in ((BpT, Bptm, "BpT", "Bptm"), (KpT, Kptm, "KpT", "Kptm"), (vb, Vtm, "vb", "Vtm")):
                pb = nb()
                psb = C.psum[pb].bitcast(BF16)
                for c in range(NCH):
                    P.op("pe", "transpose", dict(out=psb[:, c * 128:(c + 1) * 128], in_=arr[:, c, qs], identity=C.rw_id),
                         reads=[(rtag_,)], writes=[("ps", pb)])
                ACT(P, dst, psb, AF.Copy, [("ps", pb)], [(wtag,)])
            P.stage = 'rw.N#%d' % ti
            for gi in range(4):
                heads = HORDER[gi * 4:(gi + 1) * 4]
                gsl = slice(gi * 512, (gi + 1) * 512)
                specs = ((Mm, AhT, BT, SLm, "Mm"), (Mt, BT, AhT, SUm, "Mt"), (NrbT, BT, RT, UIm, "NrbT"),
                         (NrkT, KT, RT, UIm, "NrkT"), (MakT, KT, AhT, SUm, "MakT"))
                for dst, la, ra, mask, tg in specs:
                    pb = nb()
                    for i, h in enumerate(heads):
                        p_, rb = h // 2, 64 * (h % 2)
                        MM(P, C.psum[pb][:, i * 128:(i + 1) * 128], la[rb:rb + 64, p_, qs], ra[rb:rb + 64, p_, qs], True, True,
                           [("AhT",), ("BT",), ("RT",), ("KT",)], [("ps", pb)])
                    TT_(P, "dve", dst[:, gsl], C.psum[pb], mask, ALU.mult, [("ps", pb)], [(tg, gi)])
            P.stage = 'rw.Neu#%d' % ti
            cur = [0, 0, 0, 0]
            for gi in range(4):
                gsl = slice(gi * 512, (gi + 1) * 512)
                TT_(P, "dve", XT[:, gsl], Mt[:, gsl], ID4, ALU.add, [("Mt", gi)], [("XT", gi)])
            for step in range(6):
                for gi in range(4):
                    gsl = slice(gi * 512, (gi + 1) * 512)
                    if step == 0:
                        Tc, Ttc, tcr = Mm[:, gsl], Mt[:, gsl], [("Mm", gi), ("Mt", gi)]
                    else:
                        Tc, Ttc = Tpp[gi][cur[gi]], Ttpp[gi][cur[gi]]
                        tcr = [("Tpp", gi, cur[gi]), ("Ttpp", gi, cur[gi])]
                    nxt = 1 - cur[gi] if step > 0 else 0
                    pb = nb()
                    for i in range(4):
                        bs = slice(i * 128, (i + 1) * 128)
                        MM(P, C.psum[pb][:, bs], Ttc[:, bs], Tc[:, bs], True, True, tcr, [("ps", pb)])
                    ACT(P, Tpp[gi][nxt], C.psum[pb], AF.Copy, [("ps", pb)], [("Tpp", gi, nxt)])
                    if step < 5:
                        pb2 = nb()
                        for i in range(4):
                            bs = slice(i * 128, (i + 1) * 128)
                            MM(P, C.psum[pb2][:, bs], Tc[:, bs], Ttc[:, bs], True, True, tcr, [("ps", pb2)])
                        ACT(P, Ttpp[gi][nxt], C.psum[pb2], AF.Copy, [("ps", pb2)], [("Ttpp", gi, nxt)])
                    cur[gi] = nxt
                for gi in range(4):
                    gsl = slice(gi * 512, (gi + 1) * 512)
                    Tn = Tpp[gi][cur[gi]]
                    pb = nb()
                    for i in range(4):
                        bs = slice(i * 128, (i + 1) * 128)
                        MM(P, C.psum[pb][:, bs], Tn[:, bs], XT[:, gi * 512 + i * 128:gi * 512 + (i + 1) * 128], True, True,
                           [("Tpp", gi, cur[gi]), ("XT", gi)], [("ps", pb)])
                    TT_(P, "dve", XT[:, gsl], C.psum[pb], XT[:, gsl], ALU.add, [("ps", pb), ("XT", gi)], [("XT", gi)])

            def hm(arr, h):
                pos = HPOS[h]
                return arr[:, pos * 128:(pos + 1) * 128]

            def htag(name, h):
                return (name, HPOS[h] // 4)

            P.stage = 'rw.S#%d' % ti
            zb = [nb(), nb()]
            for p_ in range(NCH):
                bk = C.psum[zb[p_ // 4]]
                co = (p_ % 4) * 128
                MM(P, bk[:, co:co + 128], AhT[:, p_, qs], Sbd[:, p_, :], True, False, [("AhT",), ("Sbd",)], [("ps", zb[p_ // 4])])
                for hh in range(2):
                    h = 2 * p_ + hh
                    MM(P, bk[:, co + hh * 64:co + (hh + 1) * 64], hm(MakT, h), Vtm[:, h * 64:(h + 1) * 64], False, hh == 1,
                       [htag("MakT", h), ("Vtm",)], [("ps", zb[p_ // 4])])
            ACT(P, Zb[:, 0:512], C.psum[zb[0]], AF.Copy, [("ps", zb[0])], [("Zb", 0)])
            P.op("dve", "tensor_copy", dict(out=Zb[:, 512:1024], in_=C.psum[zb[1]]), reads=[("ps", zb[1])], writes=[("Zb", 1)])
            ub = [nb(), nb()]
            for h in range(RH):
                bk = C.psum[ub[h // 8]]
                co = (h % 8) * 64
                MM(P, bk[:, co:co + 64], hm(XT, h), Zb[:, h * 64:(h + 1) * 64], True, True,
                   [htag("XT", h), ("Zb", h // 8)], [("ps", ub[h // 8])])
            ACT(P, Ub[:, 0:512], C.psum[ub[0]], AF.Copy, [("ps", ub[0])], [("Ub", 0)])
            P.op("dve", "tensor_copy", dict(out=Ub[:, 512:1024], in_=C.psum[ub[1]]), reads=[("ps", ub[1])], writes=[("Ub", 1)])
            yb = [nb(), nb()]
            for p_ in range(NCH):
                bk = C.psum[yb[p_ // 4]]
                co = (p_ % 4) * 128
                MM(P, bk[:, co:co + 128], RT[:, p_, qs], Sbd[:, p_, :], True, False, [("RT",), ("Sbd",)], [("ps", yb[p_ // 4])])
                for hh in range(2):
                    h = 2 * p_ + hh
                    MM(P, bk[:, co + hh * 64:co + (hh + 1) * 64], hm(NrbT, h), Ub[:, h * 64:(h + 1) * 64], False, False,
                       [htag("NrbT", h), ("Ub", h // 8)], [("ps", yb[p_ // 4])])
                    MM(P, bk[:, co + hh * 64:co + (hh + 1) * 64], hm(NrkT, h), Vtm[:, h * 64:(h + 1) * 64], False, hh == 1,
                       [htag("NrkT", h), ("Vtm",)], [("ps", yb[p_ // 4])])
            ACT(P, Ytm[:, q, 0:512], C.psum[yb[0]], AF.Copy, [("ps", yb[0])], [("Ytm", q)])
            P.op("dve", "tensor_copy", dict(out=Ytm[:, q, 512:1024], in_=C.psum[yb[1]]), reads=[("ps", yb[1])], writes=[("Ytm", q)])
            sb_ = [nb(), nb()]
            for p_ in range(NCH):
                bk = C.psum[sb_[p_ // 4]]
                co = (p_ % 4) * 128
                ps_ = slice(p_ * 128, (p_ + 1) * 128)
                MM(P, bk[:, co:co + 128], Bptm[:, ps_], Ub[:, ps_], True, False, [("Bptm",), ("Ub", p_ // 4)], [("ps", sb_[p_ // 4])])
                MM(P, bk[:, co:co + 128], Kptm[:, ps_], Vtm[:, ps_], False, True, [("Kptm",), ("Vtm",)], [("ps", sb_[p_ // 4])])
            for hb in range(2):
                TT_(P, "dve", tmpm[:, hb * 512:(hb + 1) * 512], C.psum[sb_[hb]], BD4, ALU.mult, [("ps", sb_[hb])], [("tmpm",)])
            for p_ in range(NCH):
                STT(P, Sbd32[:, p_, :], Sbd32[:, p_, :], WLt[:, p_, q:q + 1], tmpm[:, p_ * 128:(p_ + 1) * 128], ALU.mult, ALU.add,
                    [("Sbd32",), ("WLt",), ("tmpm",)], [("Sbd32",)])
            P.op("dve", "tensor_copy", dict(out=Sbd, in_=Sbd32), reads=[("Sbd32",)], writes=[("Sbd",)])

        P.stage = 'rw.G#%d' % ti
        trb = [nb(), nb()]
        y4 = Ytm.rearrange("p q (h n) -> p (q h) n", h=RH)
        P.op("dve", "tensor_reduce", dict(out=gst[:, 0, :], in_=y4, axis=AX.X, op=ALU.add), reads=[("Ytm", 0), ("Ytm", 1)], writes=[("gst",)])
        ACT(P, ysq, Ytm, AF.Square, [("Ytm", 0), ("Ytm", 1)], [("ysq",)])
        P.op("dve", "tensor_reduce", dict(out=gst[:, 1, :], in_=ysq.rearrange("p q (h n) -> p (q h) n", h=RH), axis=AX.X, op=ALU.add),
             reads=[("ysq",)], writes=[("gst",)])
        TS(P, "dve", gst[:, 2, :], gst[:, 0, :], 1.0 / 64, None, ALU.mult, None, [("gst",)], [("gst",)])
        TT_(P, "dve", gst[:, 3, :], gst[:, 2, :], gst[:, 2, :], ALU.mult, [("gst",)], [("gst",)])
        STT(P, gst[:, 3, :], gst[:, 1, :], 1.0 / 64, gst[:, 3, :], ALU.mult, ALU.subtract, [("gst",)], [("gst",)])
        ACT(P, gst[:, 3, :], gst[:, 3, :], AF.Sqrt, [("gst",)], [("gst",)], bias=C.eps_ap(64e-5), scale=1.0)
        P.op("dve", "reciprocal", dict(out=gst[:, 3, :], in_=gst[:, 3, :]), reads=[("gst",)], writes=[("gst",)])
        STT(P, gst[:, 4, :], gst[:, 2, :], -1.0, gst[:, 3, :], ALU.mult, ALU.mult, [("gst",)], [("gst",)])
        for q in range(2):
            for h in range(RH):
                j = q * RH + h
                if h % 2 == 0:
                    ACT(P, ynb[:, q, h * 64:(h + 1) * 64], Ytm[:, q, h * 64:(h + 1) * 64], AF.Identity, [("Ytm", q), ("gst",)], [("ynb", q, h)],
                        bias=gst[:, 4, j:j + 1], scale=gst[:, 3, j:j + 1])
                else:
                    TS(P, "dve", ynb[:, q, h * 64:(h + 1) * 64], Ytm[:, q, h * 64:(h + 1) * 64], gst[:, 2, j:j + 1], gst[:, 3, j:j + 1],
                       ALU.subtract, ALU.mult, [("Ytm", q), ("gst",)], [("ynb", q, h)])
            for c in range(NCH):
                psb = C.psum[trb[c // 4]].bitcast(BF16)
                o0 = (c % 4) * TR + q * 128
                P.op("pe", "transpose", dict(out=psb[:, o0:o0 + 128], in_=ynb[:, q, c * 128:(c + 1) * 128], identity=C.rw_id),
                     reads=[("ynb", q, 2 * c), ("ynb", q, 2 * c + 1)], writes=[("ps", trb[c // 4])])
        for c in range(NCH):
            psb = C.psum[trb[c // 4]].bitcast(BF16)
            o0 = (c % 4) * TR
            TS(P, "dve", ytmp[:, c, :], psb[:, o0:o0 + TR], vec[:, c, 11:12], vec[:, c, 12:13], ALU.mult, ALU.add, [("ps", trb[c // 4])], [("ytmp", c // 4)])
        for hf in range(2):
            cs = slice(hf * 4, (hf + 1) * 4)
            TT_(P, "dve", ytmp[:, cs, :], ytmp[:, cs, :], bonus[:, cs, :], ALU.add, [("ytmp", hf), ("bonus",)], [("ytmp", hf)])
            TT_(P, "dve", yg[:, cs, :], ytmp[:, cs, :], gT[:, cs, :], ALU.mult, [("ytmp", hf), ("gT",)], [("yg", hf)])
        P.stage = 'rw.O#%d' % ti
        for dc in range(NCH):
            pb = nb()
            for c in range(NCH):
                MM(P, C.psum[pb][:, 0:TR], wbuf[wb][:, c, dc * 128:(dc + 1) * 128], yg[:, c, :], c == 0, c == NCH - 1,
                   [("wbuf", wb), ("yg", c // 4)], [("ps", pb)])
            TT_(P, "dve", xt[:, dc, :], C.psum[pb][:, 0:TR], xt[:, dc, :], ALU.add, [("ps", pb), ("xt",)], [("xt",)])
        DMA(P, "sp", C.xres[s][:, :, tsl], xt, [("xt",)], [("xres", s, ti // 2)], ("xts",))
        P.barrier()
        A.pop()
    C.xsrc[s] = (C.xres[s], "xres")
    A.pop()

CVALS = [1e-6, 1e-5, 64e-5, 0.0, 1.0, -1.0, 0.5, 1e-24]
C_NG = 7


def build(phases, annotate=False):
    nc = bass.Bass("TRN2", target_bir_lowering=False)
    C = Ctx()
    C.nc = nc
    dt = lambda name, shape, dtype=F32, kind="ExternalInput": nc.dram_tensor(name, list(shape), dtype, kind=kind).ap()
    xin = dt("xin", [SEQ_PER_CORE, 128, NCH, S])
    C.out = dt("out", [SEQ_PER_CORE, 128, NCH, S], kind="ExternalOutput")
    C.xres = dt("xres", [SEQ_PER_CORE, 128, NCH, S], kind="Internal")
    C.xsrc = {s: (xin[s], "xin") for s in range(SEQ_PER_CORE)}
    C.pos = dt("pos", [SEQ_PER_CORE, 1, S], I32)
    C.w_gate = dt("w_gate", [4, 128, NCH, DFF])
    C.w_up = dt("w_up", [4, 128, NCH, DFF])
    C.w_down = dt("w_down", [4, 128, DFF // 128, D])
    gains_d = dt("gains", [128, C_NG * NCH])
    ones_d = dt("ones_f32", [128, 128])
    cvals_d = dt("cvals", [128, len(CVALS)])
    tri_d = dt("tri", [128, 128])
    ropec_d = dt("ropec", [128, 4])
    C.m0_win_conv = dt("m0_win_conv", [128, 2 * NCH, NCH, 128])
    C.m0_win_lat = dt("m0_win_lat", [128, NCH, 896])
    m0_cvec_d = dt("m0_cvec", [128, NCH, 34])
    m0_vec_d = dt("m0_vec", [128, 6])
    C.m0_wuq = dt("m0_wuq", [128, MLA_H, 4, 256])
    C.m0_wukv = dt("m0_wukv", [128, MLA_H, 2, 256])
    C.m0_wout = dt("m0_wout", [128, 2 * NCH, D])
    rwkv_decl(C, dt)

    import contextlib
    with contextlib.ExitStack() as st:
        ARENA = 194 * 1024
        CONST = 12 * 1024
        arena_t = st.enter_context(nc.sbuf_tensor("arena", [128, ARENA], mybir.dt.uint8))
        cst_t = st.enter_context(nc.sbuf_tensor("consts", [128, CONST], mybir.dt.uint8))
        C.arena = Arena(arena_t[:], ARENA)
        CA = Arena(cst_t[:], CONST)
        C.psum = [st.enter_context(nc.psum_tensor("ps%d" % i, [128, 512], F32))[:] for i in range(8)]
        C.P = Prog(nc)
        P = C.P
        P.annotate = annotate
        C.gains = CA.alloc([C_NG * NCH], F32)
        C.ones_f32 = CA.alloc([128], F32)
        C.ones_bf = CA.alloc([128], BF16)
        C.tri_bf = CA.alloc([128], BF16)
        C.cvals = CA.alloc([len(CVALS)], F32)
        C.ropec = CA.alloc([4], F32)
        C.m0_cvec = CA.alloc([NCH, 34], F32)
        C.m0_vec = CA.alloc([6], F32)
        C.eps_ap = lambda v: C.cvals[:, CVALS.index(v):CVALS.index(v) + 1]
        C.G_FINAL = 6
        DMA(P, "sp", C.gains, gains_d, [], [("c", 0)], "c0")
        DMA(P, "sp", C.ones_f32, ones_d, [], [("c", 1)], "c1")
        DMA(P, "pool", C.ones_bf, ones_d, [], [("c", 2)], "c2")
        DMA(P, "pool", C.tri_bf, tri_d, [], [("c", 3)], "c3")
        DMA(P, "sp", C.cvals, cvals_d, [], [("c", 4)], "c4")
        DMA(P, "sp", C.ropec, ropec_d, [], [("c", 5)], "c5")
        DMA(P, "sp", C.m0_cvec, m0_cvec_d, [], [("c", 6)], "c6")
        DMA(P, "sp", C.m0_vec, m0_vec_d, [], [("c", 7)], "c7")
        rwkv_consts(C, CA)
        P.barrier()

        for pi, ph in enumerate(phases):
            kind = ph[0]
            P.prefix = '%02d%s:' % (pi, kind)
            if kind == "ffn":
                phase_ffn(C, ph[1], ph[2], *ph[3:])
            elif kind == "final":
                phase_final(C, ph[1])
            elif kind == "dump":
                phase_dump(C, ph[1])
            elif kind == "mix0":
                phase_mix0(C, ph[1])
            elif kind == "rwkv":
                phase_rwkv(C, ph[1])
            else:
                raise ValueError(kind)
        tags = []
        for s in range(SEQ_PER_CORE):
            for tt in range(NTT):
                tags.append(("out", s, tt))
                tags.append(("xres", s, tt))
        P.op("sp", None, None, reads=tags)
        P.barrier()
        P.finalize_and_emit()
    return nc


def _fm(w, nk):
    w = np.asarray(w, np.float32)
    return np.ascontiguousarray(w.reshape(nk, 128, -1).transpose(1, 0, 2))


def _vec(v, nk):
    return np.ascontiguousarray(np.asarray(v, np.float32).reshape(nk, 128).T)


def prep_shared(inp):
    f32 = np.float32
    sh = {}
    wg = np.asarray(inp["ffn_w_gate"], f32).reshape(4, NCH, 128, DFF).transpose(0, 2, 1, 3)
    wu = np.asarray(inp["ffn_w_up"], f32).reshape(4, NCH, 128, DFF).transpose(0, 2, 1, 3)
    wd = np.asarray(inp["ffn_w_down"], f32).reshape(4, DFF // 128, 128, D).transpose(0, 2, 1, 3)
    sh["w_gate"] = np.ascontiguousarray(wg)
    sh["w_up"] = np.ascontiguousarray(wu)
    sh["w_down"] = np.ascontiguousarray(wd)
    gl = [np.asarray(inp["ffn_norm"], f32).reshape(4, D)[i] for i in range(4)]
    gl.append(np.asarray(inp["mix_norm_even"], f32).reshape(D))
    gl.append(np.asarray(inp["mix_norm_odd"], f32).reshape(D))
    gl.append(np.asarray(inp["final_norm"], f32).reshape(D))
    gains = np.stack([g.reshape(NCH, 128).T for g in gl], axis=1)
    sh["gains"] = np.ascontiguousarray(gains.reshape(128, C_NG * NCH))
    sh["ones_f32"] = np.ones((128, 128), f32)
    sh["cvals"] = np.tile(np.asarray(CVALS, f32)[None, :], (128, 1))
    sh["tri"] = np.triu(np.ones((128, 128), f32))
    invf = (1.0 / (np.float32(10000.0) ** (np.arange(0, 64, 2, dtype=f32) / f32(64)))).astype(f32)
    rc = np.zeros((128, 4), f32)
    rc[0:64, 0] = np.concatenate([invf, invf])
    rc[0:64, 1] = np.pi / 2
    rc[0:32, 2] = np.pi
    sh["ropec"] = rc
    w_in = np.asarray(inp["w_in"], f32)[0]
    sh["m0_win_conv"] = np.ascontiguousarray(w_in[:, 0:2 * D].reshape(NCH, 128, 2 * NCH, 128).transpose(1, 2, 0, 3))
    lat = np.concatenate([w_in[:, 2 * D:2 * D + 832], w_in[:, 2 * D + 800:2 * D + 832], w_in[:, 2 * D + 768:2 * D + 800]], axis=1)
    sh["m0_win_lat"] = _fm(lat, NCH)
    cvec = np.zeros((128, NCH, 34), f32)
    cvec[:, :, 0:31] = np.asarray(inp["conv_w"], f32)[0].reshape(31, NCH, 128).transpose(2, 1, 0)
    cvec[:, :, 31] = _vec(inp["conv_b"][0], NCH)
    cvec[:, :, 32] = _vec(inp["conv_ln_g"][0], NCH)
    cvec[:, :, 33] = _vec(inp["conv_ln_b"][0], NCH)
    sh["m0_cvec"] = cvec
    sh["m0_vec"] = np.concatenate([_vec(inp["q_norm"][0], 4), _vec(inp["kv_norm"][0], 2)], axis=1)
    wuq = np.asarray(inp["w_uq"], f32)[0].reshape(512, MLA_H, 192)
    wuq = np.concatenate([wuq, wuq[:, :, 160:192], wuq[:, :, 128:160]], axis=2)
    sh["m0_wuq"] = np.ascontiguousarray(wuq.reshape(4, 128, MLA_H, 256).transpose(1, 2, 0, 3))
    wukv = np.asarray(inp["w_ukv"], f32)[0].reshape(256, MLA_H, 256)
    sh["m0_wukv"] = np.ascontiguousarray(wukv.reshape(2, 128, MLA_H, 256).transpose(1, 2, 0, 3))
    sh["m0_wout"] = _fm(np.asarray(inp["w_out"], f32)[0], 2 * NCH)
    rwkv_prep(inp, sh)
    return sh


def default_phases():
    ph = []
    for s in range(SEQ_PER_CORE):
        ph += [("ffn", s, 0), ("mix0", s), ("ffn", s, 1, True, False), ("ffn", s, 2, False, True), ("rwkv", s), ("ffn", s, 3, True, False, True)]
    return ph


def core_inputs(inp, sh, c):
    x = np.asarray(inp["x"], np.float32)
    xc = x[c * SEQ_PER_CORE:(c + 1) * SEQ_PER_CORE]
    xT = xc.reshape(SEQ_PER_CORE, S, NCH, 128).transpose(0, 3, 2, 1)
    m = dict(sh)
    m["xin"] = np.ascontiguousarray(xT)
    m["pos"] = np.ascontiguousarray(np.asarray(inp["positions"], np.int32)[c * SEQ_PER_CORE:(c + 1) * SEQ_PER_CORE].reshape(SEQ_PER_CORE, 1, S))
    return m


def kernel(**inp):
    sh = prep_shared(inp)
    nc = build(default_phases())
    in_maps = [core_inputs(inp, sh, c) for c in range(NCORES)]
    res = run_bass_kernel_spmd(nc, in_maps, core_ids=list(range(NCORES)))
    outs = []
    for c in range(NCORES):
        o = np.asarray(res.results[c]["out"], np.float32)
        outs.append(o.transpose(0, 3, 2, 1).reshape(SEQ_PER_CORE, S, D))
    return np.ascontiguousarray(np.concatenate(outs, axis=0))
```
